# Optimizing a Trainium2 kernel written in Bass

```python
import math
import jax, jax.numpy as jnp
from jax import lax
import numpy as np

D_MODEL = 1024
BATCH = 32
SEQ = 2048
DEPTH = 2

GRID_W = 64
CTX_LEN = 256
HEAD_DIM = 64
A_HEADS = 6
A_KV_HEADS = 2
C_HEADS = 6
C_KV_HEADS = 2
HY_WIDTH = 256
HY_ORDER = 2
HY_BANDS = 16
HY_EMB = 1 + 2 * HY_BANDS
HY_HIDDEN = 64
HY_SHORT = 3
HY_FAST_DECAY = 0.3
HY_SLOW_DECAY = 1.5
HY_TARGET = 1e-2
BLOCK = 128
WINDOW = 128
ROPE_THETA = 10000.0
NORM_EPS = 1e-6
NEG_INF = -1e30
MIX_WIDTH = (A_HEADS + C_HEADS) * HEAD_DIM + HY_WIDTH
IN_SIZES = (A_HEADS * HEAD_DIM, A_KV_HEADS * HEAD_DIM, A_KV_HEADS * HEAD_DIM, A_HEADS * HEAD_DIM,
            HY_WIDTH, HY_WIDTH, HY_WIDTH, HY_WIDTH,
            C_HEADS * HEAD_DIM, C_KV_HEADS * HEAD_DIM, C_KV_HEADS * HEAD_DIM, C_HEADS * HEAD_DIM)
IN_WIDTH = sum(IN_SIZES)

kernel_name = "hymba_style_gqa_hyena_swa_diffusion_block"


def _in_bounds():
    return [int(v) for v in np.cumsum((0,) + IN_SIZES)]


def _split_in(p):
    return jnp.split(p, _in_bounds()[1:-1], axis=-1)


def _rmsnorm(x, g):
    xf = x.astype(jnp.float32)
    y = xf * lax.rsqrt(jnp.mean(xf * xf, axis=-1, keepdims=True) + NORM_EPS)
    return (y * g.astype(jnp.float32)).astype(x.dtype)


def _axial_rope_tables(L, dtype):
    rows = L // GRID_W
    row = jnp.repeat(jnp.arange(rows), GRID_W, total_repeat_length=rows * GRID_W)
    col = jnp.tile(jnp.arange(GRID_W), rows)
    n_freq = HEAD_DIM // 4
    inv_freq = ROPE_THETA ** (-jnp.arange(n_freq, dtype=jnp.float32) / n_freq)
    ang = jnp.stack([row.astype(jnp.float32)[:, None] * inv_freq,
                     col.astype(jnp.float32)[:, None] * inv_freq], axis=1)
    return jnp.cos(ang).astype(dtype), jnp.sin(ang).astype(dtype)


def _apply_rope(x, cos, sin):
    B, L, H, _ = x.shape
    xr = x.reshape(B, L, H, 2, 2, HEAD_DIM // 4)
    xa, xb = xr[..., 0, :], xr[..., 1, :]
    c, s = cos[None, :, None], sin[None, :, None]
    out = jnp.stack([xa * c - xb * s, xb * c + xa * s], axis=-2)
    return out.reshape(B, L, H, HEAD_DIM)


def _heads(t, n):
    return t.reshape(t.shape[0], t.shape[1], n, HEAD_DIM)


def _group(q, n_kv):
    B, L, H, D = q.shape
    return q.reshape(B, L, n_kv, H // n_kv, D)


def _global_attention(q, k, v):
    B, L, KV, G, D = q.shape
    nb = L // BLOCK
    scale = 1.0 / math.sqrt(D)
    qb = jnp.moveaxis(q.reshape(B, nb, BLOCK, KV, G, D), 1, 0)

    def block(qblk):
        s = jnp.einsum('bqkgd,bskd->bkgqs', qblk, k).astype(jnp.float32) * scale
        p = jax.nn.softmax(s, axis=-1).astype(v.dtype)
        return jnp.einsum('bkgqs,bskd->bqkgd', p, v)

    out = lax.map(block, qb)
    return jnp.moveaxis(out, 0, 1).reshape(B, L, KV * G * D)


def _sink_softmax(s, sink):
    B, K, G, Q, _ = s.shape
    sk = jnp.broadcast_to(sink.astype(jnp.float32).reshape(1, K, G, 1, 1), (B, K, G, Q, 1))
    return jax.nn.softmax(jnp.concatenate([s, sk], axis=-1), axis=-1)[..., :-1]


def _window_attention(q, k, v, k_ctx, v_ctx, sink):
    B, L, KV, G, D = q.shape
    nb = L // BLOCK
    band = BLOCK + 2 * WINDOW
    n_ctx = k_ctx.shape[1]
    scale = 1.0 / math.sqrt(D)
    pad = ((0, 0), (WINDOW, WINDOW), (0, 0), (0, 0))
    kp, vp = jnp.pad(k, pad), jnp.pad(v, pad)
    qb = jnp.moveaxis(q.reshape(B, nb, BLOCK, KV, G, D), 1, 0)

    def block(args):
        qblk, i = args
        start = i * BLOCK
        kb = lax.dynamic_slice_in_dim(kp, start, band, axis=1)
        vb = lax.dynamic_slice_in_dim(vp, start, band, axis=1)
        qpos = start + jnp.arange(BLOCK)
        kpos = start - WINDOW + jnp.arange(band)
        valid = ((jnp.abs(qpos[:, None] - kpos[None, :]) <= WINDOW)
                 & (kpos >= 0)[None, :] & (kpos < L)[None, :])
        s_c = jnp.einsum('bqkgd,bskd->bkgqs', qblk, k_ctx).astype(jnp.float32) * scale
        s_b = jnp.einsum('bqkgd,bskd->bkgqs', qblk, kb).astype(jnp.float32) * scale
        s_b = jnp.where(valid, s_b, NEG_INF)
        p = _sink_softmax(jnp.concatenate([s_c, s_b], axis=-1), sink).astype(v.dtype)
        return (jnp.einsum('bkgqs,bskd->bqkgd', p[..., :n_ctx], v_ctx)
                + jnp.einsum('bkgqs,bskd->bqkgd', p[..., n_ctx:], vb))

    out = lax.map(block, (qb, jnp.arange(nb)))
    return jnp.moveaxis(out, 0, 1).reshape(B, L, KV * G * D)


def _ctx_sink_attention(q, k, v, sink):
    B, S, KV, G, D = q.shape
    s = jnp.einsum('bqkgd,bskd->bkgqs', q, k).astype(jnp.float32) / math.sqrt(D)
    p = _sink_softmax(s, sink).astype(v.dtype)
    return jnp.einsum('bkgqs,bskd->bqkgd', p, v).reshape(B, S, KV * G * D)


def _hyena_filter_spectrum(L, w1, b1, w2, b2, w3, freq):
    f32 = jnp.float32
    t = jnp.linspace(0.0, 1.0, L, dtype=f32)[:, None]
    bands = jnp.linspace(1e-4, HY_BANDS - 1, HY_BANDS, dtype=f32)
    w = 2.0 * math.pi * jnp.arange(L, dtype=f32)[:, None] / L
    z = jnp.concatenate([t, jnp.cos(bands * w), jnp.sin(bands * w)], axis=-1)
    fr = freq.astype(f32)
    h = jnp.sin(fr[0] * (z @ w1.astype(f32) + b1.astype(f32)))
    h = jnp.sin(fr[1] * (h @ w2.astype(f32) + b2.astype(f32)))
    h = (h @ w3.astype(f32)).reshape(L, HY_ORDER, 2, HY_WIDTH)
    min_decay = math.log(HY_TARGET) / HY_SLOW_DECAY
    max_decay = math.log(HY_TARGET) / HY_FAST_DECAY
    deltas = jnp.linspace(min_decay, max_decay, HY_WIDTH, dtype=f32)
    h = h * jnp.exp(-t * jnp.abs(deltas))[:, None, None, :]
    fwd, bwd = h[:, :, 0], h[:, :, 1]
    filt = jnp.concatenate([fwd, jnp.zeros_like(fwd[:1]), bwd[:0:-1]], axis=0)
    filt = filt / jnp.sum(jnp.abs(filt), axis=0, keepdims=True)
    return jnp.fft.rfft(filt, axis=0)


def _fft_conv(u, spec_o, d_o):
    L = u.shape[1]
    uf = u.astype(jnp.float32)
    y = jnp.fft.irfft(jnp.fft.rfft(uf, n=2 * L, axis=1) * spec_o[None], n=2 * L, axis=1)[:, :L]
    return (y + uf * d_o.astype(jnp.float32)).astype(u.dtype)


def _short_conv(u, w, b):
    L = u.shape[1]
    r = HY_SHORT // 2
    up = jnp.pad(u, ((0, 0), (r, r), (0, 0)))
    out = b
    for j in range(HY_SHORT):
        out = out + up[:, j:j + L] * w[j]
    return out


def _hyena_mixer(v, x1, x2, conv_w, conv_b, spec, d_bias):
    u = _short_conv(jnp.concatenate([v, x1, x2], axis=-1), conv_w, conv_b)
    parts = jnp.split(u, HY_ORDER + 1, axis=-1)
    z = parts[0]
    for o in range(HY_ORDER):
        z = parts[o + 1] * _fft_conv(z, spec[:, o], d_bias[o])
    return z


def _layer(x, ctx, c, c_ctx, norm_g, w_mod, b_mod, w_in, w_out, qn_a, kn_a, qn_c, kn_c, sink_c,
           hy_conv_w, hy_conv_b, hy_w1, hy_b1, hy_w2, hy_b2, hy_w3, hy_freq, hy_bias, last):
    shift, scale, gate = jnp.split(jax.nn.silu(c) @ w_mod + b_mod, 3, axis=-1)
    shift_c, scale_c, gate_c = jnp.split(jax.nn.silu(c_ctx) @ w_mod + b_mod, 3, axis=-1)
    h = _rmsnorm(x, norm_g) * (1 + scale[:, None]) + shift[:, None]
    hc = _rmsnorm(ctx, norm_g) * (1 + scale_c) + shift_c
    aq, ak, av, ag, bv, bx1, bx2, bg, cq, ck, cv, cg = _split_in(h @ w_in)
    off = _in_bounds()
    if last:
        ak_c, av_c = jnp.split(hc @ w_in[:, off[1]:off[3]], 2, axis=-1)
        ck_c, cv_c = jnp.split(hc @ w_in[:, off[9]:off[11]], 2, axis=-1)
    else:
        (aq_c, ak_c, av_c, ag_c, bv_c, bx1_c, bx2_c, bg_c,
         cq_c, ck_c, cv_c, cg_c) = _split_in(hc @ w_in)
    L = x.shape[1]
    cos, sin = _axial_rope_tables(L, x.dtype)
    hy_p = (hy_w1, hy_b1, hy_w2, hy_b2, hy_w3, hy_freq)

    qa = _apply_rope(_rmsnorm(_heads(aq, A_HEADS), qn_a), cos, sin)
    ka = _apply_rope(_rmsnorm(_heads(ak, A_KV_HEADS), kn_a), cos, sin)
    ka_c = _rmsnorm(_heads(ak_c, A_KV_HEADS), kn_a)
    va, va_c = _heads(av, A_KV_HEADS), _heads(av_c, A_KV_HEADS)
    a_out = _global_attention(_group(qa, A_KV_HEADS), jnp.concatenate([ka_c, ka], axis=1),
                              jnp.concatenate([va_c, va], axis=1)) * jax.nn.silu(ag)
    spec = _hyena_filter_spectrum(L, *hy_p)
    b_out = _hyena_mixer(bv, bx1, bx2, hy_conv_w, hy_conv_b, spec, hy_bias) * jax.nn.silu(bg)
    qc = _apply_rope(_rmsnorm(_heads(cq, C_HEADS), qn_c), cos, sin)
    kc = _apply_rope(_rmsnorm(_heads(ck, C_KV_HEADS), kn_c), cos, sin)
    kc_c = _rmsnorm(_heads(ck_c, C_KV_HEADS), kn_c)
    vc, vc_c = _heads(cv, C_KV_HEADS), _heads(cv_c, C_KV_HEADS)
    c_out = _window_attention(_group(qc, C_KV_HEADS), kc, vc, kc_c, vc_c, sink_c) * jax.nn.silu(cg)

    x = x + gate[:, None] * (jnp.concatenate([a_out, b_out, c_out], axis=-1) @ w_out)
    if last:
        return x, None

    qa_c = _group(_rmsnorm(_heads(aq_c, A_HEADS), qn_a), A_KV_HEADS)
    a_c = _global_attention(qa_c, ka_c, va_c) * jax.nn.silu(ag_c)
    spec_c = _hyena_filter_spectrum(ctx.shape[1], *hy_p)
    b_c = _hyena_mixer(bv_c, bx1_c, bx2_c, hy_conv_w, hy_conv_b, spec_c, hy_bias) * jax.nn.silu(bg_c)
    qc_c = _group(_rmsnorm(_heads(cq_c, C_HEADS), qn_c), C_KV_HEADS)
    c_c = _ctx_sink_attention(qc_c, kc_c, vc_c, sink_c) * jax.nn.silu(cg_c)
    ctx = ctx + gate_c * (jnp.concatenate([a_c, b_c, c_c], axis=-1) @ w_out)
    return x, ctx


def setup_inputs(seed: int = 0) -> dict:
    key = jax.random.key(seed)
    ks = jax.random.split(key, 24)
    f32 = jnp.float32

    def nrm(k, shape, s):
        return jax.random.normal(k, shape, f32) * s

    D = D_MODEL
    return {
        "x": nrm(ks[0], (BATCH, SEQ, D), 1.0),
        "c": nrm(ks[1], (BATCH, D), 1.0),
        "ctx": nrm(ks[2], (BATCH, CTX_LEN, D), 1.0),
        "c_ctx": nrm(ks[3], (D,), 1.0),
        "norm_g": 1.0 + nrm(ks[4], (DEPTH, D), 0.1),
        "w_mod": nrm(ks[5], (DEPTH, D, 3 * D), D ** -0.5),
        "b_mod": nrm(ks[6], (DEPTH, 3 * D), 0.02),
        "w_in": nrm(ks[7], (DEPTH, D, IN_WIDTH), D ** -0.5),
        "w_out": nrm(ks[8], (DEPTH, MIX_WIDTH, D), MIX_WIDTH ** -0.5),
        "qn_a": 1.0 + nrm(ks[9], (DEPTH, HEAD_DIM), 0.1),
        "kn_a": 1.0 + nrm(ks[10], (DEPTH, HEAD_DIM), 0.1),
        "qn_c": 1.0 + nrm(ks[11], (DEPTH, HEAD_DIM), 0.1),
        "kn_c": 1.0 + nrm(ks[12], (DEPTH, HEAD_DIM), 0.1),
        "sink_c": nrm(ks[13], (DEPTH, C_HEADS), 0.5),
        "hy_conv_w": nrm(ks[14], (DEPTH, HY_SHORT, 3 * HY_WIDTH), HY_SHORT ** -0.5),
        "hy_conv_b": nrm(ks[15], (DEPTH, 3 * HY_WIDTH), 0.02),
        "hy_w1": nrm(ks[16], (DEPTH, HY_EMB, HY_HIDDEN), HY_EMB ** -0.5),
        "hy_b1": nrm(ks[17], (DEPTH, HY_HIDDEN), 0.1),
        "hy_w2": nrm(ks[18], (DEPTH, HY_HIDDEN, HY_HIDDEN), HY_HIDDEN ** -0.5),
        "hy_b2": nrm(ks[19], (DEPTH, HY_HIDDEN), 0.1),
        "hy_w3": nrm(ks[20], (DEPTH, HY_HIDDEN, HY_ORDER * 2 * HY_WIDTH), HY_HIDDEN ** -0.5),
        "hy_freq": 1.0 + nrm(ks[21], (DEPTH, 2, HY_HIDDEN), 0.1),
        "hy_bias": nrm(ks[22], (DEPTH, HY_ORDER, HY_WIDTH), 0.5),
    }


def reference(x, c, ctx, c_ctx, norm_g, w_mod, b_mod, w_in, w_out, qn_a, kn_a, qn_c, kn_c, sink_c,
              hy_conv_w, hy_conv_b, hy_w1, hy_b1, hy_w2, hy_b2, hy_w3, hy_freq, hy_bias):
    for l in range(DEPTH):
        x, ctx = _layer(x, ctx, c, c_ctx, norm_g[l], w_mod[l], b_mod[l], w_in[l], w_out[l],
                        qn_a[l], kn_a[l], qn_c[l], kn_c[l], sink_c[l],
                        hy_conv_w[l], hy_conv_b[l], hy_w1[l], hy_b1[l], hy_w2[l], hy_b2[l],
                        hy_w3[l], hy_freq[l], hy_bias[l], last=(l == DEPTH - 1))
    return x
```

```python
import math
from contextlib import ExitStack

import numpy as np
import ml_dtypes

import concourse.bass as bass
import concourse.mybir as mybir
from concourse.bass_utils import run_bass_kernel_spmd

F32 = mybir.dt.float32
BF16 = mybir.dt.bfloat16
AF = mybir.ActivationFunctionType
ALU = mybir.AluOpType

NCORES = 8
NB = 4
D = 1024
L = 2048
LC = 256
NT = 18
EPS = 1e-6
MAGIC = 12582912.0
USE_ARS = False
NF = 4096


class Sched:
    ENG = ("pe", "act", "dve", "pool", "sp")

    def __init__(self, nc, ndma=8):
        self.nc = nc
        self.streams = {e: [] for e in self.ENG}
        self.cnt = {}
        self.waited = {}
        self.res = {}
        self.dmai = {e: 0 for e in self.ENG}
        self.ndma = ndma
        self.epoch = 0
        self.semh = {}

    def op(self, eng, fn, reads=(), writes=(), adds=(), dma=False):
        deps = {}

        def add(k, v):
            if deps.get(k, 0) < v:
                deps[k] = v

        for r in reads:
            st = self.res.get(r)
            if st:
                for k, v in st[0].items():
                    add(k, v)
                if isinstance(r, tuple) and r[0] == "bank":
                    for k, v in st[1].items():
                        add(k, v)
        for w in writes:
            st = self.res.get(w)
            if st:
                for k, v in st[0].items():
                    add(k, v)
                for k, v in st[1].items():
                    add(k, v)
        for w in adds:
            st = self.res.get(w)
            if st:
                for k, v in st[1].items():
                    add(k, v)
        if dma:
            i = self.dmai[eng]
            self.dmai[eng] += 1
            key = ("dma", eng, i % self.ndma)
            prev = self.cnt.get(key, 0)
            if prev:
                add(key, prev)
            inc = 16
        else:
            key = ("done", eng, self.epoch)
            inc = 1
        newv = self.cnt.get(key, 0) + inc
        st = self.streams[eng]
        for k, v in deps.items():
            if eng == "pe" and k[0] == "done" and k[1] == "pe":
                continue
            if self.waited.get((eng, k), 0) >= v:
                continue
            self.waited[(eng, k)] = v
            st.append(("w", k, v))
        st.append(("o", fn, key, inc))
        self.cnt[key] = newv
        for w in writes:
            self.res[w] = [{key: newv}, {}]
        for w in adds:
            s_ = self.res.setdefault(w, [{}, {}])
            if s_[0].get(key, 0) < newv:
                s_[0][key] = newv
        for r in reads:
            s_ = self.res.setdefault(r, [{}, {}])
            if s_[1].get(key, 0) < newv:
                s_[1][key] = newv

    def barrier(self):
        for eng in self.ENG:
            for k, v in list(self.cnt.items()):
                if eng == "pe" and k[0] == "done" and k[1] == "pe":
                    continue
                if self.waited.get((eng, k), 0) >= v:
                    continue
                self.waited[(eng, k)] = v
                self.streams[eng].append(("w", k, v))

    def finish(self):
        for eng in self.ENG:
            for k, v in list(self.cnt.items()):
                if k[0] == "dma" and k[1] == eng:
                    self.streams[eng].append(("w", k, v))

    def make_sems(self, es):
        keys = set(self.cnt.keys())
        for i, k in enumerate(sorted(keys, key=str)):
            self.semh[k] = es.enter_context(self.nc.semaphore("s%d" % i))

    def replay(self, eng, e):
        for it in self.streams[eng]:
            if it[0] == "w":
                e.wait_ge(self.semh[it[1]], it[2])
            else:
                ins = it[1](e)
                ins.then_inc(self.semh[it[2]], it[3])


_CONST = None


def _bf(a):
    return np.ascontiguousarray(a.astype(np.float32)).astype(ml_dtypes.bfloat16)


def _partner(d):
    blk, r = divmod(d, 64)
    axis, r2 = divmod(r, 32)
    half, f = divmod(r2, 16)
    return blk * 64 + axis * 32 + (1 - half) * 16 + f


def _hy_pos(Lx):
    t = np.linspace(0.0, 1.0, Lx, dtype=np.float32).astype(np.float64)
    bands = np.linspace(1e-4, 15.0, 16, dtype=np.float32).astype(np.float64)
    w = (2.0 * math.pi * np.arange(Lx, dtype=np.float32) / Lx).astype(np.float32).astype(np.float64)
    z = np.concatenate([t[:, None], np.cos(bands[None] * w[:, None]), np.sin(bands[None] * w[:, None])], -1)
    min_decay = math.log(1e-2) / 1.5
    max_decay = math.log(1e-2) / 0.3
    deltas = np.linspace(min_decay, max_decay, 256, dtype=np.float32).astype(np.float64)
    dec = np.exp(-t[:, None] * np.abs(deltas)[None])
    return z, dec


def _hy_consts(Lx):
    N = 2 * Lx
    z, dec = _hy_pos(Lx)
    zc = np.zeros((N, 33), np.float64)
    dc = np.zeros((N, 256), np.float64)
    zc[:Lx] = z
    dc[:Lx] = dec
    for m in range(Lx + 1, N):
        zc[m] = z[N - m]
        dc[m] = -dec[N - m]
    H = Lx // 2
    nft = H // 128
    fidx = np.concatenate([np.arange(H), Lx - 1 - np.arange(H)]).astype(np.float64)
    om_all = 2.0 * math.pi * (fidx + 0.5) / N
    m = np.arange(N, dtype=np.float64)
    ang = m[:, None] * om_all[None, :]
    Fw = np.concatenate([np.cos(ang), -np.sin(ang)], 1)
    nrt = N // 128
    nch = N // 128
    FwT = Fw.reshape(nch, 128, nrt, 128).transpose(2, 1, 0, 3)
    om = 2.0 * math.pi * (np.arange(H, dtype=np.float64) + 0.5) / N
    tau = np.arange(H, dtype=np.float64)
    angE = (2 * tau)[:, None] * om[None, :]
    angO = (2 * tau + 1)[:, None] * om[None, :]
    parts = [np.cos(angE), -np.sin(angE), np.cos(angO), -np.sin(angO)]
    FwEO = np.stack([p_.reshape(nft, 128, nft, 128).transpose(2, 1, 0, 3) for p_ in parts], 1)
    inv = []
    for par in range(2):
        mat = (2.0 / N) * np.concatenate([parts[2 * par].T, parts[2 * par + 1].T], 0)
        inv.append(mat.reshape(2 * nft, 128, nft, 128).transpose(2, 1, 0, 3))
    FinvT = np.stack(inv, 1)
    zT = np.ascontiguousarray(zc.T).astype(np.float32)
    decT = np.ascontiguousarray(dc.reshape(nch, 128, 256).transpose(1, 0, 2)).astype(np.float32)
    return zT, decT, _bf(FwT), _bf(FinvT), _bf(FwEO)


def _consts():
    global _CONST
    if _CONST is not None:
        return _CONST
    c = {}
    c["ident"] = _bf(np.eye(128))
    pm = np.zeros((128, 128), np.float32)
    for m in range(128):
        pm[_partner(m), m] = 1.0
    c["permM"] = _bf(pm)
    b64 = np.zeros((128, 128), np.float32)
    b64[:64, :64] = 1.0
    b64[64:, 64:] = 1.0
    c["blk64"] = _bf(b64)
    c["ones128"] = _bf(np.ones((128, 128)))
    inv_freq = (10000.0 ** (-np.arange(16, dtype=np.float32) / 16)).astype(np.float32)
    tok = np.arange(L)
    row = (tok // 64).astype(np.float32)
    col = (tok % 64).astype(np.float32)
    cosT = np.ones((128, NT * 128), np.float32)
    sinT = np.zeros((128, NT * 128), np.float32)
    for m in range(128):
        d = m % 64
        axis, r2 = divmod(d, 32)
        half, f = divmod(r2, 16)
        pos = row if axis == 0 else col
        ang = (pos * inv_freq[f]).astype(np.float32)
        cosT[m, 256:] = np.cos(ang)
        sinT[m, 256:] = np.sin(ang) * (-1.0 if half == 0 else 1.0)
    c["cosT"] = _bf(cosT)
    c["sinT"] = _bf(sinT)
    ki = np.arange(128)[:, None]
    qi = np.arange(128)[None, :]
    prev = (qi <= ki).astype(np.float32)
    nxt = (ki <= qi).astype(np.float32)
    one = np.ones((128, 128), np.float32)
    zero = np.zeros((128, 128), np.float32)
    masks = np.stack([np.concatenate([prev, zero], 1), np.concatenate([one, prev], 1),
                      np.concatenate([nxt, one], 1), np.concatenate([zero, nxt], 1)], 1)
    c["masks"] = _bf(masks)
    c["identf"] = np.eye(128, dtype=np.float32)
    zT, decT, FwT, FinvT, FwEO = _hy_consts(L)
    c["zT"], c["decT"], c["FwT"], c["FinvT"], c["FwEO"] = zT, decT, FwT, FinvT, FwEO
    zTc, decTc, FwTc, FinvTc, FwEOc = _hy_consts(LC)
    c["zTc"], c["decTc"], c["FwTc"], c["FinvTc"], c["FwEOc"] = zTc, decTc, FwTc, FinvTc, FwEOc
    _CONST = c
    return c


PARAM_SHAPES = {
    "norm_g": [2, 1024], "w_mod": [2, 1024, 3072], "b_mod": [2, 3072], "w_in": [2, 1024, 3072],
    "w_out": [2, 1024, 1024], "qn_a": [2, 64], "kn_a": [2, 64], "qn_c": [2, 64], "kn_c": [2, 64],
    "sink_c": [2, 6], "hy_conv_w": [2, 3, 768], "hy_conv_b": [2, 768], "hy_w1": [2, 33, 64],
    "hy_b1": [2, 64], "hy_w2": [2, 64, 64], "hy_b2": [2, 64], "hy_w3": [2, 64, 1024],
    "hy_freq": [2, 2, 64], "hy_bias": [2, 2, 256],
}

CH_AQ, CH_AK, CH_AV, CH_AG = 0, 3, 4, 5
CH_BV, CH_BX1, CH_BX2, CH_BG = 8, 10, 12, 14
CH_CQ, CH_CK, CH_CV, CH_CG = 16, 19, 20, 21


def build_program(nlayers=2, phases=("p0", "p1", "p2", "p3"), debug=(), nb=NB):
    nc = bass.Bass("TRN2", target_bir_lowering=False)
    cst = _consts()
    es = ExitStack()
    S = Sched(nc)

    def din(name, shape, dt=F32):
        return nc.dram_tensor(name, list(shape), dt, kind="ExternalInput")

    T = {}
    T["x"] = din("x", [NB, L, D])
    T["ctx"] = din("ctx", [NB, LC, D])
    T["cR"] = din("cR", [5, D])
    for k, shp in PARAM_SHAPES.items():
        T[k] = din(k, shp)
    for k, v in cst.items():
        T[k] = din("c_" + k, v.shape, BF16 if v.dtype == ml_dtypes.bfloat16 else F32)
    y = nc.dram_tensor("y", [NB, L, D], F32, kind="ExternalOutput")

    def scratch(name, shape, dt):
        return nc.dram_tensor(name, list(shape), dt)

    w_in_t = scratch("w_in_t", [2, 24, 128, 8, 128], BF16)
    w_out_t = scratch("w_out_t", [2, 128, 8, 1024], BF16)
    modrow = scratch("modrow", [2, 5, 3072], F32)
    gtab = scratch("gtab", [2, 2, 2, 16, 128, 256], BF16)
    gtabc = scratch("gtabc", [2, 2, 2, 128, 256], BF16)
    hyl = {n: scratch("hy_" + n, [16, 128, NB, 256], BF16) for n in ("v", "x1", "x2", "bg")}
    hyc = {n: scratch("hyc_" + n, [2, 128, NB, 256], BF16) for n in ("v", "x1", "x2", "bg")}
    mix = scratch("mix", [NB, NT * 128, D], BF16)
    xs = scratch("xs", [NB, L, D], F32)
    ctxs = scratch("ctxs", [NB, LC, D], F32)
    dbg_out = {}
    for ent in debug:
        name, shape = ent[0], ent[1]
        dbg_out[name] = nc.dram_tensor("dbg_" + name, list(shape), BF16 if (len(ent) > 2 and ent[2] == "bf16") else F32,
                                       kind="ExternalOutput")

    used_names = {}

    def sb(name, shape, dt, stk=None):
        n = used_names.get(name, 0)
        used_names[name] = n + 1
        if n:
            name = "%s_%d" % (name, n)
        return (stk or es).enter_context(nc.sbuf_tensor(name, list(shape), dt))

    PS = [es.enter_context(nc.psum_tensor("ps%d" % i, [128, 512], F32)) for i in range(8)]
    PSB = [p.bitcast(BF16) for p in PS]

    def bank(i):
        return ("bank", i)

    def dma(q, out, in_, reads, writes=(), adds=(), **kw):
        def fn(e):
            return e.dma_start(out=out, in_=in_, **kw)
        S.op(q, fn, reads=reads, writes=writes, adds=adds, dma=True)

    def pe(mms, reads, writes):
        mms = list(mms)

        def fn(e):
            ins = None
            for (o, l, r, st, sp) in mms:
                ins = e.matmul(o, lhsT=l, rhs=r, start=st, stop=sp)
            return ins
        S.op("pe", fn, reads=reads, writes=writes)

    def pe_tr(trs, ident, reads, writes):
        trs = list(trs)

        def fn(e):
            ins = None
            for (o, i) in trs:
                ins = e.transpose(o, i, ident)
            return ins
        S.op("pe", fn, reads=reads, writes=writes)

    def act(out, in_, func, reads, writes=(), scale=1.0, bias=0.0, accum=None, adds=()):
        def fn(e):
            kw = {}
            if accum is not None:
                kw["accum_out"] = accum
            return e.activation(out=out, in_=in_, func=func, bias=bias, scale=scale, **kw)
        S.op("act", fn, reads=reads, writes=writes, adds=adds)

    def tsc(eng, out, in0, s1, s2, op0, op1, reads, writes=(), adds=()):
        def fn(e):
            if s2 is None:
                return e.tensor_scalar(out=out, in0=in0, scalar1=s1, scalar2=None, op0=op0)
            return e.tensor_scalar(out=out, in0=in0, scalar1=s1, scalar2=s2, op0=op0, op1=op1)
        S.op(eng, fn, reads=reads, writes=writes, adds=adds)

    def tt(eng, out, in0, in1, op, reads, writes=(), adds=()):
        def fn(e):
            return e.tensor_tensor(out=out, in0=in0, in1=in1, op=op)
        S.op(eng, fn, reads=reads, writes=writes, adds=adds)

    def stt(eng, out, in0, scalar, in1, op0, op1, reads, writes=(), adds=()):
        def fn(e):
            return e.scalar_tensor_tensor(out=out, in0=in0, scalar=scalar, in1=in1, op0=op0, op1=op1)
        S.op(eng, fn, reads=reads, writes=writes, adds=adds)

    def recip(out, in_, reads, writes):
        def fn(e):
            return e.reciprocal(out=out, in_=in_)
        S.op("dve", fn, reads=reads, writes=writes)

    def copy(eng, out, in_, reads, writes=(), adds=()):
        if eng == "act":
            def fn(e):
                return e.copy(out=out, in_=in_)
        else:
            def fn(e):
                return e.tensor_copy(out=out, in_=in_)
        S.op(eng, fn, reads=reads, writes=writes, adds=adds)

    def memset(eng, ap, val, writes):
        def fn(e):
            return e.memset(ap, val)
        S.op(eng, fn, writes=writes)


    ident = sb("ident", [128, 128], BF16)
    permM = sb("permM", [128, 128], BF16)
    blk64 = sb("blk64", [128, 128], BF16)
    ones128 = sb("ones128", [128, 128], BF16)
    AB = sb("AB", [128, 2, 2, 8, 5], F32)
    identf = sb("identf", [128, 128], F32)
    for name, t in (("ident", ident), ("permM", permM), ("blk64", blk64), ("ones128", ones128), ("identf", identf)):
        dma("sp", t[:], T[name].ap(), reads=[], writes=[name])
    gsb = sb("gsb", [128, 16], F32)
    cTs = sb("cTs", [128, 48], F32)
    cwb = sb("cwb", [128, 48], F32)
    gains = sb("gains", [128, 16], F32)
    bfr = sb("bfr", [64, 16], F32)
    esk = sb("esk", [128, 16], F32)
    with ExitStack() as pr:
        r_g = sb("r_g", [16, 128], F32, pr)
        r_c = sb("r_c", [40, 128], F32, pr)
        r_w = sb("r_w", [48, 128], F32, pr)
        r_q = sb("r_q", [8, 128], F32, pr)
        r_b = sb("r_b", [8, 64], F32, pr)
        dma("sp", r_g[:], T["norm_g"].ap().rearrange("l (k p) -> (l k) p", p=128), reads=[], writes=["r_g"])
        dma("sp", r_c[:], T["cR"].ap().rearrange("j (k p) -> (j k) p", p=128), reads=[], writes=["r_c"])
        dma("sp", r_w[0:36, :], T["hy_conv_w"].ap().rearrange("l j (c p) -> (l j c) p", p=128), reads=[], adds=["r_w"])
        dma("sp", r_w[36:48, :], T["hy_conv_b"].ap().rearrange("l (c p) -> (l c) p", p=128), reads=[], adds=["r_w"])
        for v, nm in enumerate(("qn_a", "kn_a", "qn_c", "kn_c")):
            for hh in range(2):
                dma("sp", r_q[2 * v:2 * v + 2, 64 * hh:64 * hh + 64], T[nm].ap(), reads=[], adds=["r_q"])
        dma("sp", r_b[0:2, :], T["hy_b1"].ap(), reads=[], adds=["r_b"])
        dma("sp", r_b[2:4, :], T["hy_b2"].ap(), reads=[], adds=["r_b"])
        dma("sp", r_b[4:8, :], T["hy_freq"].ap().rearrange("l j p -> (l j) p"), reads=[], adds=["r_b"])
        dma("sp", esk[:, 0:12], T["sink_c"].ap().rearrange("l h -> (l h)").partition_broadcast(128), reads=[], writes=["esk"])
        for (rt, n, m, dst, nm) in ((r_g, 16, 128, gsb, "gsb"), (r_c, 40, 128, cTs, "cTs"), (r_w, 48, 128, cwb, "cwb"),
                                    (r_q, 8, 128, gains, "gains"), (r_b, 8, 64, bfr, "bfr")):
            pe([(PS[7][0:m, 0:n], rt[0:n, 0:m], identf[0:n, 0:n], True, True)],
               reads=["r_" + nm[0] if nm != "gains" and nm != "bfr" and nm != "cwb" and nm != "cTs" else
                      {"gains": "r_q", "bfr": "r_b", "cwb": "r_w", "cTs": "r_c"}[nm], "identf"], writes=[bank(7)])
            copy("dve", dst[0:m, 0:n], PS[7][0:m, 0:n], reads=[bank(7)], writes=[nm])
        act(esk[:, 0:12], esk[:, 0:12], AF.Exp, reads=["esk"], writes=["esk"])
    S.barrier()

    if "p0" in phases:
        for l in range(nlayers):
            for k in range(8):
                rows = slice(128 * k, 128 * (k + 1))
                for (c0, off) in ((CH_AQ, 0), (CH_CQ, 2048)):
                    for c in range(3):
                        dst = w_in_t[l, c0 + c, :, k, :].rearrange("r (p d) -> r p d", p=2)
                        src = T["w_in"][l, rows, off:off + 384].rearrange("r (p c d) -> r c p d", p=2, c=3)[:, c]
                        dma("pool", dst, src, reads=[], adds=[("w_in_t", l)])
                for (n0, n1) in ((3, 16), (19, 24)):
                    dst = w_in_t[l, n0:n1, :, k, :].rearrange("c r d -> r c d")
                    src = T["w_in"][l, rows, 128 * n0:128 * n1].rearrange("r (c d) -> r c d", d=128)
                    dma("pool", dst, src, reads=[], adds=[("w_in_t", l)])
                dma("pool", w_out_t[l, :, k, :], T["w_out"][l, rows, :], reads=[], adds=[("w_out_t", l)])

        with ExitStack() as p0:
            sc = sb("sc", [128, 48], F32, p0)
            bmod = sb("bmod", [5, 3072], F32, p0)
            mrow = sb("mrow", [5, 3072], F32, p0)
            wm = [sb("wm%d" % i, [128, 3072], F32, p0) for i in range(2)]
            act(sc[:, 0:40], cTs[:, 0:40], AF.Silu, reads=["cTs"], writes=["sc"])
            sc3 = sc[:, 0:40].rearrange("p (j k) -> p j k", k=8)
            wi = 0
            for l in range(nlayers):
                dma("sp", bmod[:], T["b_mod"][l].partition_broadcast(5), reads=[], writes=["bmod"])
                for k in range(8):
                    w = wm[wi % 2]
                    wr = ("wm", wi % 2)
                    wi += 1
                    dma("sp", w[:], T["w_mod"][l, 128 * k:128 * (k + 1), :], reads=[], writes=[wr])
                    pe([(PS[n][0:5, :], sc3[:, :, k], w[:, 512 * n:512 * (n + 1)], k == 0, k == 7) for n in range(6)],
                       reads=["sc", wr], writes=[bank(n) for n in range(6)])
                for n in range(6):
                    tt("dve", mrow[:, 512 * n:512 * (n + 1)], PS[n][0:5, :], bmod[:, 512 * n:512 * (n + 1)], ALU.add,
                       reads=[bank(n), "bmod"], writes=[("mrow", n)])
                dma("sp", modrow[l], mrow[:], reads=[("mrow", n) for n in range(6)], writes=[("modrow", l)])
                ps6 = PS[6][:, 0:80].rearrange("p (s k j) -> p s k j", s=2, k=8)
                pe([(ps6[:, s, k, :], mrow[:, 1024 * s + 128 * k:1024 * s + 128 * (k + 1)], identf[0:5, 0:5], True, True)
                    for s in range(2) for k in range(8)],
                   reads=[("mrow", n) for n in range(4)] + ["identf"], writes=[bank(6)])
                copy("dve", AB[:, l, 1, :, :], ps6[:, 0, :, :], reads=[bank(6)], writes=[("AB", l, 1)])
                for k in range(8):
                    tsc("dve", AB[:, l, 0, k, :], ps6[:, 1, k, :], 1.0, None, ALU.add, None,
                        reads=[bank(6)], writes=[("AB", l, 0, k)])
                    tsc("dve", AB[:, l, 0, k, :], AB[:, l, 0, k, :], gsb[:, 8 * l + k:8 * l + k + 1], None, ALU.mult, None,
                        reads=[("AB", l, 0, k), "gsb"], writes=[("AB", l, 0, k)])

        S.barrier()
        def filter_tables(l, Lx, zT_d, dec_d, Fw_d, dst_tab, tag):
            N = 2 * Lx
            nblk = N // 512
            nmt = N // 128
            half = nmt // 2
            with ExitStack() as pf:
                w1 = sb("hw1" + tag, [33, 64], F32, pf)
                w2 = sb("hw2" + tag, [64, 64], F32, pf)
                w3 = sb("hw3" + tag, [64, 1024], F32, pf)
                zTs = sb("zTs" + tag, [33, N], F32, pf)
                dec = sb("dec" + tag, [128, nmt, 256], F32, pf)
                dbc = sb("dbc" + tag, [128, 512], F32, pf)
                h2T = sb("h2T" + tag, [64, N], F32, pf)
                Gb = sb("Gb" + tag, [128, nmt, 512], BF16, pf)
                rn = sb("rn" + tag, [128, 512], F32, pf)
                tmp = [sb("ft%d" % i + tag, [64, 512], F32, pf) for i in range(8)]
                g32 = [sb("g32%d" % i + tag, [128, 512], F32, pf) for i in range(2)]
                gab = [sb("gab%d" % i + tag, [128, 512], BF16, pf) for i in range(2)]
                fw = [sb("fw%d" % i + tag, [128, nmt, 128], BF16, pf) for i in range(2)]
                tb = [sb("tb%d" % i + tag, [128, 512], F32, pf) for i in range(2)]
                tbb = [sb("tbb%d" % i + tag, [128, 512], BF16, pf) for i in range(2)]
                R = lambda n: (n, tag)
                dma("sp", w1[:], T["hy_w1"][l], reads=[], writes=[R("w1")])
                dma("sp", w2[:], T["hy_w2"][l], reads=[], writes=[R("w2")])
                dma("sp", w3[:], T["hy_w3"][l], reads=[], writes=[R("w3")])
                dma("sp", zTs[:], zT_d.ap(), reads=[], writes=[R("zTs")])
                dma("sp", dec[:], dec_d.ap(), reads=[], writes=[R("dec")])
                dma("sp", dbc[:], T["hy_bias"][l].rearrange("o c -> (o c)").partition_broadcast(128),
                    reads=[], writes=[R("dbc")])
                bfr_r = ["bfr"]

                slc = [0]

                def sin_layer(src_ps, srcbank, bcol, fcol, dst, dst_res, i):
                    o_ = 4 * (slc[0] % 2)
                    slc[0] += 1
                    a, t1, t2, a2 = tmp[o_:o_ + 4]
                    ra = [R("tmp%d" % (o_ + j)) for j in range(4)]
                    tsc("dve", a[:], src_ps, bfr[:, bcol:bcol + 1], bfr[:, fcol:fcol + 1], ALU.add, ALU.mult,
                        reads=[srcbank] + bfr_r, writes=[ra[0]])
                    tsc("dve", t1[:], a[:], float(1.0 / (2 * math.pi)), MAGIC, ALU.mult, ALU.add,
                        reads=[ra[0]], writes=[ra[1]])
                    tsc("dve", t2[:], t1[:], -MAGIC, float(-2 * math.pi), ALU.add, ALU.mult,
                        reads=[ra[1]], writes=[ra[2]])
                    tt("dve", a2[:], t2[:], a[:], ALU.add, reads=[ra[2], ra[0]], writes=[ra[3]])
                    tsc("dve", a2[:], a2[:], 3.1415925, -3.1415925, ALU.min, ALU.max, reads=[ra[3]], writes=[ra[3]])
                    act(dst, a2[:], AF.Sin, reads=[ra[3]], writes=[dst_res])

                h1s = [sb("h1%d" % i + tag, [64, 512], F32, pf) for i in range(2)]
                for blk in range(nblk):
                    cs = slice(512 * blk, 512 * (blk + 1))
                    h1 = h1s[blk % 2]
                    b0, b1 = 2 * (blk % 2), 2 * (blk % 2) + 1
                    pe([(PS[b0][0:64, :], w1[:], zTs[:, cs], True, True)], reads=[R("w1"), R("zTs")], writes=[bank(b0)])
                    sin_layer(PS[b0][0:64, :], bank(b0), l, 4 + 2 * l, h1[:], R(("h1", blk % 2)), blk)
                    pe([(PS[b1][0:64, :], w2[:], h1[:], True, True)], reads=[R("w2"), R(("h1", blk % 2))], writes=[bank(b1)])
                    sin_layer(PS[b1][0:64, :], bank(b1), 2 + l, 5 + 2 * l, h2T[:, cs], R(("h2T", blk)), blk)
                for mt in range(nmt):
                    dr = 0 if mt < half else 1
                    pb = 6 + (mt % 2)
                    pe([(PS[pb][:, 256 * o:256 * (o + 1)], h2T[:, 128 * mt:128 * (mt + 1)],
                         w3[:, 512 * o + 256 * dr:512 * o + 256 * dr + 256], True, True) for o in range(2)],
                       reads=[R(("h2T", mt // 4)), R("w3")], writes=[bank(pb)])
                    g = g32[mt % 2]
                    for o in range(2):
                        tt("dve", g[:, 256 * o:256 * (o + 1)], PS[pb][:, 256 * o:256 * (o + 1)], dec[:, mt, :], ALU.mult,
                           reads=[bank(pb), R("dec")], writes=[R(("g32", mt % 2, o))])
                    gr = [R(("g32", mt % 2, o)) for o in range(2)]
                    copy("act", Gb[:, mt, :], g[:], reads=gr, writes=[R(("Gb", mt))])
                    stt("dve", gab[mt % 2][:], g[:], -1.0, g[:], ALU.mult, ALU.max, reads=gr, writes=[R(("gab", mt % 2))])
                    pe([(PS[4][:], ones128[:], gab[mt % 2][:], mt == 0, mt == nmt - 1)],
                       reads=["ones128", R(("gab", mt % 2))], writes=[bank(4)])
                recip(rn[:], PS[4][:], reads=[bank(4)], writes=[R("rn")])
                for rt in range(nmt):
                    f_ = fw[rt % 2]
                    fr_ = R(("fw", rt % 2))
                    dma("sp", f_[:], Fw_d[rt], reads=[], writes=[fr_])
                    pb = 2 + (rt % 2)
                    pe([(PS[pb][:], f_[:, c, :], Gb[:, c, :], c == 0, c == nmt - 1) for c in range(nmt)],
                       reads=[fr_] + [R(("Gb", c)) for c in range(nmt)], writes=[bank(pb)])
                    t_ = tb[rt % 2]
                    tr_ = R(("tb", rt % 2))
                    tq_ = tbb[rt % 2]
                    tqr_ = R(("tbb", rt % 2))
                    if rt < half:
                        tt("dve", t_[:], PS[pb][:], rn[:], ALU.mult, reads=[bank(pb), R("rn")], writes=[tr_])
                        tt("pool", tq_[:], t_[:], dbc[:], ALU.add, reads=[tr_, R("dbc")], writes=[tqr_])
                    else:
                        tt("dve", tq_[:], PS[pb][:], rn[:], ALU.mult, reads=[bank(pb), R("rn")], writes=[tqr_])
                    ri, ft = (0, rt) if rt < half else (1, rt - half)
                    for o in range(2):
                        dma("pool", dst_tab(o, ri, ft), tq_[:, 256 * o:256 * (o + 1)], reads=[tqr_],
                            adds=[("gtab", tag)])
            S.barrier()

        for l in range(nlayers):
            filter_tables(l, L, T["zT"], T["decT"], T["FwT"],
                          lambda o, ri, ft, l=l: gtab[l, o, ri, ft], "L%d" % l)
        filter_tables(0, LC, T["zTc"], T["decTc"], T["FwTc"],
                      lambda o, ri, ft: gtabc[o, ri, ft], "C")

    cosT = sb("cosT", [128, NT * 128], BF16)
    sinT = sb("sinT", [128, NT * 128], BF16)
    masks = sb("masks", [128, 4, 256], BF16)
    dma("sp", cosT[:], T["cosT"].ap(), reads=[], writes=["cosT"])
    dma("sp", sinT[:], T["sinT"].ap(), reads=[], writes=["sinT"])
    dma("sp", masks[:], T["masks"].ap(), reads=[], writes=["masks"])

    rr = {}

    def nxt(name, n):
        v = rr.get(name, 0)
        rr[name] = v + 1
        return v % n

    def xsrc(l, b, ti):
        if ti < 2:
            t = T["ctx"] if l == 0 else ctxs
            return t[b, 128 * ti:128 * (ti + 1), :], ([] if l == 0 else [("xs", b, ti)])
        t = T["x"] if l == 0 else xs
        return t[b, 128 * (ti - 2):128 * (ti - 1), :], ([] if l == 0 else [("xs", b, ti)])

    def phase1(l, b, P):
        ctxq = (l == 0)
        hT, vaug, gX, PT = P["hT"], P["vaug"], P["gX"], P["PT"]
        xt, xn, junk, st4 = P["xt"], P["xn"], P["junk"], P["st4"]
        wfm, wtm, wbg = P["wfm"], P["wtm"], P["wbg"]
        ubuf, dg = P["ubuf"], P["dg"]
        qg, sq, t1, t2, rs, rr = P["qg"], P["sq"], P["t1"], P["t2"], P["rs"], P["rr"]
        ostage, hstage, bstage, den = P["ostage"], P["hstage"], P["bstage"], P["den"]

        def h_items(bh, hbanks):
            itemsA, itemsB = [], []
            for ti in range(NT):
                st = {}

                def partA(ti=ti, st=st):
                    s_ = nxt("xt", 4)
                    st["s"] = s_
                    src, sres = xsrc(l, bh, ti)
                    dma("sp", xt[s_][:], src, reads=sres, writes=[("xt", s_)])
                    act(junk[:], xt[s_][:], AF.Square, reads=[("xt", s_)], writes=["junk", ("ss", s_)],
                        accum=st4[:, 4 * s_:4 * s_ + 1])
                    act(st4[:, 4 * s_ + 1:4 * s_ + 2], st4[:, 4 * s_:4 * s_ + 1], AF.Sqrt, scale=1.0 / D, bias=EPS,
                        reads=[("ss", s_)], writes=[("ss2", s_)])
                    recip(st4[:, 4 * s_ + 2:4 * s_ + 3], st4[:, 4 * s_ + 1:4 * s_ + 2], reads=[("ss2", s_)], writes=[("rstd", s_)])
                    tsc("dve", xn[s_][:], xt[s_][:], st4[:, 4 * s_ + 2:4 * s_ + 3], None, ALU.mult, None,
                        reads=[("xt", s_), ("rstd", s_)], writes=[("xn", s_)])

                def partB(ti=ti, st=st):
                    s_ = st["s"]
                    j = 4 if ti < 2 else bh
                    pb = hbanks[nxt("hb", 1000) % len(hbanks)]
                    pe_tr([(PSB[pb][:, 128 * k:128 * (k + 1)], xn[s_][:, 128 * k:128 * (k + 1)]) for k in range(8)], ident[:],
                          reads=[("xn", s_), "ident"], writes=[bank(pb)])
                    for k in range(8):
                        tsc("dve", hT[:, k, 128 * ti:128 * (ti + 1)], PSB[pb][:, 128 * k:128 * (k + 1)],
                            AB[:, l, 0, k, j:j + 1], AB[:, l, 1, k, j:j + 1], ALU.mult, ALU.add,
                            reads=[bank(pb), ("AB", l, 0, k), ("AB", l, 1)], adds=[("hT", ti)])
                itemsA.append(partA)
                itemsB.append(partB)
            out = []
            for i in range(NT + 2):
                if i < NT:
                    out.append(itemsA[i])
                if i >= 2:
                    out.append(itemsB[i - 2])
            return out

        if not P.get("h_done", False):
            for it in h_items(b, [0, 1]):
                it()
        P["h_done"] = False

        blocks_l = [(256 + 512 * i, 512) for i in range(4)]
        blocks_c = [(0, 256)]

        def blk_tiles(c0, n):
            return [("hT", t) for t in range(c0 // 128, (c0 + n) // 128)]

        BK = {"fm": [2, 3], "nr": [(4, 5), (6, 7)]}

        def fm_items(chunk, blocks, post):
            st = {}
            items = []
            for bi, (c0, n) in enumerate(blocks):
                def item(bi=bi, c0=c0, n=n):
                    if bi == 0:
                        st["s"] = nxt("wfm", 2)
                        dma("sp", wfm[st["s"]][:], w_in_t[l, chunk], reads=[("w_in_t", l)], writes=[("wfm", st["s"])])
                    s_ = st["s"]
                    fb = BK["fm"]
                    pb = fb[nxt("fmb", 1000) % len(fb)]
                    pe([(PS[pb][:, 0:n], wfm[s_][:, k, :], hT[:, k, c0:c0 + n], k == 0, k == 7) for k in range(8)],
                       reads=[("wfm", s_)] + blk_tiles(c0, n), writes=[bank(pb)])
                    post(pb, c0, n)
                items.append(item)
            return items

        def fm_chunk(chunk, blocks, post):
            for it in fm_items(chunk, blocks, post):
                it()

        def qk_post(gcol, dst_fn):
            def post(pb, c0, n):
                s_ = nxt("nr", 2)
                pp, pq = BK["nr"][nxt("nrb", 1000) % len(BK["nr"])]
                act(qg[s_][:, 0:n], PS[pb][:, 0:n], AF.Identity, scale=gains[:, gcol:gcol + 1],
                    reads=[bank(pb), "gains"], writes=[("qg", s_)])
                act(sq[s_][:, 0:n], PS[pb][:, 0:n], AF.Square, reads=[bank(pb)], writes=[("sq", s_)])
                pe([(PS[pp][:, 0:n], permM[:], qg[s_][:, 0:n], True, True)], reads=["permM", ("qg", s_)], writes=[bank(pp)])
                pe([(PS[pq][:, 0:n], blk64[:], sq[s_][:, 0:n], True, True)], reads=["blk64", ("sq", s_)], writes=[bank(pq)])
                tt("pool", t1[s_][:, 0:n], qg[s_][:, 0:n], cosT[:, c0:c0 + n], ALU.mult,
                   reads=[("qg", s_), "cosT"], writes=[("t1", s_)])
                tt("dve", t2[s_][:, 0:n], PS[pp][:, 0:n], sinT[:, c0:c0 + n], ALU.mult,
                   reads=[bank(pp), "sinT"], writes=[("t2", s_)])
                tt("pool", t1[s_][:, 0:n], t1[s_][:, 0:n], t2[s_][:, 0:n], ALU.add,
                   reads=[("t1", s_), ("t2", s_)], writes=[("t1", s_)])
                if USE_ARS:
                    act(rr[s_][:, 0:n], PS[pq][:, 0:n], AF.Abs_reciprocal_sqrt, scale=1.0 / 64, bias=EPS,
                        reads=[bank(pq)], writes=[("rr", s_)])
                else:
                    act(rs[s_][:, 0:n], PS[pq][:, 0:n], AF.Ln, scale=1.0 / 64, bias=EPS, reads=[bank(pq)], writes=[("rs", s_)])
                    act(rr[s_][:, 0:n], rs[s_][:, 0:n], AF.Exp, scale=-0.5, reads=[("rs", s_)], writes=[("rr", s_)])
                dst, dres = dst_fn(c0, n)
                tt("dve", dst, t1[s_][:, 0:n], rr[s_][:, 0:n], ALU.mult, reads=[("t1", s_), ("rr", s_)], writes=[dres])
            return post

        def tm_group(wt, wres, chunks, ncols, tiles, post):
            for ci, ch in enumerate(chunks):
                dma("sp", wt[:, :, 128 * ci:128 * (ci + 1)], w_in_t[l, ch], reads=[("w_in_t", l)], adds=[wres])
            for ti in tiles:
                pb = nxt("tmb", 2)
                pe([(PS[pb][:, 0:ncols], hT[:, k, 128 * ti:128 * (ti + 1)], wt[:, k, 0:ncols], k == 0, k == 7) for k in range(8)],
                   reads=[wres, ("hT", ti)], writes=[bank(pb)])
                post(pb, ti)

        def attention(X, xoff, qX, kX, inter):
            esoff = 6 * l
            inter = list(inter)
            qblocks = ([(0, None)] if ctxq else []) + [(2 + 2 * i, 2 * i) for i in range(8)]
            units = []
            for (qt0, lt0) in qblocks:
                for h in range(6):
                    units.append((qt0, lt0, h))
            state = {}

            def keylist(lt0):
                if lt0 is None:
                    return [(0, None, True, True), (1, None, True, True)]
                if X == "A":
                    return [(t, None, True, True) for t in range(NT)]
                KL = [(0, None, True, True), (1, None, True, True)]
                for mi, jt in enumerate((lt0 - 1, lt0, lt0 + 1, lt0 + 2)):
                    if 0 <= jt < 16:
                        KL.append((2 + jt, mi, mi != 3, mi != 0))
                return KL

            def emit_qk(ui, lo, hi):
                qt0, lt0, h = units[ui]
                c, p = h % 3, h // 3
                rows = slice(64 * p, 64 * p + 64)
                if ui not in state:
                    state[ui] = (keylist(lt0), nxt("PT", 2))
                KL, ps_ = state[ui]
                npairs = (len(KL) + 1) // 2
                for pi in range(lo, min(hi, npairs)):
                    idxs = [i for i in (2 * pi, 2 * pi + 1) if i < len(KL)]
                    spb = nxt("sreg", 5)
                    pe([(PS[spb][:, 256 * j:256 * (j + 1)], kX[rows, 128 * KL[i][0]:128 * (KL[i][0] + 1)],
                         qX[rows, c, 128 * qt0:128 * qt0 + 256], True, True) for j, i in enumerate(idxs)],
                       reads=[("kX", KL[i][0]) for i in idxs] + [("qX", c, qt0 // 2)], writes=[bank(spb)])
                    n2 = len(idxs)
                    act(PT[ps_][:, idxs[0]:idxs[0] + n2, :].rearrange("p a b -> p (a b)"), PS[spb][:, 0:256 * n2], AF.Exp,
                        scale=0.125, reads=[bank(spb)], writes=[("PT", ps_, i) for i in idxs])
                    for i in idxs:
                        mk = KL[i][1]
                        if mk is not None:
                            pat = masks[:, 0 if mk < 2 else 2, 0:128]
                            co = 128 * (mk % 2)
                            tt("pool", PT[ps_][:, i, co:co + 128], PT[ps_][:, i, co:co + 128], pat, ALU.mult,
                               reads=[("PT", ps_, i), "masks"], writes=[("PT", ps_, i)])
                return npairs

            def emit_pv(ui):
                qt0, lt0, h = units[ui]
                c, p = h % 3, h // 3
                KL, ps_ = state.pop(ui)
                if h == 0:
                    state["os"] = nxt("ostage", 2)
                os_ = state["os"]
                ob = 5 + nxt("ob", 2)
                O3 = PS[ob][:, 0:130].rearrange("p (u e) -> p u e", e=65)
                for u2 in range(2):
                    lst = [idx for idx, kl in enumerate(KL) if kl[2 + u2]]
                    pe([(O3[:, u2, :], PT[ps_][:, idx, 128 * u2:128 * (u2 + 1)], vaug[:, KL[idx][0], p, :],
                         i == 0, i == len(lst) - 1) for i, idx in enumerate(lst)],
                       reads=[("PT", ps_, idx) for idx in lst] + [("vaug", KL[idx][0]) for idx in lst], writes=[bank(ob)])
                ds = nxt("den", 4)
                dn = den[:, 2 * ds:2 * ds + 2]
                if X == "C":
                    tsc("dve", dn, O3[:, :, 64], esk[:, esoff + h:esoff + h + 1], None, ALU.add, None,
                        reads=[bank(ob), "esk"], writes=[("den", ds)])
                    recip(dn, dn, reads=[("den", ds)], writes=[("den", ds)])
                else:
                    recip(dn, O3[:, :, 64], reads=[bank(ob)], writes=[("den", ds)])
                for u2 in range(2):
                    stt("dve", ostage[os_][:, u2, 64 * h:64 * h + 64], O3[:, u2, 0:64], den[:, 2 * ds + u2:2 * ds + u2 + 1],
                        gX[:, qt0 + u2, 64 * h:64 * h + 64], ALU.mult, ALU.mult,
                        reads=[bank(ob), ("den", ds), ("gX", qt0 + u2)], adds=[("ostage", os_)])
                if h == 5:
                    dma("pool", mix[b, 128 * qt0:128 * (qt0 + 2), xoff:xoff + 384].rearrange("(u p) n -> p u n", p=128),
                        ostage[os_][:], reads=[("ostage", os_)], adds=[("mix", b, qt0), ("mix", b, qt0 + 1)])

            NPRE = 4
            n_inter = len(inter)
            emit_qk(0, 0, 99)
            for ui in range(len(units)):
                if ui + 1 < len(units):
                    emit_qk(ui + 1, 0, NPRE)
                emit_pv(ui)
                if ui + 1 < len(units):
                    emit_qk(ui + 1, NPRE, 99)
                if inter:
                    target = ((ui + 1) * n_inter + len(units) - 1) // len(units)
                    while inter and (n_inter - len(inter)) < target:
                        inter.pop(0)()
            while inter:
                inter.pop(0)()

        all_tiles = list(range(NT))
        q_blocks = (blocks_c if ctxq else []) + blocks_l
        k_blocks = blocks_c + blocks_l
        def qk_items(X, qb, kb, chq, chk, gq, gk):
            items = []
            for c in range(3):
                def qpost_wrap(c=c):
                    base = qk_post(gq, lambda c0, n: (qb[:, c, c0:c0 + n], ("qX", c, c0 // 256)))

                    def post(pb, c0, n):
                        base(pb, c0, n)
                        if n == 512:
                            S.res[("qX", c, c0 // 256 + 1)] = S.res[("qX", c, c0 // 256)]
                    return post
                items += fm_items(chq + c, q_blocks, qpost_wrap())
            kbase = qk_post(gk, lambda c0, n: (kb[:, c0:c0 + n], ("kXblk", c0)))

            def kpost(pb, c0, n):
                kbase(pb, c0, n)
                for t in range(c0 // 128, (c0 + n) // 128):
                    S.res[("kX", t)] = S.res[("kXblk", c0)]
            items += fm_items(chk, k_blocks, kpost)
            return items

        def vg_post(pb, ti):
            copy("dve", vaug[:, ti, :, 0:64], PS[pb][:, 0:128].rearrange("p (h d) -> p h d", h=2),
                 reads=[bank(pb)], writes=[("vaug", ti)])
            if ti >= 2 or ctxq:
                act(gX[:, ti, :], PS[pb][:, 128:512], AF.Silu, reads=[bank(pb)], writes=[("gX", ti)])

        qA, kA, qC, kC = P["qX"], P["kX"], P["qX"], P["kX"]
        for it in qk_items("A", qA, kA, CH_AQ, CH_AK, 0 + l, 2 + l):
            it()
        tm_group(wtm, "wtm", [CH_AV, CH_AV + 1, CH_AV + 2, CH_AV + 3], 512, all_tiles, vg_post)
        attention("A", 0, qA, kA, [])
        for it in qk_items("C", qC, kC, CH_CQ, CH_CK, 4 + l, 6 + l):
            it()
        tm_group(wtm, "wtm", [CH_CV, CH_CV + 1, CH_CV + 2, CH_CV + 3], 512, all_tiles, vg_post)

        def finish_attention_c():
            if b + 1 < P["nb"]:
                inter = h_items(b + 1, [7])
                P["h_done"] = True
            else:
                inter = []
            attention("C", 640, qC, kC, inter)

        for tix, (tname, ch0) in enumerate((("v", CH_BV), ("x1", CH_BX1), ("x2", CH_BX2))):
            for jj in range(2):
                chn = ch0 + jj
                ccol = 2 * tix + jj

                pbs = nxt("pbuf", 2)
                pbuf = P["pbuf"][pbs]
                pres = ("pbuf", pbs)

                def hpost(pb, c0, n, pbuf=pbuf, pres=pres):
                    off = c0 + 1 if c0 < 256 else c0 + 3
                    copy("act", pbuf[:, off:off + n], PS[pb][:, 0:n], reads=[bank(pb)], adds=[pres])
                fm_chunk(chn, q_blocks, hpost)
                for (c0, n) in q_blocks:
                    off = c0 + 1 if c0 < 256 else c0 + 3
                    pb = 2 + nxt("fmb", 2)
                    pe([(PS[pb][:, 0:n], dg[:, 6 * j + ccol, :], pbuf[:, off - 1 + j:off - 1 + j + n], j == 0, j == 2)
                        for j in range(3)], reads=[pres, "dg"], writes=[bank(pb)])
                    act(ubuf[:, jj, c0:c0 + n], PS[pb][:, 0:n], AF.Identity,
                        bias=cwb[:, 36 + 6 * l + ccol:36 + 6 * l + ccol + 1],
                        reads=[bank(pb), "cwb"], adds=[("ubuf", jj, 0 if c0 < 256 else 256)])
            groups = ([("c", 0, 2)] if ctxq else []) + [("l", 4 * g, 4) for g in range(4)]
            for (kind, t0, nt_) in groups:
                pb = nxt("hb", 2)
                base = 0 if kind == "c" else 256
                pe_tr([(PSB[pb][:, 256 * q + 128 * jj:256 * q + 128 * (jj + 1)],
                        ubuf[:, jj, base + 128 * (t0 + q):base + 128 * (t0 + q + 1)]) for q in range(nt_) for jj in range(2)],
                      ident[:], reads=[("ubuf", 0, base), ("ubuf", 1, base), "ident"], writes=[bank(pb)])
                hs = nxt("hstage", 2)
                copy("act", hstage[hs][:, 0:256 * nt_], PSB[pb][:, 0:256 * nt_], reads=[bank(pb)], writes=[("hstage", hs)])
                dstt = (hyc if kind == "c" else hyl)[tname]
                dma("act", dstt[t0:t0 + nt_, :, b, :].rearrange("c p n -> p c n"),
                    hstage[hs][:, 0:256 * nt_].rearrange("p (c n) -> p c n", n=256),
                    reads=[("hstage", hs)], adds=[("hy", kind, tname)])

        def bg_post(pb, ti):
            bs = nxt("bstage", 2)
            act(bstage[bs][:], PS[pb][:, 0:256], AF.Silu, reads=[bank(pb)], writes=[("bstage", bs)])
            if ti < 2:
                d_, kind = hyc["bg"][ti, :, b, :], "c"
            else:
                d_, kind = hyl["bg"][ti - 2, :, b, :], "l"
            dma("act", d_, bstage[bs][:], reads=[("bstage", bs)], adds=[("hy", kind, "bg")])
        tm_group(wbg, "wbg", [CH_BG, CH_BG + 1], 256, (all_tiles if ctxq else all_tiles[2:]), bg_post)
        finish_attention_c()

    def phase2(l, kind, P):
        nft = 8 if kind == "l" else 1
        hy = hyl if kind == "l" else hyc
        Fd = T["FwEO"] if kind == "l" else T["FwEOc"]
        Fi = T["FinvT"] if kind == "l" else T["FinvTc"]
        tb = 2 if kind == "l" else 0
        U, ZZ, Gr, Gi = P["U"], P["ZZ"], P["Gr"], P["Gi"]
        fw, fin, xo, bgt, ys, yt = P["fw"], P["fin"], P["xo"], P["bgt"], P["ys"], P["yt"]
        osb, XX, mm, zz = P["osb"], P["XX"], P["mm"], P["zz"]
        tagk = ("L%d" % l) if kind == "l" else "C"

        def tile_src(name, cp, par):
            out = []
            for hlf in range(2):
                src = hy[name][2 * cp + hlf].rearrange("(p two) b n -> p two (b n)", two=2)[:, par, :]
                out.append((slice(64 * hlf, 64 * hlf + 64), src))
            return out

        for cp in range(nft):
            for par in range(2):
                for (psl, src) in tile_src("v", cp, par):
                    dma("sp", U[par][psl, cp, :], src, reads=[("hy", kind, "v")], adds=[("U", par, cp)])
        for o in range(2):
            tab = (lambda ri: gtab[l, o, ri]) if kind == "l" else (lambda ri: gtabc[o, ri])
            dma("sp", Gr[:, 0:2 * nft, :], tab(0).rearrange("f p n -> p f n"), reads=[("gtab", tagk)], writes=["Gr"])
            dma("sp", Gi[:, 0:2 * nft, :], tab(1).rearrange("f p n -> p f n"), reads=[("gtab", tagk)], writes=["Gi"])
            for ft in range(nft):
                fs = nxt("fw", 2)
                dma("sp", fw[fs][:, :, 0:nft, :], Fd[ft].rearrange("a p k n -> p a k n"), reads=[], writes=[("fw", fs)])
                for hh in range(2):
                    cs = slice(512 * hh, 512 * (hh + 1))
                    b0 = 4 * nxt("fwb", 2)
                    pe([(PS[b0 + a][:], fw[fs][:, a, kc, :], U[a // 2][:, kc, cs], kc == 0, kc == nft - 1)
                        for a in range(4) for kc in range(nft)],
                       reads=[("fw", fs)] + [("U", par, kc) for par in range(2) for kc in range(nft)],
                       writes=[bank(b0 + a) for a in range(4)])
                    q = nxt("osb", 2)
                    copy("act", osb[q][0][:], PS[b0 + 2][:], reads=[bank(b0 + 2)], writes=[("osr", q)])
                    copy("act", osb[q][1][:], PS[b0 + 3][:], reads=[bank(b0 + 3)], writes=[("osi", q)])
                    xq = nxt("XX", 2)
                    xr, xi, xr2, xi2 = XX[xq]
                    tt("dve", xr[:], PS[b0][:], osb[q][0][:], ALU.add, reads=[bank(b0), ("osr", q)], writes=[("xr", xq)])
                    tt("dve", xr2[:], PS[b0][:], osb[q][0][:], ALU.subtract, reads=[bank(b0), ("osr", q)], writes=[("xr2", xq)])
                    tt("dve", xi[:], PS[b0 + 1][:], osb[q][1][:], ALU.add, reads=[bank(b0 + 1), ("osi", q)], writes=[("xi", xq)])
                    tt("dve", xi2[:], osb[q][1][:], PS[b0 + 1][:], ALU.subtract, reads=[bank(b0 + 1), ("osi", q)], writes=[("xi2", xq)])
                    for bb in range(2):
                        c2 = slice(256 * bb, 256 * (bb + 1))
                        zc = slice(512 * hh + 256 * bb, 512 * hh + 256 * (bb + 1))
                        zq = nxt("zz", 2)
                        zr, zi, zr2, zi2 = zz[zq]

                        def prod(dst, dres, xa, xares, ga, gares, xb, xbres, gb, gbres, op, gt):
                            mq = nxt("mm", 4)
                            ma, mb = mm[mq]
                            tt("dve", ma[:], xa[:, c2], ga[:, gt, :], ALU.mult, reads=[(xares, xq), gares], writes=[("ma", mq)])
                            tt("dve", mb[:], xb[:, c2], gb[:, gt, :], ALU.mult, reads=[(xbres, xq), gbres], writes=[("mb", mq)])
                            tt("pool", dst[:], ma[:], mb[:], op, reads=[("ma", mq), ("mb", mq)], writes=[(dres, zq)])
                        prod(zr, "zr", xr, "xr", Gr, "Gr", xi, "xi", Gi, "Gi", ALU.subtract, ft)
                        prod(zi, "zi", xr, "xr", Gi, "Gi", xi, "xi", Gr, "Gr", ALU.add, ft)
                        prod(zr2, "zr2", xr2, "xr2", Gr, "Gr", xi2, "xi2", Gi, "Gi", ALU.subtract, nft + ft)
                        prod(zi2, "zi2", xr2, "xr2", Gi, "Gi", xi2, "xi2", Gr, "Gr", ALU.add, nft + ft)
                        tt("pool", ZZ[0][:, ft, zc], zr[:], zr2[:], ALU.add, reads=[("zr", zq), ("zr2", zq)], adds=[("Z", 0, ft)])
                        tt("pool", ZZ[1][:, ft, zc], zr[:], zr2[:], ALU.subtract, reads=[("zr", zq), ("zr2", zq)], adds=[("Z", 1, ft)])
                        tt("pool", ZZ[0][:, nft + ft, zc], zi[:], zi2[:], ALU.subtract, reads=[("zi", zq), ("zi2", zq)],
                           adds=[("Z", 0, nft + ft)])
                        tt("pool", ZZ[1][:, nft + ft, zc], zi[:], zi2[:], ALU.add, reads=[("zi", zq), ("zi2", zq)],
                           adds=[("Z", 1, nft + ft)])
            xname = "x1" if o == 0 else "x2"
            for cp in range(nft):
                for par in range(2):
                    is_ = nxt("fin", 2)
                    dma("sp", fin[is_][:, 0:2 * nft, :], Fi[cp, par], reads=[], writes=[("fin", is_)])
                    xs_ = nxt("xo", 2)
                    for (psl, src) in tile_src(xname, cp, par):
                        dma("sp", xo[xs_][psl, :], src, reads=[("hy", kind, xname)], adds=[("xo", xs_)])
                    if o == 1:
                        for (psl, src) in tile_src("bg", cp, par):
                            dma("sp", bgt[xs_][psl, :], src, reads=[("hy", kind, "bg")], adds=[("bgt", xs_)])
                    pr0 = 2 * nxt("ivb", 4)
                    pe([(PS[pr0 + hh][:], fin[is_][:, kc, :], ZZ[par][:, kc, 512 * hh:512 * (hh + 1)], kc == 0, kc == 2 * nft - 1)
                        for kc in range(2 * nft) for hh in range(2)],
                       reads=[("fin", is_)] + [("Z", par, kc) for kc in range(2 * nft)], writes=[bank(pr0), bank(pr0 + 1)])
                    for hh in range(2):
                        cs = slice(512 * hh, 512 * (hh + 1))
                        if o == 0:
                            tt("dve", U[par][:, cp, cs], PS[pr0 + hh][:], xo[xs_][:, cs], ALU.mult,
                               reads=[bank(pr0 + hh), ("xo", xs_)], adds=[("U", par, cp)])
                        else:
                            ysl = nxt("ys", 2)
                            tt("dve", yt[ysl][:], PS[pr0 + hh][:], xo[xs_][:, cs], ALU.mult,
                               reads=[bank(pr0 + hh), ("xo", xs_)], writes=[("yt", ysl)])
                            tt("pool", ys[xs_][:, cs], yt[ysl][:], bgt[xs_][:, cs], ALU.mult,
                               reads=[("yt", ysl), ("bgt", xs_)], adds=[("ys", xs_)])
                    if o == 1:
                        r0 = 128 * tb + 256 * cp
                        dma("pool", mix[:, r0:r0 + 256, 384:640].rearrange("b (p two) n -> p two b n", two=2)[:, par],
                            ys[xs_][:].rearrange("p (b n) -> p b n", n=256), reads=[("ys", xs_)],
                            adds=[("mix", bb, tb + 2 * cp + hf) for bb in range(NB) for hf in range(2)])

    def phase3(l, b, P):
        wout, gate, mixt, mixT, xt, tmp, xo = P["wout"], P["gate"], P["mixt"], P["mixT"], P["xt"], P["tmp"], P["xo"]
        last = (l == nlayers - 1)
        tiles = list(range(NT)) if l == 0 else list(range(2, NT))
        dma("sp", gate[0][:], modrow[l, b, 2048:3072].partition_broadcast(128), reads=[("modrow", l)], writes=[("gate", 0)])
        if l == 0:
            dma("sp", gate[1][:], modrow[l, 4, 2048:3072].partition_broadcast(128), reads=[("modrow", l)], writes=[("gate", 1)])
        for ti in tiles:
            ms = nxt("mixt", 4)
            dma("sp", mixt[ms][:], mix[b, 128 * ti:128 * (ti + 1), :], reads=[("mix", b, ti)], writes=[("mixt", ms)])
            pb = nxt("hb", 2)
            pe_tr([(PSB[pb][:, 128 * k:128 * (k + 1)], mixt[ms][:, 128 * k:128 * (k + 1)]) for k in range(8)], ident[:],
                  reads=[("mixt", ms), "ident"], writes=[bank(pb)])
            copy("act", mixT[ms][:], PSB[pb][:], reads=[bank(pb)], writes=[("mixT", ms)])
            ob = 2 + 2 * nxt("opb", 3)
            pe([(PS[ob + hh][:], mixT[ms][:, 128 * k:128 * (k + 1)], wout[:, k, 512 * hh:512 * (hh + 1)], k == 0, k == 7)
                for k in range(8) for hh in range(2)], reads=[("mixT", ms), "wout"], writes=[bank(ob), bank(ob + 1)])
            s_ = nxt("xt3", 4)
            src, sres = xsrc(l, b, ti)
            dma("sp", xt[s_][:], src, reads=sres, writes=[("xt3", s_)])
            gsel = 1 if ti < 2 else 0
            for hh in range(2):
                cs = slice(512 * hh, 512 * (hh + 1))
                tt("dve", tmp[s_][:, cs], PS[ob + hh][:], gate[gsel][:, cs], ALU.mult,
                   reads=[bank(ob + hh), ("gate", gsel)], adds=[("tmp3", s_)])
            tt("pool", xo[s_][:], xt[s_][:], tmp[s_][:], ALU.add, reads=[("xt3", s_), ("tmp3", s_)], writes=[("xo3", s_)])
            if last:
                dst = y[b, 128 * (ti - 2):128 * (ti - 1), :]
                dres = ("y", b, ti)
            elif ti < 2:
                dst = ctxs[b, 128 * ti:128 * (ti + 1), :]
                dres = ("xs", b, ti)
            else:
                dst = xs[b, 128 * (ti - 2):128 * (ti - 1), :]
                dres = ("xs", b, ti)
            dma("pool", dst, xo[s_][:], reads=[("xo3", s_)], writes=[dres])

    for l in range(nlayers):
        S.epoch = l + 1
        if "p1" in phases:
            with ExitStack() as p1:
                P = {}
                P["hT"] = sb("hT", [128, 8, NT * 128], BF16, p1)
                P["qX"] = sb("qX", [128, 3, NT * 128], BF16, p1)
                P["kX"] = sb("kX", [128, NT * 128], BF16, p1)
                P["vaug"] = sb("vaug", [128, NT, 2, 65], BF16, p1)
                P["gX"] = sb("gX", [128, NT, 384], BF16, p1)
                P["PT"] = [sb("PT%d" % i, [128, NT, 256], BF16, p1) for i in range(2)]
                P["xt"] = [sb("xt%d" % i, [128, D], F32, p1) for i in range(4)]
                P["xn"] = [sb("xn%d" % i, [128, D], BF16, p1) for i in range(4)]
                P["junk"] = sb("junk", [128, D], BF16, p1)
                P["st4"] = sb("st4", [128, 16], F32, p1)
                P["wfm"] = [sb("wfm%d" % i, [128, 8, 128], BF16, p1) for i in range(2)]
                P["wtm"] = sb("wtm", [128, 8, 512], BF16, p1)
                P["wbg"] = sb("wbg", [128, 8, 256], BF16, p1)
                P["pbuf"] = [sb("pbuf%d" % i, [128, 2308], BF16, p1) for i in range(2)]
                P["dg"] = sb("dg", [128, 18, 128], BF16, p1)
                for jc in range(18):
                    tsc("dve", P["dg"][:, jc, :], ident[:], cwb[:, 18 * l + jc:18 * l + jc + 1], None, ALU.mult, None,
                        reads=["ident", "cwb"], adds=["dg"])
                P["ubuf"] = sb("ubuf", [128, 2, NT * 128], BF16, p1)
                for nm, dt_ in (("qg", BF16), ("sq", BF16), ("t1", BF16), ("t2", BF16), ("rs", F32), ("rr", F32)):
                    P[nm] = [sb(nm + "%d" % i, [128, 512], dt_, p1) for i in range(2)]
                P["ostage"] = [sb("ostage%d" % i, [128, 2, 384], BF16, p1) for i in range(2)]
                P["hstage"] = [sb("hstage%d" % i, [128, 1024], BF16, p1) for i in range(2)]
                P["bstage"] = [sb("bstage%d" % i, [128, 256], BF16, p1) for i in range(2)]
                P["den"] = sb("den", [128, 8], F32, p1)
                for i in range(2):
                    memset("pool", P["pbuf"][i][:], 0.0, writes=[("pbuf", i)])
                memset("pool", P["vaug"][:], 1.0, writes=[("vaug", t) for t in range(NT)])
                P["nb"] = nb
                P["h_done"] = False
                for b in range(nb):
                    phase1(l, b, P)
            S.barrier()
        if "p2" in phases:
            with ExitStack() as p2:
                P = {}
                P["U"] = [sb("U%d" % i, [128, 8, 1024], BF16, p2) for i in range(2)]
                P["ZZ"] = [sb("ZZ%d" % i, [128, 16, 1024], BF16, p2) for i in range(2)]
                P["Gr"] = sb("Gr", [128, 16, 256], BF16, p2)
                P["Gi"] = sb("Gi", [128, 16, 256], BF16, p2)
                P["fw"] = [sb("fw%d" % i, [128, 4, 8, 128], BF16, p2) for i in range(2)]
                P["fin"] = [sb("fin%d" % i, [128, 16, 128], BF16, p2) for i in range(2)]
                P["xo"] = [sb("xo%d" % i, [128, 1024], BF16, p2) for i in range(2)]
                P["bgt"] = [sb("bgt%d" % i, [128, 1024], BF16, p2) for i in range(2)]
                P["yt"] = [sb("yt%d" % i, [128, 512], F32, p2) for i in range(2)]
                P["ys"] = [sb("ys%d" % i, [128, 1024], BF16, p2) for i in range(2)]
                P["osb"] = [[sb("os%d%d" % (i, j), [128, 512], F32, p2) for j in range(2)] for i in range(2)]
                P["XX"] = [[sb("xx%d%d" % (i, j), [128, 512], BF16, p2) for j in range(4)] for i in range(2)]
                P["mm"] = [[sb("mm%d%d" % (i, j), [128, 256], F32, p2) for j in range(2)] for i in range(4)]
                P["zz"] = [[sb("zz%d%d" % (i, j), [128, 256], F32, p2) for j in range(4)] for i in range(2)]
                phase2(l, "l", P)
                if l == 0 and nlayers > 1:
                    phase2(l, "c", P)
            S.barrier()
        if "p3" in phases:
            with ExitStack() as p3:
                P = {}
                P["wout"] = sb("wout", [128, 8, D], BF16, p3)
                P["gate"] = [sb("gate%d" % i, [128, D], F32, p3) for i in range(2)]
                P["mixt"] = [sb("mixt%d" % i, [128, D], BF16, p3) for i in range(4)]
                P["mixT"] = [sb("mixT%d" % i, [128, D], BF16, p3) for i in range(4)]
                P["xt"] = [sb("x3t%d" % i, [128, D], F32, p3) for i in range(4)]
                P["tmp"] = [sb("x3m%d" % i, [128, D], F32, p3) for i in range(4)]
                P["xo"] = [sb("x3o%d" % i, [128, D], F32, p3) for i in range(4)]
                dma("sp", P["wout"][:], w_out_t[l], reads=[("w_out_t", l)], writes=["wout"])
                for b in range(nb):
                    phase3(l, b, P)
            S.barrier()

    if "modrow" in dbg_out:
        dma("sp", dbg_out["modrow"].ap(), modrow.ap(), reads=[("modrow", 0), ("modrow", 1)], writes=["dbg_modrow"])
    if "gtab" in dbg_out:
        dma("sp", dbg_out["gtab"].ap(), gtab.ap(), reads=[("gtab", "L0"), ("gtab", "L1")], writes=["dbg_gtab"])
    if "gtabc" in dbg_out:
        dma("sp", dbg_out["gtabc"].ap(), gtabc.ap(), reads=[("gtab", "C")], writes=["dbg_gtabc"])
    if "AB" in dbg_out:
        dma("sp", dbg_out["AB"].ap(), AB[:],
            reads=[("AB", l, 1) for l in range(2)] + [("AB", l, 0, k) for l in range(2) for k in range(8)],
            writes=["dbg_AB"])

    S.barrier()
    for name, src in (("mix", mix), ("xs", xs), ("ctxs", ctxs), ("hy_v", hyl["v"]), ("hy_x1", hyl["x1"]), ("hy_x2", hyl["x2"]),
                      ("hy_bg", hyl["bg"]), ("hyc_v", hyc["v"]), ("w_in_t", w_in_t)):
        if name in dbg_out:
            dma("sp", dbg_out[name].ap(), src.ap(), reads=[], writes=["dbg_" + name])
    S.finish()
    S.make_sems(es)
    with nc.Block() as block:
        @block.tensor
        def _(e):
            S.replay("pe", e)

        @block.scalar
        def _(e):
            S.replay("act", e)

        @block.vector
        def _(e):
            S.replay("dve", e)

        @block.gpsimd
        def _(e):
            S.replay("pool", e)

        @block.sync
        def _(e):
            S.replay("sp", e)
    es.close()
    return nc


def make_in_maps(inputs):
    cst = _consts()
    maps = []
    for i in range(NCORES):
        m = {}
        b0 = i * NB
        m["x"] = np.ascontiguousarray(inputs["x"][b0:b0 + NB])
        m["ctx"] = np.ascontiguousarray(inputs["ctx"][b0:b0 + NB])
        m["cR"] = np.ascontiguousarray(
            np.concatenate([inputs["c"][b0:b0 + NB], inputs["c_ctx"][None, :]], 0))
        for k in PARAM_SHAPES:
            m[k] = np.ascontiguousarray(inputs[k])
        for k, v in cst.items():
            m["c_" + k] = v
        maps.append(m)
    return maps


def kernel(**inputs):
    inputs = {k: np.asarray(v) for k, v in inputs.items()}
    nc = build_program()
    maps = make_in_maps(inputs)
    res = run_bass_kernel_spmd(nc, maps, core_ids=list(range(NCORES)))
    return np.concatenate([r["y"] for r in res.results], axis=0).astype(np.float32)
```

```python
import math
from contextlib import ExitStack

import numpy as np
import ml_dtypes

import concourse.bass as bass
import concourse.mybir as mybir
from concourse.bass_utils import run_bass_kernel_spmd

F32 = mybir.dt.float32
BF16 = mybir.dt.bfloat16
AF = mybir.ActivationFunctionType
ALU = mybir.AluOpType

NCORES = 8
NB = 4
D = 1024
L = 2048
LC = 256
NT = 18
EPS = 1e-6
MAGIC = 12582912.0
USE_ARS = False
NF = 4096


class Sched:
    ENG = ("pe", "act", "dve", "pool", "sp")

    def __init__(self, nc, ndma=8):
        self.nc = nc
        self.streams = {e: [] for e in self.ENG}
        self.cnt = {}
        self.waited = {}
        self.res = {}
        self.dmai = {e: 0 for e in self.ENG}
        self.ndma = ndma
        self.epoch = 0
        self.semh = {}

    def op(self, eng, fn, reads=(), writes=(), adds=(), dma=False):
        deps = {}

        def add(k, v):
            if deps.get(k, 0) < v:
                deps[k] = v

        for r in reads:
            st = self.res.get(r)
            if st:
                for k, v in st[0].items():
                    add(k, v)
                if isinstance(r, tuple) and r[0] == "bank":
                    for k, v in st[1].items():
                        add(k, v)
        for w in writes:
            st = self.res.get(w)
            if st:
                for k, v in st[0].items():
                    add(k, v)
                for k, v in st[1].items():
                    add(k, v)
        for w in adds:
            st = self.res.get(w)
            if st:
                for k, v in st[1].items():
                    add(k, v)
        if dma:
            i = self.dmai[eng]
            self.dmai[eng] += 1
            key = ("dma", eng, i % self.ndma)
            prev = self.cnt.get(key, 0)
            if prev:
                add(key, prev)
            inc = 16
        else:
            key = ("done", eng, self.epoch)
            inc = 1
        newv = self.cnt.get(key, 0) + inc
        st = self.streams[eng]
        for k, v in deps.items():
            if eng == "pe" and k[0] == "done" and k[1] == "pe":
                continue
            if self.waited.get((eng, k), 0) >= v:
                continue
            self.waited[(eng, k)] = v
            st.append(("w", k, v))
        st.append(("o", fn, key, inc))
        self.cnt[key] = newv
        for w in writes:
            self.res[w] = [{key: newv}, {}]
        for w in adds:
            s_ = self.res.setdefault(w, [{}, {}])
            if s_[0].get(key, 0) < newv:
                s_[0][key] = newv
        for r in reads:
            s_ = self.res.setdefault(r, [{}, {}])
            if s_[1].get(key, 0) < newv:
                s_[1][key] = newv

    def barrier(self):
        for eng in self.ENG:
            for k, v in list(self.cnt.items()):
                if eng == "pe" and k[0] == "done" and k[1] == "pe":
                    continue
                if self.waited.get((eng, k), 0) >= v:
                    continue
                self.waited[(eng, k)] = v
                self.streams[eng].append(("w", k, v))

    def finish(self):
        for eng in self.ENG:
            for k, v in list(self.cnt.items()):
                if k[0] == "dma" and k[1] == eng:
                    self.streams[eng].append(("w", k, v))

    def make_sems(self, es):
        keys = set(self.cnt.keys())
        for i, k in enumerate(sorted(keys, key=str)):
            self.semh[k] = es.enter_context(self.nc.semaphore("s%d" % i))

    def replay(self, eng, e):
        for it in self.streams[eng]:
            if it[0] == "w":
                e.wait_ge(self.semh[it[1]], it[2])
            else:
                ins = it[1](e)
                ins.then_inc(self.semh[it[2]], it[3])


_CONST = None


def _bf(a):
    return np.ascontiguousarray(a.astype(np.float32)).astype(ml_dtypes.bfloat16)


def _partner(d):
    blk, r = divmod(d, 64)
    axis, r2 = divmod(r, 32)
    half, f = divmod(r2, 16)
    return blk * 64 + axis * 32 + (1 - half) * 16 + f


def _hy_pos(Lx):
    t = np.linspace(0.0, 1.0, Lx, dtype=np.float32).astype(np.float64)
    bands = np.linspace(1e-4, 15.0, 16, dtype=np.float32).astype(np.float64)
    w = (2.0 * math.pi * np.arange(Lx, dtype=np.float32) / Lx).astype(np.float32).astype(np.float64)
    z = np.concatenate([t[:, None], np.cos(bands[None] * w[:, None]), np.sin(bands[None] * w[:, None])], -1)
    min_decay = math.log(1e-2) / 1.5
    max_decay = math.log(1e-2) / 0.3
    deltas = np.linspace(min_decay, max_decay, 256, dtype=np.float32).astype(np.float64)
    dec = np.exp(-t[:, None] * np.abs(deltas)[None])
    return z, dec


def _hy_consts(Lx):
    N = 2 * Lx
    z, dec = _hy_pos(Lx)
    zc = np.zeros((N, 33), np.float64)
    dc = np.zeros((N, 256), np.float64)
    zc[:Lx] = z
    dc[:Lx] = dec
    for m in range(Lx + 1, N):
        zc[m] = z[N - m]
        dc[m] = -dec[N - m]
    H = Lx // 2
    nft = H // 128
    fidx = np.concatenate([np.arange(H), Lx - 1 - np.arange(H)]).astype(np.float64)
    om_all = 2.0 * math.pi * (fidx + 0.5) / N
    m = np.arange(N, dtype=np.float64)
    ang = m[:, None] * om_all[None, :]
    Fw = np.concatenate([np.cos(ang), -np.sin(ang)], 1)
    nrt = N // 128
    nch = N // 128
    FwT = Fw.reshape(nch, 128, nrt, 128).transpose(2, 1, 0, 3)
    om = 2.0 * math.pi * (np.arange(H, dtype=np.float64) + 0.5) / N
    tau = np.arange(H, dtype=np.float64)
    angE = (2 * tau)[:, None] * om[None, :]
    angO = (2 * tau + 1)[:, None] * om[None, :]
    parts = [np.cos(angE), -np.sin(angE), np.cos(angO), -np.sin(angO)]
    FwEO = np.stack([p_.reshape(nft, 128, nft, 128).transpose(2, 1, 0, 3) for p_ in parts], 1)
    inv = []
    for par in range(2):
        mat = (2.0 / N) * np.concatenate([parts[2 * par].T, parts[2 * par + 1].T], 0)
        inv.append(mat.reshape(2 * nft, 128, nft, 128).transpose(2, 1, 0, 3))
    FinvT = np.stack(inv, 1)
    zT = np.ascontiguousarray(zc.T).astype(np.float32)
    decT = np.ascontiguousarray(dc.reshape(nch, 128, 256).transpose(1, 0, 2)).astype(np.float32)
    return zT, decT, _bf(FwT), _bf(FinvT), _bf(FwEO)


def _consts():
    global _CONST
    if _CONST is not None:
        return _CONST
    c = {}
    c["ident"] = _bf(np.eye(128))
    pm = np.zeros((128, 128), np.float32)
    for m in range(128):
        pm[_partner(m), m] = 1.0
    c["permM"] = _bf(pm)
    b64 = np.zeros((128, 128), np.float32)
    b64[:64, :64] = 1.0
    b64[64:, 64:] = 1.0
    c["blk64"] = _bf(b64)
    c["ones128"] = _bf(np.ones((128, 128)))
    inv_freq = (10000.0 ** (-np.arange(16, dtype=np.float32) / 16)).astype(np.float32)
    tok = np.arange(L)
    row = (tok // 64).astype(np.float32)
    col = (tok % 64).astype(np.float32)
    cosT = np.ones((128, NT * 128), np.float32)
    sinT = np.zeros((128, NT * 128), np.float32)
    for m in range(128):
        d = m % 64
        axis, r2 = divmod(d, 32)
        half, f = divmod(r2, 16)
        pos = row if axis == 0 else col
        ang = (pos * inv_freq[f]).astype(np.float32)
        cosT[m, 256:] = np.cos(ang)
        sinT[m, 256:] = np.sin(ang) * (-1.0 if half == 0 else 1.0)
    c["cosT"] = _bf(cosT)
    c["sinT"] = _bf(sinT)
    ki = np.arange(128)[:, None]
    qi = np.arange(128)[None, :]
    prev = (qi <= ki).astype(np.float32)
    nxt = (ki <= qi).astype(np.float32)
    one = np.ones((128, 128), np.float32)
    zero = np.zeros((128, 128), np.float32)
    masks = np.stack([np.concatenate([prev, zero], 1), np.concatenate([one, prev], 1),
                      np.concatenate([nxt, one], 1), np.concatenate([zero, nxt], 1)], 1)
    c["masks"] = _bf(masks)
    c["identf"] = np.eye(128, dtype=np.float32)
    zT, decT, FwT, FinvT, FwEO = _hy_consts(L)
    c["zT"], c["decT"], c["FwT"], c["FinvT"], c["FwEO"] = zT, decT, FwT, FinvT, FwEO
    zTc, decTc, FwTc, FinvTc, FwEOc = _hy_consts(LC)
    c["zTc"], c["decTc"], c["FwTc"], c["FinvTc"], c["FwEOc"] = zTc, decTc, FwTc, FinvTc, FwEOc
    _CONST = c
    return c


PARAM_SHAPES = {
    "norm_g": [2, 1024], "w_mod": [2, 1024, 3072], "b_mod": [2, 3072], "w_in": [2, 1024, 3072],
    "w_out": [2, 1024, 1024], "qn_a": [2, 64], "kn_a": [2, 64], "qn_c": [2, 64], "kn_c": [2, 64],
    "sink_c": [2, 6], "hy_conv_w": [2, 3, 768], "hy_conv_b": [2, 768], "hy_w1": [2, 33, 64],
    "hy_b1": [2, 64], "hy_w2": [2, 64, 64], "hy_b2": [2, 64], "hy_w3": [2, 64, 1024],
    "hy_freq": [2, 2, 64], "hy_bias": [2, 2, 256],
}

CH_AQ, CH_AK, CH_AV, CH_AG = 0, 3, 4, 5
CH_BV, CH_BX1, CH_BX2, CH_BG = 8, 10, 12, 14
CH_CQ, CH_CK, CH_CV, CH_CG = 16, 19, 20, 21


def build_program(nlayers=2, phases=("p0", "p1", "p2", "p3"), debug=(), nb=NB):
    nc = bass.Bass("TRN2", target_bir_lowering=False)
    cst = _consts()
    es = ExitStack()
    S = Sched(nc)

    def din(name, shape, dt=F32):
        return nc.dram_tensor(name, list(shape), dt, kind="ExternalInput")

    T = {}
    T["x"] = din("x", [NB, L, D])
    T["ctx"] = din("ctx", [NB, LC, D])
    T["cR"] = din("cR", [5, D])
    for k, shp in PARAM_SHAPES.items():
        T[k] = din(k, shp)
    for k, v in cst.items():
        T[k] = din("c_" + k, v.shape, BF16 if v.dtype == ml_dtypes.bfloat16 else F32)
    y = nc.dram_tensor("y", [NB, L, D], F32, kind="ExternalOutput")

    def scratch(name, shape, dt):
        return nc.dram_tensor(name, list(shape), dt)

    w_in_t = scratch("w_in_t", [2, 24, 128, 8, 128], BF16)
    w_out_t = scratch("w_out_t", [2, 128, 8, 1024], BF16)
    modrow = scratch("modrow", [2, 5, 3072], F32)
    gtab = scratch("gtab", [2, 2, 2, 16, 128, 256], BF16)
    gtabc = scratch("gtabc", [2, 2, 2, 128, 256], BF16)
    hyl = {n: scratch("hy_" + n, [16, 128, NB, 256], BF16) for n in ("v", "x1", "x2", "bg")}
    hyc = {n: scratch("hyc_" + n, [2, 128, NB, 256], BF16) for n in ("v", "x1", "x2", "bg")}
    mix = scratch("mix", [NB, NT * 128, D], BF16)
    xs = scratch("xs", [NB, L, D], F32)
    ctxs = scratch("ctxs", [NB, LC, D], F32)
    dbg_out = {}
    for ent in debug:
        name, shape = ent[0], ent[1]
        dbg_out[name] = nc.dram_tensor("dbg_" + name, list(shape), BF16 if (len(ent) > 2 and ent[2] == "bf16") else F32,
                                       kind="ExternalOutput")

    used_names = {}

    def sb(name, shape, dt, stk=None):
        n = used_names.get(name, 0)
        used_names[name] = n + 1
        if n:
            name = "%s_%d" % (name, n)
        return (stk or es).enter_context(nc.sbuf_tensor(name, list(shape), dt))

    PS = [es.enter_context(nc.psum_tensor("ps%d" % i, [128, 512], F32)) for i in range(8)]
    PSB = [p.bitcast(BF16) for p in PS]

    def bank(i):
        return ("bank", i)

    def dma(q, out, in_, reads, writes=(), adds=(), **kw):
        def fn(e):
            return e.dma_start(out=out, in_=in_, **kw)
        S.op(q, fn, reads=reads, writes=writes, adds=adds, dma=True)

    def pe(mms, reads, writes):
        mms = list(mms)

        def fn(e):
            ins = None
            for (o, l, r, st, sp) in mms:
                ins = e.matmul(o, lhsT=l, rhs=r, start=st, stop=sp)
            return ins
        S.op("pe", fn, reads=reads, writes=writes)

    def pe_tr(trs, ident, reads, writes):
        trs = list(trs)

        def fn(e):
            ins = None
            for (o, i) in trs:
                ins = e.transpose(o, i, ident)
            return ins
        S.op("pe", fn, reads=reads, writes=writes)

    def act(out, in_, func, reads, writes=(), scale=1.0, bias=0.0, accum=None, adds=()):
        def fn(e):
            kw = {}
            if accum is not None:
                kw["accum_out"] = accum
            return e.activation(out=out, in_=in_, func=func, bias=bias, scale=scale, **kw)
        S.op("act", fn, reads=reads, writes=writes, adds=adds)

    def tsc(eng, out, in0, s1, s2, op0, op1, reads, writes=(), adds=()):
        def fn(e):
            if s2 is None:
                return e.tensor_scalar(out=out, in0=in0, scalar1=s1, scalar2=None, op0=op0)
            return e.tensor_scalar(out=out, in0=in0, scalar1=s1, scalar2=s2, op0=op0, op1=op1)
        S.op(eng, fn, reads=reads, writes=writes, adds=adds)

    def tt(eng, out, in0, in1, op, reads, writes=(), adds=()):
        def fn(e):
            return e.tensor_tensor(out=out, in0=in0, in1=in1, op=op)
        S.op(eng, fn, reads=reads, writes=writes, adds=adds)

    def stt(eng, out, in0, scalar, in1, op0, op1, reads, writes=(), adds=()):
        def fn(e):
            return e.scalar_tensor_tensor(out=out, in0=in0, scalar=scalar, in1=in1, op0=op0, op1=op1)
        S.op(eng, fn, reads=reads, writes=writes, adds=adds)

    def recip(out, in_, reads, writes):
        def fn(e):
            return e.reciprocal(out=out, in_=in_)
        S.op("dve", fn, reads=reads, writes=writes)

    def copy(eng, out, in_, reads, writes=(), adds=()):
        if eng == "act":
            def fn(e):
                return e.copy(out=out, in_=in_)
        else:
            def fn(e):
                return e.tensor_copy(out=out, in_=in_)
        S.op(eng, fn, reads=reads, writes=writes, adds=adds)

    def memset(eng, ap, val, writes):
        def fn(e):
            return e.memset(ap, val)
        S.op(eng, fn, writes=writes)


    ident = sb("ident", [128, 128], BF16)
    permM = sb("permM", [128, 128], BF16)
    blk64 = sb("blk64", [128, 128], BF16)
    ones128 = sb("ones128", [128, 128], BF16)
    AB = sb("AB", [128, 2, 2, 8, 5], F32)
    identf = sb("identf", [128, 128], F32)
    for name, t in (("ident", ident), ("permM", permM), ("blk64", blk64), ("ones128", ones128), ("identf", identf)):
        dma("sp", t[:], T[name].ap(), reads=[], writes=[name])
    gsb = sb("gsb", [128, 16], F32)
    cTs = sb("cTs", [128, 48], F32)
    cwb = sb("cwb", [128, 48], F32)
    gains = sb("gains", [128, 16], F32)
    bfr = sb("bfr", [64, 16], F32)
    esk = sb("esk", [128, 16], F32)
    with ExitStack() as pr:
        r_g = sb("r_g", [16, 128], F32, pr)
        r_c = sb("r_c", [40, 128], F32, pr)
        r_w = sb("r_w", [48, 128], F32, pr)
        r_q = sb("r_q", [8, 128], F32, pr)
        r_b = sb("r_b", [8, 64], F32, pr)
        dma("sp", r_g[:], T["norm_g"].ap().rearrange("l (k p) -> (l k) p", p=128), reads=[], writes=["r_g"])
        dma("sp", r_c[:], T["cR"].ap().rearrange("j (k p) -> (j k) p", p=128), reads=[], writes=["r_c"])
        dma("sp", r_w[0:36, :], T["hy_conv_w"].ap().rearrange("l j (c p) -> (l j c) p", p=128), reads=[], adds=["r_w"])
        dma("sp", r_w[36:48, :], T["hy_conv_b"].ap().rearrange("l (c p) -> (l c) p", p=128), reads=[], adds=["r_w"])
        for v, nm in enumerate(("qn_a", "kn_a", "qn_c", "kn_c")):
            for hh in range(2):
                dma("sp", r_q[2 * v:2 * v + 2, 64 * hh:64 * hh + 64], T[nm].ap(), reads=[], adds=["r_q"])
        dma("sp", r_b[0:2, :], T["hy_b1"].ap(), reads=[], adds=["r_b"])
        dma("sp", r_b[2:4, :], T["hy_b2"].ap(), reads=[], adds=["r_b"])
        dma("sp", r_b[4:8, :], T["hy_freq"].ap().rearrange("l j p -> (l j) p"), reads=[], adds=["r_b"])
        dma("sp", esk[:, 0:12], T["sink_c"].ap().rearrange("l h -> (l h)").partition_broadcast(128), reads=[], writes=["esk"])
        for (rt, n, m, dst, nm) in ((r_g, 16, 128, gsb, "gsb"), (r_c, 40, 128, cTs, "cTs"), (r_w, 48, 128, cwb, "cwb"),
                                    (r_q, 8, 128, gains, "gains"), (r_b, 8, 64, bfr, "bfr")):
            pe([(PS[7][0:m, 0:n], rt[0:n, 0:m], identf[0:n, 0:n], True, True)],
               reads=["r_" + nm[0] if nm != "gains" and nm != "bfr" and nm != "cwb" and nm != "cTs" else
                      {"gains": "r_q", "bfr": "r_b", "cwb": "r_w", "cTs": "r_c"}[nm], "identf"], writes=[bank(7)])
            copy("dve", dst[0:m, 0:n], PS[7][0:m, 0:n], reads=[bank(7)], writes=[nm])
        act(esk[:, 0:12], esk[:, 0:12], AF.Exp, reads=["esk"], writes=["esk"])
    S.barrier()

    if "p0" in phases:
        for l in range(nlayers):
            for k in range(8):
                rows = slice(128 * k, 128 * (k + 1))
                for (c0, off) in ((CH_AQ, 0), (CH_CQ, 2048)):
                    for c in range(3):
                        dst = w_in_t[l, c0 + c, :, k, :].rearrange("r (p d) -> r p d", p=2)
                        src = T["w_in"][l, rows, off:off + 384].rearrange("r (p c d) -> r c p d", p=2, c=3)[:, c]
                        dma("pool", dst, src, reads=[], adds=[("w_in_t", l)])
                for (n0, n1) in ((3, 16), (19, 24)):
                    dst = w_in_t[l, n0:n1, :, k, :].rearrange("c r d -> r c d")
                    src = T["w_in"][l, rows, 128 * n0:128 * n1].rearrange("r (c d) -> r c d", d=128)
                    dma("pool", dst, src, reads=[], adds=[("w_in_t", l)])
                dma("pool", w_out_t[l, :, k, :], T["w_out"][l, rows, :], reads=[], adds=[("w_out_t", l)])

        with ExitStack() as p0:
            sc = sb("sc", [128, 48], F32, p0)
            bmod = sb("bmod", [5, 3072], F32, p0)
            mrow = sb("mrow", [5, 3072], F32, p0)
            wm = [sb("wm%d" % i, [128, 3072], F32, p0) for i in range(2)]
            act(sc[:, 0:40], cTs[:, 0:40], AF.Silu, reads=["cTs"], writes=["sc"])
            sc3 = sc[:, 0:40].rearrange("p (j k) -> p j k", k=8)
            wi = 0
            for l in range(nlayers):
                dma("sp", bmod[:], T["b_mod"][l].partition_broadcast(5), reads=[], writes=["bmod"])
                for k in range(8):
                    w = wm[wi % 2]
                    wr = ("wm", wi % 2)
                    wi += 1
                    dma("sp", w[:], T["w_mod"][l, 128 * k:128 * (k + 1), :], reads=[], writes=[wr])
                    pe([(PS[n][0:5, :], sc3[:, :, k], w[:, 512 * n:512 * (n + 1)], k == 0, k == 7) for n in range(6)],
                       reads=["sc", wr], writes=[bank(n) for n in range(6)])
                for n in range(6):
                    tt("dve", mrow[:, 512 * n:512 * (n + 1)], PS[n][0:5, :], bmod[:, 512 * n:512 * (n + 1)], ALU.add,
                       reads=[bank(n), "bmod"], writes=[("mrow", n)])
                dma("sp", modrow[l], mrow[:], reads=[("mrow", n) for n in range(6)], writes=[("modrow", l)])
                ps6 = PS[6][:, 0:80].rearrange("p (s k j) -> p s k j", s=2, k=8)
                pe([(ps6[:, s, k, :], mrow[:, 1024 * s + 128 * k:1024 * s + 128 * (k + 1)], identf[0:5, 0:5], True, True)
                    for s in range(2) for k in range(8)],
                   reads=[("mrow", n) for n in range(4)] + ["identf"], writes=[bank(6)])
                copy("dve", AB[:, l, 1, :, :], ps6[:, 0, :, :], reads=[bank(6)], writes=[("AB", l, 1)])
                for k in range(8):
                    tsc("dve", AB[:, l, 0, k, :], ps6[:, 1, k, :], 1.0, None, ALU.add, None,
                        reads=[bank(6)], writes=[("AB", l, 0, k)])
                    tsc("dve", AB[:, l, 0, k, :], AB[:, l, 0, k, :], gsb[:, 8 * l + k:8 * l + k + 1], None, ALU.mult, None,
                        reads=[("AB", l, 0, k), "gsb"], writes=[("AB", l, 0, k)])

        S.barrier()
        def filter_tables(l, Lx, zT_d, dec_d, Fw_d, dst_tab, tag):
            N = 2 * Lx
            nblk = N // 512
            nmt = N // 128
            half = nmt // 2
            with ExitStack() as pf:
                w1 = sb("hw1" + tag, [33, 64], F32, pf)
                w2 = sb("hw2" + tag, [64, 64], F32, pf)
                w3 = sb("hw3" + tag, [64, 1024], F32, pf)
                zTs = sb("zTs" + tag, [33, N], F32, pf)
                dec = sb("dec" + tag, [128, nmt, 256], F32, pf)
                dbc = sb("dbc" + tag, [128, 512], F32, pf)
                h2T = sb("h2T" + tag, [64, N], F32, pf)
                Gb = sb("Gb" + tag, [128, nmt, 512], BF16, pf)
                rn = sb("rn" + tag, [128, 512], F32, pf)
                tmp = [sb("ft%d" % i + tag, [64, 512], F32, pf) for i in range(8)]
                g32 = [sb("g32%d" % i + tag, [128, 512], F32, pf) for i in range(2)]
                gab = [sb("gab%d" % i + tag, [128, 512], BF16, pf) for i in range(2)]
                fw = [sb("fw%d" % i + tag, [128, nmt, 128], BF16, pf) for i in range(2)]
                tb = [sb("tb%d" % i + tag, [128, 512], F32, pf) for i in range(2)]
                tbb = [sb("tbb%d" % i + tag, [128, 512], BF16, pf) for i in range(2)]
                R = lambda n: (n, tag)
                dma("sp", w1[:], T["hy_w1"][l], reads=[], writes=[R("w1")])
                dma("sp", w2[:], T["hy_w2"][l], reads=[], writes=[R("w2")])
                dma("sp", w3[:], T["hy_w3"][l], reads=[], writes=[R("w3")])
                dma("sp", zTs[:], zT_d.ap(), reads=[], writes=[R("zTs")])
                dma("sp", dec[:], dec_d.ap(), reads=[], writes=[R("dec")])
                dma("sp", dbc[:], T["hy_bias"][l].rearrange("o c -> (o c)").partition_broadcast(128),
                    reads=[], writes=[R("dbc")])
                bfr_r = ["bfr"]

                slc = [0]

                def sin_layer(src_ps, srcbank, bcol, fcol, dst, dst_res, i):
                    o_ = 4 * (slc[0] % 2)
                    slc[0] += 1
                    a, t1, t2, a2 = tmp[o_:o_ + 4]
                    ra = [R("tmp%d" % (o_ + j)) for j in range(4)]
                    tsc("dve", a[:], src_ps, bfr[:, bcol:bcol + 1], bfr[:, fcol:fcol + 1], ALU.add, ALU.mult,
                        reads=[srcbank] + bfr_r, writes=[ra[0]])
                    tsc("dve", t1[:], a[:], float(1.0 / (2 * math.pi)), MAGIC, ALU.mult, ALU.add,
                        reads=[ra[0]], writes=[ra[1]])
                    tsc("dve", t2[:], t1[:], -MAGIC, float(-2 * math.pi), ALU.add, ALU.mult,
                        reads=[ra[1]], writes=[ra[2]])
                    tt("dve", a2[:], t2[:], a[:], ALU.add, reads=[ra[2], ra[0]], writes=[ra[3]])
                    tsc("dve", a2[:], a2[:], 3.1415925, -3.1415925, ALU.min, ALU.max, reads=[ra[3]], writes=[ra[3]])
                    act(dst, a2[:], AF.Sin, reads=[ra[3]], writes=[dst_res])

                h1s = [sb("h1%d" % i + tag, [64, 512], F32, pf) for i in range(2)]
                for blk in range(nblk):
                    cs = slice(512 * blk, 512 * (blk + 1))
                    h1 = h1s[blk % 2]
                    b0, b1 = 2 * (blk % 2), 2 * (blk % 2) + 1
                    pe([(PS[b0][0:64, :], w1[:], zTs[:, cs], True, True)], reads=[R("w1"), R("zTs")], writes=[bank(b0)])
                    sin_layer(PS[b0][0:64, :], bank(b0), l, 4 + 2 * l, h1[:], R(("h1", blk % 2)), blk)
                    pe([(PS[b1][0:64, :], w2[:], h1[:], True, True)], reads=[R("w2"), R(("h1", blk % 2))], writes=[bank(b1)])
                    sin_layer(PS[b1][0:64, :], bank(b1), 2 + l, 5 + 2 * l, h2T[:, cs], R(("h2T", blk)), blk)
                for mt in range(nmt):
                    dr = 0 if mt < half else 1
                    pb = 6 + (mt % 2)
                    pe([(PS[pb][:, 256 * o:256 * (o + 1)], h2T[:, 128 * mt:128 * (mt + 1)],
                         w3[:, 512 * o + 256 * dr:512 * o + 256 * dr + 256], True, True) for o in range(2)],
                       reads=[R(("h2T", mt // 4)), R("w3")], writes=[bank(pb)])
                    g = g32[mt % 2]
                    for o in range(2):
                        tt("dve", g[:, 256 * o:256 * (o + 1)], PS[pb][:, 256 * o:256 * (o + 1)], dec[:, mt, :], ALU.mult,
                           reads=[bank(pb), R("dec")], writes=[R(("g32", mt % 2, o))])
                    gr = [R(("g32", mt % 2, o)) for o in range(2)]
                    copy("act", Gb[:, mt, :], g[:], reads=gr, writes=[R(("Gb", mt))])
                    stt("dve", gab[mt % 2][:], g[:], -1.0, g[:], ALU.mult, ALU.max, reads=gr, writes=[R(("gab", mt % 2))])
                    pe([(PS[4][:], ones128[:], gab[mt % 2][:], mt == 0, mt == nmt - 1)],
                       reads=["ones128", R(("gab", mt % 2))], writes=[bank(4)])
                recip(rn[:], PS[4][:], reads=[bank(4)], writes=[R("rn")])
                for rt in range(nmt):
                    f_ = fw[rt % 2]
                    fr_ = R(("fw", rt % 2))
                    dma("sp", f_[:], Fw_d[rt], reads=[], writes=[fr_])
                    pb = 2 + (rt % 2)
                    pe([(PS[pb][:], f_[:, c, :], Gb[:, c, :], c == 0, c == nmt - 1) for c in range(nmt)],
                       reads=[fr_] + [R(("Gb", c)) for c in range(nmt)], writes=[bank(pb)])
                    t_ = tb[rt % 2]
                    tr_ = R(("tb", rt % 2))
                    tq_ = tbb[rt % 2]
                    tqr_ = R(("tbb", rt % 2))
                    if rt < half:
                        tt("dve", t_[:], PS[pb][:], rn[:], ALU.mult, reads=[bank(pb), R("rn")], writes=[tr_])
                        tt("pool", tq_[:], t_[:], dbc[:], ALU.add, reads=[tr_, R("dbc")], writes=[tqr_])
                    else:
                        tt("dve", tq_[:], PS[pb][:], rn[:], ALU.mult, reads=[bank(pb), R("rn")], writes=[tqr_])
                    ri, ft = (0, rt) if rt < half else (1, rt - half)
                    for o in range(2):
                        dma("pool", dst_tab(o, ri, ft), tq_[:, 256 * o:256 * (o + 1)], reads=[tqr_],
                            adds=[("gtab", tag)])
            S.barrier()

        for l in range(nlayers):
            filter_tables(l, L, T["zT"], T["decT"], T["FwT"],
                          lambda o, ri, ft, l=l: gtab[l, o, ri, ft], "L%d" % l)
        filter_tables(0, LC, T["zTc"], T["decTc"], T["FwTc"],
                      lambda o, ri, ft: gtabc[o, ri, ft], "C")

    cosT = sb("cosT", [128, NT * 128], BF16)
    sinT = sb("sinT", [128, NT * 128], BF16)
    masks = sb("masks", [128, 4, 256], BF16)
    dma("sp", cosT[:], T["cosT"].ap(), reads=[], writes=["cosT"])
    dma("sp", sinT[:], T["sinT"].ap(), reads=[], writes=["sinT"])
    dma("sp", masks[:], T["masks"].ap(), reads=[], writes=["masks"])

    rr = {}

    def nxt(name, n):
        v = rr.get(name, 0)
        rr[name] = v + 1
        return v % n

    def xsrc(l, b, ti):
        if ti < 2:
            t = T["ctx"] if l == 0 else ctxs
            return t[b, 128 * ti:128 * (ti + 1), :], ([] if l == 0 else [("xs", b, ti)])
        t = T["x"] if l == 0 else xs
        return t[b, 128 * (ti - 2):128 * (ti - 1), :], ([] if l == 0 else [("xs", b, ti)])

    def phase1(l, b, P):
        ctxq = (l == 0)
        hT, vaug, gX, PT = P["hT"], P["vaug"], P["gX"], P["PT"]
        xt, xn, junk, st4 = P["xt"], P["xn"], P["junk"], P["st4"]
        wfm, wtm, wbg = P["wfm"], P["wtm"], P["wbg"]
        ubuf, dg = P["ubuf"], P["dg"]
        qg, sq, t1, t2, rs, rr = P["qg"], P["sq"], P["t1"], P["t2"], P["rs"], P["rr"]
        ostage, hstage, bstage, den = P["ostage"], P["hstage"], P["bstage"], P["den"]

        def h_items(bh, hbanks):
            itemsA, itemsB = [], []
            for ti in range(NT):
                st = {}

                def partA(ti=ti, st=st):
                    s_ = nxt("xt", 4)
                    st["s"] = s_
                    src, sres = xsrc(l, bh, ti)
                    dma("sp", xt[s_][:], src, reads=sres, writes=[("xt", s_)])
                    act(junk[:], xt[s_][:], AF.Square, reads=[("xt", s_)], writes=["junk", ("ss", s_)],
                        accum=st4[:, 4 * s_:4 * s_ + 1])
                    act(st4[:, 4 * s_ + 1:4 * s_ + 2], st4[:, 4 * s_:4 * s_ + 1], AF.Sqrt, scale=1.0 / D, bias=EPS,
                        reads=[("ss", s_)], writes=[("ss2", s_)])
                    recip(st4[:, 4 * s_ + 2:4 * s_ + 3], st4[:, 4 * s_ + 1:4 * s_ + 2], reads=[("ss2", s_)], writes=[("rstd", s_)])
                    tsc("dve", xn[s_][:], xt[s_][:], st4[:, 4 * s_ + 2:4 * s_ + 3], None, ALU.mult, None,
                        reads=[("xt", s_), ("rstd", s_)], writes=[("xn", s_)])

                def partB(ti=ti, st=st):
                    s_ = st["s"]
                    j = 4 if ti < 2 else bh
                    pb = hbanks[nxt("hb", 1000) % len(hbanks)]
                    pe_tr([(PSB[pb][:, 128 * k:128 * (k + 1)], xn[s_][:, 128 * k:128 * (k + 1)]) for k in range(8)], ident[:],
                          reads=[("xn", s_), "ident"], writes=[bank(pb)])
                    for k in range(8):
                        tsc("dve", hT[:, k, 128 * ti:128 * (ti + 1)], PSB[pb][:, 128 * k:128 * (k + 1)],
                            AB[:, l, 0, k, j:j + 1], AB[:, l, 1, k, j:j + 1], ALU.mult, ALU.add,
                            reads=[bank(pb), ("AB", l, 0, k), ("AB", l, 1)], adds=[("hT", ti)])
                itemsA.append(partA)
                itemsB.append(partB)
            out = []
            for i in range(NT + 2):
                if i < NT:
                    out.append(itemsA[i])
                if i >= 2:
                    out.append(itemsB[i - 2])
            return out

        if not P.get("h_done", False):
            for it in h_items(b, [0, 1]):
                it()
        P["h_done"] = False

        blocks_l = [(256 + 512 * i, 512) for i in range(4)]
        blocks_c = [(0, 256)]

        def blk_tiles(c0, n):
            return [("hT", t) for t in range(c0 // 128, (c0 + n) // 128)]

        BK = {"fm": [2, 3], "nr": [(4, 5), (6, 7)]}

        def fm_items(chunk, blocks, post):
            st = {}
            items = []
            for bi, (c0, n) in enumerate(blocks):
                def item(bi=bi, c0=c0, n=n):
                    if bi == 0:
                        st["s"] = nxt("wfm", 2)
                        dma("sp", wfm[st["s"]][:], w_in_t[l, chunk], reads=[("w_in_t", l)], writes=[("wfm", st["s"])])
                    s_ = st["s"]
                    fb = BK["fm"]
                    pb = fb[nxt("fmb", 1000) % len(fb)]
                    pe([(PS[pb][:, 0:n], wfm[s_][:, k, :], hT[:, k, c0:c0 + n], k == 0, k == 7) for k in range(8)],
                       reads=[("wfm", s_)] + blk_tiles(c0, n), writes=[bank(pb)])
                    post(pb, c0, n)
                items.append(item)
            return items

        def fm_chunk(chunk, blocks, post):
            for it in fm_items(chunk, blocks, post):
                it()

        def qk_post(gcol, dst_fn):
            def post(pb, c0, n):
                s_ = nxt("nr", 2)
                pp, pq = BK["nr"][nxt("nrb", 1000) % len(BK["nr"])]
                act(qg[s_][:, 0:n], PS[pb][:, 0:n], AF.Identity, scale=gains[:, gcol:gcol + 1],
                    reads=[bank(pb), "gains"], writes=[("qg", s_)])
                act(sq[s_][:, 0:n], PS[pb][:, 0:n], AF.Square, reads=[bank(pb)], writes=[("sq", s_)])
                pe([(PS[pp][:, 0:n], permM[:], qg[s_][:, 0:n], True, True)], reads=["permM", ("qg", s_)], writes=[bank(pp)])
                pe([(PS[pq][:, 0:n], blk64[:], sq[s_][:, 0:n], True, True)], reads=["blk64", ("sq", s_)], writes=[bank(pq)])
                tt("pool", t1[s_][:, 0:n], qg[s_][:, 0:n], cosT[:, c0:c0 + n], ALU.mult,
                   reads=[("qg", s_), "cosT"], writes=[("t1", s_)])
                tt("dve", t2[s_][:, 0:n], PS[pp][:, 0:n], sinT[:, c0:c0 + n], ALU.mult,
                   reads=[bank(pp), "sinT"], writes=[("t2", s_)])
                tt("pool", t1[s_][:, 0:n], t1[s_][:, 0:n], t2[s_][:, 0:n], ALU.add,
                   reads=[("t1", s_), ("t2", s_)], writes=[("t1", s_)])
                if USE_ARS:
                    act(rr[s_][:, 0:n], PS[pq][:, 0:n], AF.Abs_reciprocal_sqrt, scale=1.0 / 64, bias=EPS,
                        reads=[bank(pq)], writes=[("rr", s_)])
                else:
                    act(rs[s_][:, 0:n], PS[pq][:, 0:n], AF.Ln, scale=1.0 / 64, bias=EPS, reads=[bank(pq)], writes=[("rs", s_)])
                    act(rr[s_][:, 0:n], rs[s_][:, 0:n], AF.Exp, scale=-0.5, reads=[("rs", s_)], writes=[("rr", s_)])
                dst, dres = dst_fn(c0, n)
                tt("dve", dst, t1[s_][:, 0:n], rr[s_][:, 0:n], ALU.mult, reads=[("t1", s_), ("rr", s_)], writes=[dres])
            return post

        def tm_group(wt, wres, chunks, ncols, tiles, post):
            for ci, ch in enumerate(chunks):
                dma("sp", wt[:, :, 128 * ci:128 * (ci + 1)], w_in_t[l, ch], reads=[("w_in_t", l)], adds=[wres])
            for ti in tiles:
                pb = nxt("tmb", 2)
                pe([(PS[pb][:, 0:ncols], hT[:, k, 128 * ti:128 * (ti + 1)], wt[:, k, 0:ncols], k == 0, k == 7) for k in range(8)],
                   reads=[wres, ("hT", ti)], writes=[bank(pb)])
                post(pb, ti)

        def attention(X, xoff, qX, kX, inter):
            esoff = 6 * l
            inter = list(inter)
            qblocks = ([(0, None)] if ctxq else []) + [(2 + 2 * i, 2 * i) for i in range(8)]
            units = []
            for (qt0, lt0) in qblocks:
                for h in range(6):
                    units.append((qt0, lt0, h))
            state = {}

            def keylist(lt0):
                if lt0 is None:
                    return [(0, None, True, True), (1, None, True, True)]
                if X == "A":
                    return [(t, None, True, True) for t in range(NT)]
                KL = [(0, None, True, True), (1, None, True, True)]
                for mi, jt in enumerate((lt0 - 1, lt0, lt0 + 1, lt0 + 2)):
                    if 0 <= jt < 16:
                        KL.append((2 + jt, mi, mi != 3, mi != 0))
                return KL

            def emit_qk(ui, lo, hi):
                qt0, lt0, h = units[ui]
                c, p = h % 3, h // 3
                rows = slice(64 * p, 64 * p + 64)
                if ui not in state:
                    state[ui] = (keylist(lt0), nxt("PT", 2))
                KL, ps_ = state[ui]
                npairs = (len(KL) + 1) // 2
                for pi in range(lo, min(hi, npairs)):
                    idxs = [i for i in (2 * pi, 2 * pi + 1) if i < len(KL)]
                    spb = nxt("sreg", 5)
                    pe([(PS[spb][:, 256 * j:256 * (j + 1)], kX[rows, 128 * KL[i][0]:128 * (KL[i][0] + 1)],
                         qX[rows, c, 128 * qt0:128 * qt0 + 256], True, True) for j, i in enumerate(idxs)],
                       reads=[("kX", KL[i][0]) for i in idxs] + [("qX", c, qt0 // 2)], writes=[bank(spb)])
                    n2 = len(idxs)
                    act(PT[ps_][:, idxs[0]:idxs[0] + n2, :].rearrange("p a b -> p (a b)"), PS[spb][:, 0:256 * n2], AF.Exp,
                        scale=0.125, reads=[bank(spb)], writes=[("PT", ps_, i) for i in idxs])
                    for i in idxs:
                        mk = KL[i][1]
                        if mk is not None:
                            pat = masks[:, 0 if mk < 2 else 2, 0:128]
                            co = 128 * (mk % 2)
                            tt("pool", PT[ps_][:, i, co:co + 128], PT[ps_][:, i, co:co + 128], pat, ALU.mult,
                               reads=[("PT", ps_, i), "masks"], writes=[("PT", ps_, i)])
                return npairs

            def emit_pv(ui):
                qt0, lt0, h = units[ui]
                c, p = h % 3, h // 3
                KL, ps_ = state.pop(ui)
                if h == 0:
                    state["os"] = nxt("ostage", 2)
                os_ = state["os"]
                ob = 5 + nxt("ob", 2)
                O3 = PS[ob][:, 0:130].rearrange("p (u e) -> p u e", e=65)
                for u2 in range(2):
                    lst = [idx for idx, kl in enumerate(KL) if kl[2 + u2]]
                    pe([(O3[:, u2, :], PT[ps_][:, idx, 128 * u2:128 * (u2 + 1)], vaug[:, KL[idx][0], p, :],
                         i == 0, i == len(lst) - 1) for i, idx in enumerate(lst)],
                       reads=[("PT", ps_, idx) for idx in lst] + [("vaug", KL[idx][0]) for idx in lst], writes=[bank(ob)])
                ds = nxt("den", 4)
                dn = den[:, 2 * ds:2 * ds + 2]
                if X == "C":
                    tsc("dve", dn, O3[:, :, 64], esk[:, esoff + h:esoff + h + 1], None, ALU.add, None,
                        reads=[bank(ob), "esk"], writes=[("den", ds)])
                    recip(dn, dn, reads=[("den", ds)], writes=[("den", ds)])
                else:
                    recip(dn, O3[:, :, 64], reads=[bank(ob)], writes=[("den", ds)])
                for u2 in range(2):
                    stt("dve", ostage[os_][:, u2, 64 * h:64 * h + 64], O3[:, u2, 0:64], den[:, 2 * ds + u2:2 * ds + u2 + 1],
                        gX[:, qt0 + u2, 64 * h:64 * h + 64], ALU.mult, ALU.mult,
                        reads=[bank(ob), ("den", ds), ("gX", qt0 + u2)], adds=[("ostage", os_)])
                if h == 5:
                    dma("pool", mix[b, 128 * qt0:128 * (qt0 + 2), xoff:xoff + 384].rearrange("(u p) n -> p u n", p=128),
                        ostage[os_][:], reads=[("ostage", os_)], adds=[("mix", b, qt0), ("mix", b, qt0 + 1)])

            NPRE = 4
            every = max(1, len(units) // max(1, len(inter))) if inter else 0
            emit_qk(0, 0, 99)
            for ui in range(len(units)):
                if ui + 1 < len(units):
                    emit_qk(ui + 1, 0, NPRE)
                emit_pv(ui)
                if ui + 1 < len(units):
                    emit_qk(ui + 1, NPRE, 99)
                if inter and (ui % every == every - 1):
                    inter.pop(0)()
            while inter:
                inter.pop(0)()

        all_tiles = list(range(NT))
        q_blocks = (blocks_c if ctxq else []) + blocks_l
        k_blocks = blocks_c + blocks_l
        def qk_items(X, qb, kb, chq, chk, gq, gk):
            items = []
            for c in range(3):
                def qpost_wrap(c=c):
                    base = qk_post(gq, lambda c0, n: (qb[:, c, c0:c0 + n], ("qX", c, c0 // 256)))

                    def post(pb, c0, n):
                        base(pb, c0, n)
                        if n == 512:
                            S.res[("qX", c, c0 // 256 + 1)] = S.res[("qX", c, c0 // 256)]
                    return post
                items += fm_items(chq + c, q_blocks, qpost_wrap())
            kbase = qk_post(gk, lambda c0, n: (kb[:, c0:c0 + n], ("kXblk", c0)))

            def kpost(pb, c0, n):
                kbase(pb, c0, n)
                for t in range(c0 // 128, (c0 + n) // 128):
                    S.res[("kX", t)] = S.res[("kXblk", c0)]
            items += fm_items(chk, k_blocks, kpost)
            return items

        def vg_post(pb, ti):
            copy("dve", vaug[:, ti, :, 0:64], PS[pb][:, 0:128].rearrange("p (h d) -> p h d", h=2),
                 reads=[bank(pb)], writes=[("vaug", ti)])
            if ti >= 2 or ctxq:
                act(gX[:, ti, :], PS[pb][:, 128:512], AF.Silu, reads=[bank(pb)], writes=[("gX", ti)])

        qA, kA, qC, kC = P["qX"], P["kX"], P["qX"], P["kX"]
        for it in qk_items("A", qA, kA, CH_AQ, CH_AK, 0 + l, 2 + l):
            it()
        tm_group(wtm, "wtm", [CH_AV, CH_AV + 1, CH_AV + 2, CH_AV + 3], 512, all_tiles, vg_post)
        attention("A", 0, qA, kA, [])
        for it in qk_items("C", qC, kC, CH_CQ, CH_CK, 4 + l, 6 + l):
            it()
        tm_group(wtm, "wtm", [CH_CV, CH_CV + 1, CH_CV + 2, CH_CV + 3], 512, all_tiles, vg_post)

        def finish_attention_c():
            if b + 1 < P["nb"]:
                inter = h_items(b + 1, [7])
                P["h_done"] = True
            else:
                inter = []
            attention("C", 640, qC, kC, inter)

        for tix, (tname, ch0) in enumerate((("v", CH_BV), ("x1", CH_BX1), ("x2", CH_BX2))):
            for jj in range(2):
                chn = ch0 + jj
                ccol = 2 * tix + jj

                pbs = nxt("pbuf", 2)
                pbuf = P["pbuf"][pbs]
                pres = ("pbuf", pbs)

                def hpost(pb, c0, n, pbuf=pbuf, pres=pres):
                    off = c0 + 1 if c0 < 256 else c0 + 3
                    copy("act", pbuf[:, off:off + n], PS[pb][:, 0:n], reads=[bank(pb)], adds=[pres])
                fm_chunk(chn, q_blocks, hpost)
                for (c0, n) in q_blocks:
                    off = c0 + 1 if c0 < 256 else c0 + 3
                    pb = 2 + nxt("fmb", 2)
                    pe([(PS[pb][:, 0:n], dg[:, 6 * j + ccol, :], pbuf[:, off - 1 + j:off - 1 + j + n], j == 0, j == 2)
                        for j in range(3)], reads=[pres, "dg"], writes=[bank(pb)])
                    act(ubuf[:, jj, c0:c0 + n], PS[pb][:, 0:n], AF.Identity,
                        bias=cwb[:, 36 + 6 * l + ccol:36 + 6 * l + ccol + 1],
                        reads=[bank(pb), "cwb"], adds=[("ubuf", jj, 0 if c0 < 256 else 256)])
            groups = ([("c", 0, 2)] if ctxq else []) + [("l", 4 * g, 4) for g in range(4)]
            for (kind, t0, nt_) in groups:
                pb = nxt("hb", 2)
                base = 0 if kind == "c" else 256
                pe_tr([(PSB[pb][:, 256 * q + 128 * jj:256 * q + 128 * (jj + 1)],
                        ubuf[:, jj, base + 128 * (t0 + q):base + 128 * (t0 + q + 1)]) for q in range(nt_) for jj in range(2)],
                      ident[:], reads=[("ubuf", 0, base), ("ubuf", 1, base), "ident"], writes=[bank(pb)])
                hs = nxt("hstage", 2)
                copy("act", hstage[hs][:, 0:256 * nt_], PSB[pb][:, 0:256 * nt_], reads=[bank(pb)], writes=[("hstage", hs)])
                dstt = (hyc if kind == "c" else hyl)[tname]
                dma("act", dstt[t0:t0 + nt_, :, b, :].rearrange("c p n -> p c n"),
                    hstage[hs][:, 0:256 * nt_].rearrange("p (c n) -> p c n", n=256),
                    reads=[("hstage", hs)], adds=[("hy", kind, tname)])

        def bg_post(pb, ti):
            bs = nxt("bstage", 2)
            act(bstage[bs][:], PS[pb][:, 0:256], AF.Silu, reads=[bank(pb)], writes=[("bstage", bs)])
            if ti < 2:
                d_, kind = hyc["bg"][ti, :, b, :], "c"
            else:
                d_, kind = hyl["bg"][ti - 2, :, b, :], "l"
            dma("act", d_, bstage[bs][:], reads=[("bstage", bs)], adds=[("hy", kind, "bg")])
        tm_group(wbg, "wbg", [CH_BG, CH_BG + 1], 256, (all_tiles if ctxq else all_tiles[2:]), bg_post)
        finish_attention_c()

    def phase2(l, kind, P):
        nft = 8 if kind == "l" else 1
        hy = hyl if kind == "l" else hyc
        Fd = T["FwEO"] if kind == "l" else T["FwEOc"]
        Fi = T["FinvT"] if kind == "l" else T["FinvTc"]
        tb = 2 if kind == "l" else 0
        U, ZZ, Gr, Gi = P["U"], P["ZZ"], P["Gr"], P["Gi"]
        fw, fin, xo, bgt, ys, yt = P["fw"], P["fin"], P["xo"], P["bgt"], P["ys"], P["yt"]
        osb, XX, mm, zz = P["osb"], P["XX"], P["mm"], P["zz"]
        tagk = ("L%d" % l) if kind == "l" else "C"

        def tile_src(name, cp, par):
            out = []
            for hlf in range(2):
                src = hy[name][2 * cp + hlf].rearrange("(p two) b n -> p two (b n)", two=2)[:, par, :]
                out.append((slice(64 * hlf, 64 * hlf + 64), src))
            return out

        for cp in range(nft):
            for par in range(2):
                for (psl, src) in tile_src("v", cp, par):
                    dma("sp", U[par][psl, cp, :], src, reads=[("hy", kind, "v")], adds=[("U", par, cp)])
        for o in range(2):
            tab = (lambda ri: gtab[l, o, ri]) if kind == "l" else (lambda ri: gtabc[o, ri])
            dma("sp", Gr[:, 0:2 * nft, :], tab(0).rearrange("f p n -> p f n"), reads=[("gtab", tagk)], writes=["Gr"])
            dma("sp", Gi[:, 0:2 * nft, :], tab(1).rearrange("f p n -> p f n"), reads=[("gtab", tagk)], writes=["Gi"])
            for ft in range(nft):
                fs = nxt("fw", 2)
                dma("sp", fw[fs][:, :, 0:nft, :], Fd[ft].rearrange("a p k n -> p a k n"), reads=[], writes=[("fw", fs)])
                for hh in range(2):
                    cs = slice(512 * hh, 512 * (hh + 1))
                    b0 = 4 * nxt("fwb", 2)
                    pe([(PS[b0 + a][:], fw[fs][:, a, kc, :], U[a // 2][:, kc, cs], kc == 0, kc == nft - 1)
                        for a in range(4) for kc in range(nft)],
                       reads=[("fw", fs)] + [("U", par, kc) for par in range(2) for kc in range(nft)],
                       writes=[bank(b0 + a) for a in range(4)])
                    q = nxt("osb", 2)
                    copy("act", osb[q][0][:], PS[b0 + 2][:], reads=[bank(b0 + 2)], writes=[("osr", q)])
                    copy("act", osb[q][1][:], PS[b0 + 3][:], reads=[bank(b0 + 3)], writes=[("osi", q)])
                    xq = nxt("XX", 2)
                    xr, xi, xr2, xi2 = XX[xq]
                    tt("dve", xr[:], PS[b0][:], osb[q][0][:], ALU.add, reads=[bank(b0), ("osr", q)], writes=[("xr", xq)])
                    tt("dve", xr2[:], PS[b0][:], osb[q][0][:], ALU.subtract, reads=[bank(b0), ("osr", q)], writes=[("xr2", xq)])
                    tt("dve", xi[:], PS[b0 + 1][:], osb[q][1][:], ALU.add, reads=[bank(b0 + 1), ("osi", q)], writes=[("xi", xq)])
                    tt("dve", xi2[:], osb[q][1][:], PS[b0 + 1][:], ALU.subtract, reads=[bank(b0 + 1), ("osi", q)], writes=[("xi2", xq)])
                    for bb in range(2):
                        c2 = slice(256 * bb, 256 * (bb + 1))
                        zc = slice(512 * hh + 256 * bb, 512 * hh + 256 * (bb + 1))
                        zq = nxt("zz", 2)
                        zr, zi, zr2, zi2 = zz[zq]

                        def prod(dst, dres, xa, xares, ga, gares, xb, xbres, gb, gbres, op, gt):
                            mq = nxt("mm", 4)
                            ma, mb = mm[mq]
                            tt("dve", ma[:], xa[:, c2], ga[:, gt, :], ALU.mult, reads=[(xares, xq), gares], writes=[("ma", mq)])
                            tt("dve", mb[:], xb[:, c2], gb[:, gt, :], ALU.mult, reads=[(xbres, xq), gbres], writes=[("mb", mq)])
                            tt("pool", dst[:], ma[:], mb[:], op, reads=[("ma", mq), ("mb", mq)], writes=[(dres, zq)])
                        prod(zr, "zr", xr, "xr", Gr, "Gr", xi, "xi", Gi, "Gi", ALU.subtract, ft)
                        prod(zi, "zi", xr, "xr", Gi, "Gi", xi, "xi", Gr, "Gr", ALU.add, ft)
                        prod(zr2, "zr2", xr2, "xr2", Gr, "Gr", xi2, "xi2", Gi, "Gi", ALU.subtract, nft + ft)
                        prod(zi2, "zi2", xr2, "xr2", Gi, "Gi", xi2, "xi2", Gr, "Gr", ALU.add, nft + ft)
                        tt("pool", ZZ[0][:, ft, zc], zr[:], zr2[:], ALU.add, reads=[("zr", zq), ("zr2", zq)], adds=[("Z", 0, ft)])
                        tt("pool", ZZ[1][:, ft, zc], zr[:], zr2[:], ALU.subtract, reads=[("zr", zq), ("zr2", zq)], adds=[("Z", 1, ft)])
                        tt("pool", ZZ[0][:, nft + ft, zc], zi[:], zi2[:], ALU.subtract, reads=[("zi", zq), ("zi2", zq)],
                           adds=[("Z", 0, nft + ft)])
                        tt("pool", ZZ[1][:, nft + ft, zc], zi[:], zi2[:], ALU.add, reads=[("zi", zq), ("zi2", zq)],
                           adds=[("Z", 1, nft + ft)])
            xname = "x1" if o == 0 else "x2"
            for cp in range(nft):
                for par in range(2):
                    is_ = nxt("fin", 2)
                    dma("sp", fin[is_][:, 0:2 * nft, :], Fi[cp, par], reads=[], writes=[("fin", is_)])
                    xs_ = nxt("xo", 2)
                    for (psl, src) in tile_src(xname, cp, par):
                        dma("sp", xo[xs_][psl, :], src, reads=[("hy", kind, xname)], adds=[("xo", xs_)])
                    if o == 1:
                        for (psl, src) in tile_src("bg", cp, par):
                            dma("sp", bgt[xs_][psl, :], src, reads=[("hy", kind, "bg")], adds=[("bgt", xs_)])
                    pr0 = 2 * nxt("ivb", 4)
                    pe([(PS[pr0 + hh][:], fin[is_][:, kc, :], ZZ[par][:, kc, 512 * hh:512 * (hh + 1)], kc == 0, kc == 2 * nft - 1)
                        for kc in range(2 * nft) for hh in range(2)],
                       reads=[("fin", is_)] + [("Z", par, kc) for kc in range(2 * nft)], writes=[bank(pr0), bank(pr0 + 1)])
                    for hh in range(2):
                        cs = slice(512 * hh, 512 * (hh + 1))
                        if o == 0:
                            tt("dve", U[par][:, cp, cs], PS[pr0 + hh][:], xo[xs_][:, cs], ALU.mult,
                               reads=[bank(pr0 + hh), ("xo", xs_)], adds=[("U", par, cp)])
                        else:
                            ysl = nxt("ys", 2)
                            tt("dve", yt[ysl][:], PS[pr0 + hh][:], xo[xs_][:, cs], ALU.mult,
                               reads=[bank(pr0 + hh), ("xo", xs_)], writes=[("yt", ysl)])
                            tt("pool", ys[xs_][:, cs], yt[ysl][:], bgt[xs_][:, cs], ALU.mult,
                               reads=[("yt", ysl), ("bgt", xs_)], adds=[("ys", xs_)])
                    if o == 1:
                        r0 = 128 * tb + 256 * cp
                        dma("pool", mix[:, r0:r0 + 256, 384:640].rearrange("b (p two) n -> p two b n", two=2)[:, par],
                            ys[xs_][:].rearrange("p (b n) -> p b n", n=256), reads=[("ys", xs_)],
                            adds=[("mix", bb, tb + 2 * cp + hf) for bb in range(NB) for hf in range(2)])

    def phase3(l, b, P):
        wout, gate, mixt, mixT, xt, tmp, xo = P["wout"], P["gate"], P["mixt"], P["mixT"], P["xt"], P["tmp"], P["xo"]
        last = (l == nlayers - 1)
        tiles = list(range(NT)) if l == 0 else list(range(2, NT))
        dma("sp", gate[0][:], modrow[l, b, 2048:3072].partition_broadcast(128), reads=[("modrow", l)], writes=[("gate", 0)])
        if l == 0:
            dma("sp", gate[1][:], modrow[l, 4, 2048:3072].partition_broadcast(128), reads=[("modrow", l)], writes=[("gate", 1)])
        for ti in tiles:
            ms = nxt("mixt", 4)
            dma("sp", mixt[ms][:], mix[b, 128 * ti:128 * (ti + 1), :], reads=[("mix", b, ti)], writes=[("mixt", ms)])
            pb = nxt("hb", 2)
            pe_tr([(PSB[pb][:, 128 * k:128 * (k + 1)], mixt[ms][:, 128 * k:128 * (k + 1)]) for k in range(8)], ident[:],
                  reads=[("mixt", ms), "ident"], writes=[bank(pb)])
            copy("act", mixT[ms][:], PSB[pb][:], reads=[bank(pb)], writes=[("mixT", ms)])
            ob = 2 + 2 * nxt("opb", 3)
            pe([(PS[ob + hh][:], mixT[ms][:, 128 * k:128 * (k + 1)], wout[:, k, 512 * hh:512 * (hh + 1)], k == 0, k == 7)
                for k in range(8) for hh in range(2)], reads=[("mixT", ms), "wout"], writes=[bank(ob), bank(ob + 1)])
            s_ = nxt("xt3", 4)
            src, sres = xsrc(l, b, ti)
            dma("sp", xt[s_][:], src, reads=sres, writes=[("xt3", s_)])
            gsel = 1 if ti < 2 else 0
            for hh in range(2):
                cs = slice(512 * hh, 512 * (hh + 1))
                tt("dve", tmp[s_][:, cs], PS[ob + hh][:], gate[gsel][:, cs], ALU.mult,
                   reads=[bank(ob + hh), ("gate", gsel)], adds=[("tmp3", s_)])
            tt("dve", xo[s_][:], xt[s_][:], tmp[s_][:], ALU.add, reads=[("xt3", s_), ("tmp3", s_)], writes=[("xo3", s_)])
            if last:
                dst = y[b, 128 * (ti - 2):128 * (ti - 1), :]
                dres = ("y", b, ti)
            elif ti < 2:
                dst = ctxs[b, 128 * ti:128 * (ti + 1), :]
                dres = ("xs", b, ti)
            else:
                dst = xs[b, 128 * (ti - 2):128 * (ti - 1), :]
                dres = ("xs", b, ti)
            dma("pool", dst, xo[s_][:], reads=[("xo3", s_)], writes=[dres])

    for l in range(nlayers):
        S.epoch = l + 1
        if "p1" in phases:
            with ExitStack() as p1:
                P = {}
                P["hT"] = sb("hT", [128, 8, NT * 128], BF16, p1)
                P["qX"] = sb("qX", [128, 3, NT * 128], BF16, p1)
                P["kX"] = sb("kX", [128, NT * 128], BF16, p1)
                P["vaug"] = sb("vaug", [128, NT, 2, 65], BF16, p1)
                P["gX"] = sb("gX", [128, NT, 384], BF16, p1)
                P["PT"] = [sb("PT%d" % i, [128, NT, 256], BF16, p1) for i in range(2)]
                P["xt"] = [sb("xt%d" % i, [128, D], F32, p1) for i in range(4)]
                P["xn"] = [sb("xn%d" % i, [128, D], BF16, p1) for i in range(4)]
                P["junk"] = sb("junk", [128, D], BF16, p1)
                P["st4"] = sb("st4", [128, 16], F32, p1)
                P["wfm"] = [sb("wfm%d" % i, [128, 8, 128], BF16, p1) for i in range(2)]
                P["wtm"] = sb("wtm", [128, 8, 512], BF16, p1)
                P["wbg"] = sb("wbg", [128, 8, 256], BF16, p1)
                P["pbuf"] = [sb("pbuf%d" % i, [128, 2308], BF16, p1) for i in range(2)]
                P["dg"] = sb("dg", [128, 18, 128], BF16, p1)
                for jc in range(18):
                    tsc("dve", P["dg"][:, jc, :], ident[:], cwb[:, 18 * l + jc:18 * l + jc + 1], None, ALU.mult, None,
                        reads=["ident", "cwb"], adds=["dg"])
                P["ubuf"] = sb("ubuf", [128, 2, NT * 128], BF16, p1)
                for nm, dt_ in (("qg", BF16), ("sq", BF16), ("t1", BF16), ("t2", BF16), ("rs", F32), ("rr", F32)):
                    P[nm] = [sb(nm + "%d" % i, [128, 512], dt_, p1) for i in range(2)]
                P["ostage"] = [sb("ostage%d" % i, [128, 2, 384], BF16, p1) for i in range(2)]
                P["hstage"] = [sb("hstage%d" % i, [128, 1024], BF16, p1) for i in range(2)]
                P["bstage"] = [sb("bstage%d" % i, [128, 256], BF16, p1) for i in range(2)]
                P["den"] = sb("den", [128, 8], F32, p1)
                for i in range(2):
                    memset("pool", P["pbuf"][i][:], 0.0, writes=[("pbuf", i)])
                memset("pool", P["vaug"][:], 1.0, writes=[("vaug", t) for t in range(NT)])
                P["nb"] = nb
                P["h_done"] = False
                for b in range(nb):
                    phase1(l, b, P)
            S.barrier()
        if "p2" in phases:
            with ExitStack() as p2:
                P = {}
                P["U"] = [sb("U%d" % i, [128, 8, 1024], BF16, p2) for i in range(2)]
                P["ZZ"] = [sb("ZZ%d" % i, [128, 16, 1024], BF16, p2) for i in range(2)]
                P["Gr"] = sb("Gr", [128, 16, 256], BF16, p2)
                P["Gi"] = sb("Gi", [128, 16, 256], BF16, p2)
                P["fw"] = [sb("fw%d" % i, [128, 4, 8, 128], BF16, p2) for i in range(2)]
                P["fin"] = [sb("fin%d" % i, [128, 16, 128], BF16, p2) for i in range(2)]
                P["xo"] = [sb("xo%d" % i, [128, 1024], BF16, p2) for i in range(2)]
                P["bgt"] = [sb("bgt%d" % i, [128, 1024], BF16, p2) for i in range(2)]
                P["yt"] = [sb("yt%d" % i, [128, 512], F32, p2) for i in range(2)]
                P["ys"] = [sb("ys%d" % i, [128, 1024], BF16, p2) for i in range(2)]
                P["osb"] = [[sb("os%d%d" % (i, j), [128, 512], F32, p2) for j in range(2)] for i in range(2)]
                P["XX"] = [[sb("xx%d%d" % (i, j), [128, 512], BF16, p2) for j in range(4)] for i in range(2)]
                P["mm"] = [[sb("mm%d%d" % (i, j), [128, 256], F32, p2) for j in range(2)] for i in range(4)]
                P["zz"] = [[sb("zz%d%d" % (i, j), [128, 256], F32, p2) for j in range(4)] for i in range(2)]
                phase2(l, "l", P)
                if l == 0 and nlayers > 1:
                    phase2(l, "c", P)
            S.barrier()
        if "p3" in phases:
            with ExitStack() as p3:
                P = {}
                P["wout"] = sb("wout", [128, 8, D], BF16, p3)
                P["gate"] = [sb("gate%d" % i, [128, D], F32, p3) for i in range(2)]
                P["mixt"] = [sb("mixt%d" % i, [128, D], BF16, p3) for i in range(4)]
                P["mixT"] = [sb("mixT%d" % i, [128, D], BF16, p3) for i in range(4)]
                P["xt"] = [sb("x3t%d" % i, [128, D], F32, p3) for i in range(4)]
                P["tmp"] = [sb("x3m%d" % i, [128, D], F32, p3) for i in range(4)]
                P["xo"] = [sb("x3o%d" % i, [128, D], F32, p3) for i in range(4)]
                dma("sp", P["wout"][:], w_out_t[l], reads=[("w_out_t", l)], writes=["wout"])
                for b in range(nb):
                    phase3(l, b, P)
            S.barrier()

    if "modrow" in dbg_out:
        dma("sp", dbg_out["modrow"].ap(), modrow.ap(), reads=[("modrow", 0), ("modrow", 1)], writes=["dbg_modrow"])
    if "gtab" in dbg_out:
        dma("sp", dbg_out["gtab"].ap(), gtab.ap(), reads=[("gtab", "L0"), ("gtab", "L1")], writes=["dbg_gtab"])
    if "gtabc" in dbg_out:
        dma("sp", dbg_out["gtabc"].ap(), gtabc.ap(), reads=[("gtab", "C")], writes=["dbg_gtabc"])
    if "AB" in dbg_out:
        dma("sp", dbg_out["AB"].ap(), AB[:],
            reads=[("AB", l, 1) for l in range(2)] + [("AB", l, 0, k) for l in range(2) for k in range(8)],
            writes=["dbg_AB"])

    S.barrier()
    for name, src in (("mix", mix), ("xs", xs), ("ctxs", ctxs), ("hy_v", hyl["v"]), ("hy_x1", hyl["x1"]), ("hy_x2", hyl["x2"]),
                      ("hy_bg", hyl["bg"]), ("hyc_v", hyc["v"]), ("w_in_t", w_in_t)):
        if name in dbg_out:
            dma("sp", dbg_out[name].ap(), src.ap(), reads=[], writes=["dbg_" + name])
    S.finish()
    S.make_sems(es)
    with nc.Block() as block:
        @block.tensor
        def _(e):
            S.replay("pe", e)

        @block.scalar
        def _(e):
            S.replay("act", e)

        @block.vector
        def _(e):
            S.replay("dve", e)

        @block.gpsimd
        def _(e):
            S.replay("pool", e)

        @block.sync
        def _(e):
            S.replay("sp", e)
    es.close()
    return nc


def make_in_maps(inputs):
    cst = _consts()
    maps = []
    for i in range(NCORES):
        m = {}
        b0 = i * NB
        m["x"] = np.ascontiguousarray(inputs["x"][b0:b0 + NB])
        m["ctx"] = np.ascontiguousarray(inputs["ctx"][b0:b0 + NB])
        m["cR"] = np.ascontiguousarray(
            np.concatenate([inputs["c"][b0:b0 + NB], inputs["c_ctx"][None, :]], 0))
        for k in PARAM_SHAPES:
            m[k] = np.ascontiguousarray(inputs[k])
        for k, v in cst.items():
            m["c_" + k] = v
        maps.append(m)
    return maps


def kernel(**inputs):
    inputs = {k: np.asarray(v) for k, v in inputs.items()}
    nc = build_program()
    maps = make_in_maps(inputs)
    res = run_bass_kernel_spmd(nc, maps, core_ids=list(range(NCORES)))
    return np.concatenate([r["y"] for r in res.results], axis=0).astype(np.float32)
```

```python
import math
from contextlib import ExitStack

import numpy as np
import ml_dtypes

import concourse.bass as bass
import concourse.mybir as mybir
from concourse.bass_utils import run_bass_kernel_spmd

F32 = mybir.dt.float32
BF16 = mybir.dt.bfloat16
AF = mybir.ActivationFunctionType
ALU = mybir.AluOpType

NCORES = 8
NB = 4
D = 1024
L = 2048
LC = 256
NT = 18
EPS = 1e-6
MAGIC = 12582912.0
USE_ARS = False
NF = 4096


class Sched:
    ENG = ("pe", "act", "dve", "pool", "sp")

    def __init__(self, nc, ndma=8):
        self.nc = nc
        self.streams = {e: [] for e in self.ENG}
        self.cnt = {}
        self.waited = {}
        self.res = {}
        self.dmai = {e: 0 for e in self.ENG}
        self.ndma = ndma
        self.epoch = 0
        self.semh = {}

    def op(self, eng, fn, reads=(), writes=(), adds=(), dma=False):
        deps = {}

        def add(k, v):
            if deps.get(k, 0) < v:
                deps[k] = v

        for r in reads:
            st = self.res.get(r)
            if st:
                for k, v in st[0].items():
                    add(k, v)
                if isinstance(r, tuple) and r[0] == "bank":
                    for k, v in st[1].items():
                        add(k, v)
        for w in writes:
            st = self.res.get(w)
            if st:
                for k, v in st[0].items():
                    add(k, v)
                for k, v in st[1].items():
                    add(k, v)
        for w in adds:
            st = self.res.get(w)
            if st:
                for k, v in st[1].items():
                    add(k, v)
        if dma:
            i = self.dmai[eng]
            self.dmai[eng] += 1
            key = ("dma", eng, i % self.ndma)
            prev = self.cnt.get(key, 0)
            if prev:
                add(key, prev)
            inc = 16
        else:
            key = ("done", eng, self.epoch)
            inc = 1
        newv = self.cnt.get(key, 0) + inc
        st = self.streams[eng]
        for k, v in deps.items():
            if eng == "pe" and k[0] == "done" and k[1] == "pe":
                continue
            if self.waited.get((eng, k), 0) >= v:
                continue
            self.waited[(eng, k)] = v
            st.append(("w", k, v))
        st.append(("o", fn, key, inc))
        self.cnt[key] = newv
        for w in writes:
            self.res[w] = [{key: newv}, {}]
        for w in adds:
            s_ = self.res.setdefault(w, [{}, {}])
            if s_[0].get(key, 0) < newv:
                s_[0][key] = newv
        for r in reads:
            s_ = self.res.setdefault(r, [{}, {}])
            if s_[1].get(key, 0) < newv:
                s_[1][key] = newv

    def barrier(self):
        for eng in self.ENG:
            for k, v in list(self.cnt.items()):
                if eng == "pe" and k[0] == "done" and k[1] == "pe":
                    continue
                if self.waited.get((eng, k), 0) >= v:
                    continue
                self.waited[(eng, k)] = v
                self.streams[eng].append(("w", k, v))

    def finish(self):
        for eng in self.ENG:
            for k, v in list(self.cnt.items()):
                if k[0] == "dma" and k[1] == eng:
                    self.streams[eng].append(("w", k, v))

    def make_sems(self, es):
        keys = set(self.cnt.keys())
        for i, k in enumerate(sorted(keys, key=str)):
            self.semh[k] = es.enter_context(self.nc.semaphore("s%d" % i))

    def replay(self, eng, e):
        for it in self.streams[eng]:
            if it[0] == "w":
                e.wait_ge(self.semh[it[1]], it[2])
            else:
                ins = it[1](e)
                ins.then_inc(self.semh[it[2]], it[3])


_CONST = None


def _bf(a):
    return np.ascontiguousarray(a.astype(np.float32)).astype(ml_dtypes.bfloat16)


def _partner(d):
    blk, r = divmod(d, 64)
    axis, r2 = divmod(r, 32)
    half, f = divmod(r2, 16)
    return blk * 64 + axis * 32 + (1 - half) * 16 + f


def _hy_pos(Lx):
    t = np.linspace(0.0, 1.0, Lx, dtype=np.float32).astype(np.float64)
    bands = np.linspace(1e-4, 15.0, 16, dtype=np.float32).astype(np.float64)
    w = (2.0 * math.pi * np.arange(Lx, dtype=np.float32) / Lx).astype(np.float32).astype(np.float64)
    z = np.concatenate([t[:, None], np.cos(bands[None] * w[:, None]), np.sin(bands[None] * w[:, None])], -1)
    min_decay = math.log(1e-2) / 1.5
    max_decay = math.log(1e-2) / 0.3
    deltas = np.linspace(min_decay, max_decay, 256, dtype=np.float32).astype(np.float64)
    dec = np.exp(-t[:, None] * np.abs(deltas)[None])
    return z, dec


def _hy_consts(Lx):
    N = 2 * Lx
    z, dec = _hy_pos(Lx)
    zc = np.zeros((N, 33), np.float64)
    dc = np.zeros((N, 256), np.float64)
    zc[:Lx] = z
    dc[:Lx] = dec
    for m in range(Lx + 1, N):
        zc[m] = z[N - m]
        dc[m] = -dec[N - m]
    H = Lx // 2
    nft = H // 128
    fidx = np.concatenate([np.arange(H), Lx - 1 - np.arange(H)]).astype(np.float64)
    om_all = 2.0 * math.pi * (fidx + 0.5) / N
    m = np.arange(N, dtype=np.float64)
    ang = m[:, None] * om_all[None, :]
    Fw = np.concatenate([np.cos(ang), -np.sin(ang)], 1)
    nrt = N // 128
    nch = N // 128
    FwT = Fw.reshape(nch, 128, nrt, 128).transpose(2, 1, 0, 3)
    om = 2.0 * math.pi * (np.arange(H, dtype=np.float64) + 0.5) / N
    tau = np.arange(H, dtype=np.float64)
    angE = (2 * tau)[:, None] * om[None, :]
    angO = (2 * tau + 1)[:, None] * om[None, :]
    parts = [np.cos(angE), -np.sin(angE), np.cos(angO), -np.sin(angO)]
    FwEO = np.stack([p_.reshape(nft, 128, nft, 128).transpose(2, 1, 0, 3) for p_ in parts], 1)
    inv = []
    for par in range(2):
        mat = (2.0 / N) * np.concatenate([parts[2 * par].T, parts[2 * par + 1].T], 0)
        inv.append(mat.reshape(2 * nft, 128, nft, 128).transpose(2, 1, 0, 3))
    FinvT = np.stack(inv, 1)
    zT = np.ascontiguousarray(zc.T).astype(np.float32)
    decT = np.ascontiguousarray(dc.reshape(nch, 128, 256).transpose(1, 0, 2)).astype(np.float32)
    return zT, decT, _bf(FwT), _bf(FinvT), _bf(FwEO)


def _consts():
    global _CONST
    if _CONST is not None:
        return _CONST
    c = {}
    c["ident"] = _bf(np.eye(128))
    pm = np.zeros((128, 128), np.float32)
    for m in range(128):
        pm[_partner(m), m] = 1.0
    c["permM"] = _bf(pm)
    b64 = np.zeros((128, 128), np.float32)
    b64[:64, :64] = 1.0
    b64[64:, 64:] = 1.0
    c["blk64"] = _bf(b64)
    c["ones128"] = _bf(np.ones((128, 128)))
    inv_freq = (10000.0 ** (-np.arange(16, dtype=np.float32) / 16)).astype(np.float32)
    tok = np.arange(L)
    row = (tok // 64).astype(np.float32)
    col = (tok % 64).astype(np.float32)
    cosT = np.ones((128, NT * 128), np.float32)
    sinT = np.zeros((128, NT * 128), np.float32)
    for m in range(128):
        d = m % 64
        axis, r2 = divmod(d, 32)
        half, f = divmod(r2, 16)
        pos = row if axis == 0 else col
        ang = (pos * inv_freq[f]).astype(np.float32)
        cosT[m, 256:] = np.cos(ang)
        sinT[m, 256:] = np.sin(ang) * (-1.0 if half == 0 else 1.0)
    c["cosT"] = _bf(cosT)
    c["sinT"] = _bf(sinT)
    ki = np.arange(128)[:, None]
    qi = np.arange(128)[None, :]
    prev = (qi <= ki).astype(np.float32)
    nxt = (ki <= qi).astype(np.float32)
    one = np.ones((128, 128), np.float32)
    zero = np.zeros((128, 128), np.float32)
    masks = np.stack([np.concatenate([prev, zero], 1), np.concatenate([one, prev], 1),
                      np.concatenate([nxt, one], 1), np.concatenate([zero, nxt], 1)], 1)
    c["masks"] = _bf(masks)
    c["identf"] = np.eye(128, dtype=np.float32)
    zT, decT, FwT, FinvT, FwEO = _hy_consts(L)
    c["zT"], c["decT"], c["FwT"], c["FinvT"], c["FwEO"] = zT, decT, FwT, FinvT, FwEO
    zTc, decTc, FwTc, FinvTc, FwEOc = _hy_consts(LC)
    c["zTc"], c["decTc"], c["FwTc"], c["FinvTc"], c["FwEOc"] = zTc, decTc, FwTc, FinvTc, FwEOc
    _CONST = c
    return c


PARAM_SHAPES = {
    "norm_g": [2, 1024], "w_mod": [2, 1024, 3072], "b_mod": [2, 3072], "w_in": [2, 1024, 3072],
    "w_out": [2, 1024, 1024], "qn_a": [2, 64], "kn_a": [2, 64], "qn_c": [2, 64], "kn_c": [2, 64],
    "sink_c": [2, 6], "hy_conv_w": [2, 3, 768], "hy_conv_b": [2, 768], "hy_w1": [2, 33, 64],
    "hy_b1": [2, 64], "hy_w2": [2, 64, 64], "hy_b2": [2, 64], "hy_w3": [2, 64, 1024],
    "hy_freq": [2, 2, 64], "hy_bias": [2, 2, 256],
}

CH_AQ, CH_AK, CH_AV, CH_AG = 0, 3, 4, 5
CH_BV, CH_BX1, CH_BX2, CH_BG = 8, 10, 12, 14
CH_CQ, CH_CK, CH_CV, CH_CG = 16, 19, 20, 21


def build_program(nlayers=2, phases=("p0", "p1", "p2", "p3"), debug=(), nb=NB):
    nc = bass.Bass("TRN2", target_bir_lowering=False)
    cst = _consts()
    es = ExitStack()
    S = Sched(nc)

    def din(name, shape, dt=F32):
        return nc.dram_tensor(name, list(shape), dt, kind="ExternalInput")

    T = {}
    T["x"] = din("x", [NB, L, D])
    T["ctx"] = din("ctx", [NB, LC, D])
    T["cR"] = din("cR", [5, D])
    for k, shp in PARAM_SHAPES.items():
        T[k] = din(k, shp)
    for k, v in cst.items():
        T[k] = din("c_" + k, v.shape, BF16 if v.dtype == ml_dtypes.bfloat16 else F32)
    y = nc.dram_tensor("y", [NB, L, D], F32, kind="ExternalOutput")

    def scratch(name, shape, dt):
        return nc.dram_tensor(name, list(shape), dt)

    w_in_t = scratch("w_in_t", [2, 24, 128, 8, 128], BF16)
    w_out_t = scratch("w_out_t", [2, 128, 8, 1024], BF16)
    modrow = scratch("modrow", [2, 5, 3072], F32)
    gtab = scratch("gtab", [2, 2, 2, 16, 128, 256], BF16)
    gtabc = scratch("gtabc", [2, 2, 2, 128, 256], BF16)
    hyl = {n: scratch("hy_" + n, [16, 128, NB, 256], BF16) for n in ("v", "x1", "x2", "bg")}
    hyc = {n: scratch("hyc_" + n, [2, 128, NB, 256], BF16) for n in ("v", "x1", "x2", "bg")}
    mix = scratch("mix", [NB, NT * 128, D], BF16)
    xs = scratch("xs", [NB, L, D], F32)
    ctxs = scratch("ctxs", [NB, LC, D], F32)
    dbg_out = {}
    for ent in debug:
        name, shape = ent[0], ent[1]
        dbg_out[name] = nc.dram_tensor("dbg_" + name, list(shape), BF16 if (len(ent) > 2 and ent[2] == "bf16") else F32,
                                       kind="ExternalOutput")

    used_names = {}

    def sb(name, shape, dt, stk=None):
        n = used_names.get(name, 0)
        used_names[name] = n + 1
        if n:
            name = "%s_%d" % (name, n)
        return (stk or es).enter_context(nc.sbuf_tensor(name, list(shape), dt))

    PS = [es.enter_context(nc.psum_tensor("ps%d" % i, [128, 512], F32)) for i in range(8)]
    PSB = [p.bitcast(BF16) for p in PS]

    def bank(i):
        return ("bank", i)

    def dma(q, out, in_, reads, writes=(), adds=(), **kw):
        def fn(e):
            return e.dma_start(out=out, in_=in_, **kw)
        S.op(q, fn, reads=reads, writes=writes, adds=adds, dma=True)

    def pe(mms, reads, writes):
        mms = list(mms)

        def fn(e):
            ins = None
            for (o, l, r, st, sp) in mms:
                ins = e.matmul(o, lhsT=l, rhs=r, start=st, stop=sp)
            return ins
        S.op("pe", fn, reads=reads, writes=writes)

    def pe_tr(trs, ident, reads, writes):
        trs = list(trs)

        def fn(e):
            ins = None
            for (o, i) in trs:
                ins = e.transpose(o, i, ident)
            return ins
        S.op("pe", fn, reads=reads, writes=writes)

    def act(out, in_, func, reads, writes=(), scale=1.0, bias=0.0, accum=None, adds=()):
        def fn(e):
            kw = {}
            if accum is not None:
                kw["accum_out"] = accum
            return e.activation(out=out, in_=in_, func=func, bias=bias, scale=scale, **kw)
        S.op("act", fn, reads=reads, writes=writes, adds=adds)

    def tsc(eng, out, in0, s1, s2, op0, op1, reads, writes=(), adds=()):
        def fn(e):
            if s2 is None:
                return e.tensor_scalar(out=out, in0=in0, scalar1=s1, scalar2=None, op0=op0)
            return e.tensor_scalar(out=out, in0=in0, scalar1=s1, scalar2=s2, op0=op0, op1=op1)
        S.op(eng, fn, reads=reads, writes=writes, adds=adds)

    def tt(eng, out, in0, in1, op, reads, writes=(), adds=()):
        def fn(e):
            return e.tensor_tensor(out=out, in0=in0, in1=in1, op=op)
        S.op(eng, fn, reads=reads, writes=writes, adds=adds)

    def stt(eng, out, in0, scalar, in1, op0, op1, reads, writes=(), adds=()):
        def fn(e):
            return e.scalar_tensor_tensor(out=out, in0=in0, scalar=scalar, in1=in1, op0=op0, op1=op1)
        S.op(eng, fn, reads=reads, writes=writes, adds=adds)

    def recip(out, in_, reads, writes):
        def fn(e):
            return e.reciprocal(out=out, in_=in_)
        S.op("dve", fn, reads=reads, writes=writes)

    def copy(eng, out, in_, reads, writes=(), adds=()):
        if eng == "act":
            def fn(e):
                return e.copy(out=out, in_=in_)
        else:
            def fn(e):
                return e.tensor_copy(out=out, in_=in_)
        S.op(eng, fn, reads=reads, writes=writes, adds=adds)

    def memset(eng, ap, val, writes):
        def fn(e):
            return e.memset(ap, val)
        S.op(eng, fn, writes=writes)


    ident = sb("ident", [128, 128], BF16)
    permM = sb("permM", [128, 128], BF16)
    blk64 = sb("blk64", [128, 128], BF16)
    ones128 = sb("ones128", [128, 128], BF16)
    AB = sb("AB", [128, 2, 2, 8, 5], F32)
    identf = sb("identf", [128, 128], F32)
    for name, t in (("ident", ident), ("permM", permM), ("blk64", blk64), ("ones128", ones128), ("identf", identf)):
        dma("sp", t[:], T[name].ap(), reads=[], writes=[name])
    gsb = sb("gsb", [128, 16], F32)
    cTs = sb("cTs", [128, 48], F32)
    cwb = sb("cwb", [128, 48], F32)
    gains = sb("gains", [128, 16], F32)
    bfr = sb("bfr", [64, 16], F32)
    esk = sb("esk", [128, 16], F32)
    with ExitStack() as pr:
        r_g = sb("r_g", [16, 128], F32, pr)
        r_c = sb("r_c", [40, 128], F32, pr)
        r_w = sb("r_w", [48, 128], F32, pr)
        r_q = sb("r_q", [8, 128], F32, pr)
        r_b = sb("r_b", [8, 64], F32, pr)
        dma("sp", r_g[:], T["norm_g"].ap().rearrange("l (k p) -> (l k) p", p=128), reads=[], writes=["r_g"])
        dma("sp", r_c[:], T["cR"].ap().rearrange("j (k p) -> (j k) p", p=128), reads=[], writes=["r_c"])
        dma("sp", r_w[0:36, :], T["hy_conv_w"].ap().rearrange("l j (c p) -> (l j c) p", p=128), reads=[], adds=["r_w"])
        dma("sp", r_w[36:48, :], T["hy_conv_b"].ap().rearrange("l (c p) -> (l c) p", p=128), reads=[], adds=["r_w"])
        for v, nm in enumerate(("qn_a", "kn_a", "qn_c", "kn_c")):
            for hh in range(2):
                dma("sp", r_q[2 * v:2 * v + 2, 64 * hh:64 * hh + 64], T[nm].ap(), reads=[], adds=["r_q"])
        dma("sp", r_b[0:2, :], T["hy_b1"].ap(), reads=[], adds=["r_b"])
        dma("sp", r_b[2:4, :], T["hy_b2"].ap(), reads=[], adds=["r_b"])
        dma("sp", r_b[4:8, :], T["hy_freq"].ap().rearrange("l j p -> (l j) p"), reads=[], adds=["r_b"])
        dma("sp", esk[:, 0:12], T["sink_c"].ap().rearrange("l h -> (l h)").partition_broadcast(128), reads=[], writes=["esk"])
        for (rt, n, m, dst, nm) in ((r_g, 16, 128, gsb, "gsb"), (r_c, 40, 128, cTs, "cTs"), (r_w, 48, 128, cwb, "cwb"),
                                    (r_q, 8, 128, gains, "gains"), (r_b, 8, 64, bfr, "bfr")):
            pe([(PS[7][0:m, 0:n], rt[0:n, 0:m], identf[0:n, 0:n], True, True)],
               reads=["r_" + nm[0] if nm != "gains" and nm != "bfr" and nm != "cwb" and nm != "cTs" else
                      {"gains": "r_q", "bfr": "r_b", "cwb": "r_w", "cTs": "r_c"}[nm], "identf"], writes=[bank(7)])
            copy("dve", dst[0:m, 0:n], PS[7][0:m, 0:n], reads=[bank(7)], writes=[nm])
        act(esk[:, 0:12], esk[:, 0:12], AF.Exp, reads=["esk"], writes=["esk"])
    S.barrier()

    if "p0" in phases:
        for l in range(nlayers):
            for k in range(8):
                rows = slice(128 * k, 128 * (k + 1))
                for (c0, off) in ((CH_AQ, 0), (CH_CQ, 2048)):
                    for c in range(3):
                        dst = w_in_t[l, c0 + c, :, k, :].rearrange("r (p d) -> r p d", p=2)
                        src = T["w_in"][l, rows, off:off + 384].rearrange("r (p c d) -> r c p d", p=2, c=3)[:, c]
                        dma("pool", dst, src, reads=[], adds=[("w_in_t", l)])
                for (n0, n1) in ((3, 16), (19, 24)):
                    dst = w_in_t[l, n0:n1, :, k, :].rearrange("c r d -> r c d")
                    src = T["w_in"][l, rows, 128 * n0:128 * n1].rearrange("r (c d) -> r c d", d=128)
                    dma("pool", dst, src, reads=[], adds=[("w_in_t", l)])
                dma("pool", w_out_t[l, :, k, :], T["w_out"][l, rows, :], reads=[], adds=[("w_out_t", l)])

        with ExitStack() as p0:
            sc = sb("sc", [128, 48], F32, p0)
            bmod = sb("bmod", [5, 3072], F32, p0)
            mrow = sb("mrow", [5, 3072], F32, p0)
            wm = [sb("wm%d" % i, [128, 3072], F32, p0) for i in range(2)]
            act(sc[:, 0:40], cTs[:, 0:40], AF.Silu, reads=["cTs"], writes=["sc"])
            sc3 = sc[:, 0:40].rearrange("p (j k) -> p j k", k=8)
            wi = 0
            for l in range(nlayers):
                dma("sp", bmod[:], T["b_mod"][l].partition_broadcast(5), reads=[], writes=["bmod"])
                for k in range(8):
                    w = wm[wi % 2]
                    wr = ("wm", wi % 2)
                    wi += 1
                    dma("sp", w[:], T["w_mod"][l, 128 * k:128 * (k + 1), :], reads=[], writes=[wr])
                    pe([(PS[n][0:5, :], sc3[:, :, k], w[:, 512 * n:512 * (n + 1)], k == 0, k == 7) for n in range(6)],
                       reads=["sc", wr], writes=[bank(n) for n in range(6)])
                for n in range(6):
                    tt("dve", mrow[:, 512 * n:512 * (n + 1)], PS[n][0:5, :], bmod[:, 512 * n:512 * (n + 1)], ALU.add,
                       reads=[bank(n), "bmod"], writes=[("mrow", n)])
                dma("sp", modrow[l], mrow[:], reads=[("mrow", n) for n in range(6)], writes=[("modrow", l)])
                ps6 = PS[6][:, 0:80].rearrange("p (s k j) -> p s k j", s=2, k=8)
                pe([(ps6[:, s, k, :], mrow[:, 1024 * s + 128 * k:1024 * s + 128 * (k + 1)], identf[0:5, 0:5], True, True)
                    for s in range(2) for k in range(8)],
                   reads=[("mrow", n) for n in range(4)] + ["identf"], writes=[bank(6)])
                copy("dve", AB[:, l, 1, :, :], ps6[:, 0, :, :], reads=[bank(6)], writes=[("AB", l, 1)])
                for k in range(8):
                    tsc("dve", AB[:, l, 0, k, :], ps6[:, 1, k, :], 1.0, None, ALU.add, None,
                        reads=[bank(6)], writes=[("AB", l, 0, k)])
                    tsc("dve", AB[:, l, 0, k, :], AB[:, l, 0, k, :], gsb[:, 8 * l + k:8 * l + k + 1], None, ALU.mult, None,
                        reads=[("AB", l, 0, k), "gsb"], writes=[("AB", l, 0, k)])

        S.barrier()
        def filter_tables(l, Lx, zT_d, dec_d, Fw_d, dst_tab, tag):
            N = 2 * Lx
            nblk = N // 512
            nmt = N // 128
            half = nmt // 2
            with ExitStack() as pf:
                w1 = sb("hw1" + tag, [33, 64], F32, pf)
                w2 = sb("hw2" + tag, [64, 64], F32, pf)
                w3 = sb("hw3" + tag, [64, 1024], F32, pf)
                zTs = sb("zTs" + tag, [33, N], F32, pf)
                dec = sb("dec" + tag, [128, nmt, 256], F32, pf)
                dbc = sb("dbc" + tag, [128, 512], F32, pf)
                h2T = sb("h2T" + tag, [64, N], F32, pf)
                Gb = sb("Gb" + tag, [128, nmt, 512], BF16, pf)
                rn = sb("rn" + tag, [128, 512], F32, pf)
                tmp = [sb("ft%d" % i + tag, [64, 512], F32, pf) for i in range(8)]
                g32 = [sb("g32%d" % i + tag, [128, 512], F32, pf) for i in range(2)]
                gab = [sb("gab%d" % i + tag, [128, 512], BF16, pf) for i in range(2)]
                fw = [sb("fw%d" % i + tag, [128, nmt, 128], BF16, pf) for i in range(2)]
                tb = [sb("tb%d" % i + tag, [128, 512], F32, pf) for i in range(2)]
                tbb = [sb("tbb%d" % i + tag, [128, 512], BF16, pf) for i in range(2)]
                R = lambda n: (n, tag)
                dma("sp", w1[:], T["hy_w1"][l], reads=[], writes=[R("w1")])
                dma("sp", w2[:], T["hy_w2"][l], reads=[], writes=[R("w2")])
                dma("sp", w3[:], T["hy_w3"][l], reads=[], writes=[R("w3")])
                dma("sp", zTs[:], zT_d.ap(), reads=[], writes=[R("zTs")])
                dma("sp", dec[:], dec_d.ap(), reads=[], writes=[R("dec")])
                dma("sp", dbc[:], T["hy_bias"][l].rearrange("o c -> (o c)").partition_broadcast(128),
                    reads=[], writes=[R("dbc")])
                bfr_r = ["bfr"]

                slc = [0]

                def sin_layer(src_ps, srcbank, bcol, fcol, dst, dst_res, i):
                    o_ = 4 * (slc[0] % 2)
                    slc[0] += 1
                    a, t1, t2, a2 = tmp[o_:o_ + 4]
                    ra = [R("tmp%d" % (o_ + j)) for j in range(4)]
                    tsc("dve", a[:], src_ps, bfr[:, bcol:bcol + 1], bfr[:, fcol:fcol + 1], ALU.add, ALU.mult,
                        reads=[srcbank] + bfr_r, writes=[ra[0]])
                    tsc("dve", t1[:], a[:], float(1.0 / (2 * math.pi)), MAGIC, ALU.mult, ALU.add,
                        reads=[ra[0]], writes=[ra[1]])
                    tsc("dve", t2[:], t1[:], -MAGIC, float(-2 * math.pi), ALU.add, ALU.mult,
                        reads=[ra[1]], writes=[ra[2]])
                    tt("dve", a2[:], t2[:], a[:], ALU.add, reads=[ra[2], ra[0]], writes=[ra[3]])
                    tsc("dve", a2[:], a2[:], 3.1415925, -3.1415925, ALU.min, ALU.max, reads=[ra[3]], writes=[ra[3]])
                    act(dst, a2[:], AF.Sin, reads=[ra[3]], writes=[dst_res])

                h1s = [sb("h1%d" % i + tag, [64, 512], F32, pf) for i in range(2)]
                for blk in range(nblk):
                    cs = slice(512 * blk, 512 * (blk + 1))
                    h1 = h1s[blk % 2]
                    b0, b1 = 2 * (blk % 2), 2 * (blk % 2) + 1
                    pe([(PS[b0][0:64, :], w1[:], zTs[:, cs], True, True)], reads=[R("w1"), R("zTs")], writes=[bank(b0)])
                    sin_layer(PS[b0][0:64, :], bank(b0), l, 4 + 2 * l, h1[:], R(("h1", blk % 2)), blk)
                    pe([(PS[b1][0:64, :], w2[:], h1[:], True, True)], reads=[R("w2"), R(("h1", blk % 2))], writes=[bank(b1)])
                    sin_layer(PS[b1][0:64, :], bank(b1), 2 + l, 5 + 2 * l, h2T[:, cs], R(("h2T", blk)), blk)
                for mt in range(nmt):
                    dr = 0 if mt < half else 1
                    pb = 6 + (mt % 2)
                    pe([(PS[pb][:, 256 * o:256 * (o + 1)], h2T[:, 128 * mt:128 * (mt + 1)],
                         w3[:, 512 * o + 256 * dr:512 * o + 256 * dr + 256], True, True) for o in range(2)],
                       reads=[R(("h2T", mt // 4)), R("w3")], writes=[bank(pb)])
                    g = g32[mt % 2]
                    for o in range(2):
                        tt("dve", g[:, 256 * o:256 * (o + 1)], PS[pb][:, 256 * o:256 * (o + 1)], dec[:, mt, :], ALU.mult,
                           reads=[bank(pb), R("dec")], writes=[R(("g32", mt % 2, o))])
                    gr = [R(("g32", mt % 2, o)) for o in range(2)]
                    copy("act", Gb[:, mt, :], g[:], reads=gr, writes=[R(("Gb", mt))])
                    stt("dve", gab[mt % 2][:], g[:], -1.0, g[:], ALU.mult, ALU.max, reads=gr, writes=[R(("gab", mt % 2))])
                    pe([(PS[4][:], ones128[:], gab[mt % 2][:], mt == 0, mt == nmt - 1)],
                       reads=["ones128", R(("gab", mt % 2))], writes=[bank(4)])
                recip(rn[:], PS[4][:], reads=[bank(4)], writes=[R("rn")])
                for rt in range(nmt):
                    f_ = fw[rt % 2]
                    fr_ = R(("fw", rt % 2))
                    dma("sp", f_[:], Fw_d[rt], reads=[], writes=[fr_])
                    pb = 2 + (rt % 2)
                    pe([(PS[pb][:], f_[:, c, :], Gb[:, c, :], c == 0, c == nmt - 1) for c in range(nmt)],
                       reads=[fr_] + [R(("Gb", c)) for c in range(nmt)], writes=[bank(pb)])
                    t_ = tb[rt % 2]
                    tr_ = R(("tb", rt % 2))
                    tq_ = tbb[rt % 2]
                    tqr_ = R(("tbb", rt % 2))
                    if rt < half:
                        tt("dve", t_[:], PS[pb][:], rn[:], ALU.mult, reads=[bank(pb), R("rn")], writes=[tr_])
                        tt("pool", tq_[:], t_[:], dbc[:], ALU.add, reads=[tr_, R("dbc")], writes=[tqr_])
                    else:
                        tt("dve", tq_[:], PS[pb][:], rn[:], ALU.mult, reads=[bank(pb), R("rn")], writes=[tqr_])
                    ri, ft = (0, rt) if rt < half else (1, rt - half)
                    for o in range(2):
                        dma("pool", dst_tab(o, ri, ft), tq_[:, 256 * o:256 * (o + 1)], reads=[tqr_],
                            adds=[("gtab", tag)])
            S.barrier()

        for l in range(nlayers):
            filter_tables(l, L, T["zT"], T["decT"], T["FwT"],
                          lambda o, ri, ft, l=l: gtab[l, o, ri, ft], "L%d" % l)
        filter_tables(0, LC, T["zTc"], T["decTc"], T["FwTc"],
                      lambda o, ri, ft: gtabc[o, ri, ft], "C")

    cosT = sb("cosT", [128, NT * 128], BF16)
    sinT = sb("sinT", [128, NT * 128], BF16)
    masks = sb("masks", [128, 4, 256], BF16)
    dma("sp", cosT[:], T["cosT"].ap(), reads=[], writes=["cosT"])
    dma("sp", sinT[:], T["sinT"].ap(), reads=[], writes=["sinT"])
    dma("sp", masks[:], T["masks"].ap(), reads=[], writes=["masks"])

    rr = {}

    def nxt(name, n):
        v = rr.get(name, 0)
        rr[name] = v + 1
        return v % n

    def xsrc(l, b, ti):
        if ti < 2:
            t = T["ctx"] if l == 0 else ctxs
            return t[b, 128 * ti:128 * (ti + 1), :], ([] if l == 0 else [("xs", b, ti)])
        t = T["x"] if l == 0 else xs
        return t[b, 128 * (ti - 2):128 * (ti - 1), :], ([] if l == 0 else [("xs", b, ti)])

    def phase1(l, b, P):
        ctxq = (l == 0)
        hT, vaug, gX, PT = P["hT"], P["vaug"], P["gX"], P["PT"]
        xt, xn, junk, st4 = P["xt"], P["xn"], P["junk"], P["st4"]
        wfm, wtm, wbg = P["wfm"], P["wtm"], P["wbg"]
        ubuf, dg = P["ubuf"], P["dg"]
        qg, sq, t1, t2, rs, rr = P["qg"], P["sq"], P["t1"], P["t2"], P["rs"], P["rr"]
        ostage, hstage, bstage, den = P["ostage"], P["hstage"], P["bstage"], P["den"]

        def h_items(bh, hbanks):
            itemsA, itemsB = [], []
            for ti in range(NT):
                st = {}

                def partA(ti=ti, st=st):
                    s_ = nxt("xt", 4)
                    st["s"] = s_
                    src, sres = xsrc(l, bh, ti)
                    dma("sp", xt[s_][:], src, reads=sres, writes=[("xt", s_)])
                    act(junk[:], xt[s_][:], AF.Square, reads=[("xt", s_)], writes=["junk", ("ss", s_)],
                        accum=st4[:, 4 * s_:4 * s_ + 1])
                    act(st4[:, 4 * s_ + 1:4 * s_ + 2], st4[:, 4 * s_:4 * s_ + 1], AF.Sqrt, scale=1.0 / D, bias=EPS,
                        reads=[("ss", s_)], writes=[("ss2", s_)])
                    recip(st4[:, 4 * s_ + 2:4 * s_ + 3], st4[:, 4 * s_ + 1:4 * s_ + 2], reads=[("ss2", s_)], writes=[("rstd", s_)])
                    tsc("dve", xn[s_][:], xt[s_][:], st4[:, 4 * s_ + 2:4 * s_ + 3], None, ALU.mult, None,
                        reads=[("xt", s_), ("rstd", s_)], writes=[("xn", s_)])

                def partB(ti=ti, st=st):
                    s_ = st["s"]
                    j = 4 if ti < 2 else bh
                    pb = hbanks[nxt("hb", 1000) % len(hbanks)]
                    pe_tr([(PSB[pb][:, 128 * k:128 * (k + 1)], xn[s_][:, 128 * k:128 * (k + 1)]) for k in range(8)], ident[:],
                          reads=[("xn", s_), "ident"], writes=[bank(pb)])
                    for k in range(8):
                        tsc("dve", hT[:, k, 128 * ti:128 * (ti + 1)], PSB[pb][:, 128 * k:128 * (k + 1)],
                            AB[:, l, 0, k, j:j + 1], AB[:, l, 1, k, j:j + 1], ALU.mult, ALU.add,
                            reads=[bank(pb), ("AB", l, 0, k), ("AB", l, 1)], adds=[("hT", ti)])
                itemsA.append(partA)
                itemsB.append(partB)
            out = []
            for i in range(NT + 2):
                if i < NT:
                    out.append(itemsA[i])
                if i >= 2:
                    out.append(itemsB[i - 2])
            return out

        if not P.get("h_done", False):
            for it in h_items(b, [0, 1]):
                it()
        P["h_done"] = False

        blocks_l = [(256 + 512 * i, 512) for i in range(4)]
        blocks_c = [(0, 256)]

        def blk_tiles(c0, n):
            return [("hT", t) for t in range(c0 // 128, (c0 + n) // 128)]

        BK = {"fm": [2, 3], "nr": [(4, 5), (6, 7)]}

        def fm_items(chunk, blocks, post):
            st = {}
            items = []
            for bi, (c0, n) in enumerate(blocks):
                def item(bi=bi, c0=c0, n=n):
                    if bi == 0:
                        st["s"] = nxt("wfm", 2)
                        dma("sp", wfm[st["s"]][:], w_in_t[l, chunk], reads=[("w_in_t", l)], writes=[("wfm", st["s"])])
                    s_ = st["s"]
                    fb = BK["fm"]
                    pb = fb[nxt("fmb", 1000) % len(fb)]
                    pe([(PS[pb][:, 0:n], wfm[s_][:, k, :], hT[:, k, c0:c0 + n], k == 0, k == 7) for k in range(8)],
                       reads=[("wfm", s_)] + blk_tiles(c0, n), writes=[bank(pb)])
                    post(pb, c0, n)
                items.append(item)
            return items

        def fm_chunk(chunk, blocks, post):
            for it in fm_items(chunk, blocks, post):
                it()

        def qk_post(gcol, dst_fn):
            def post(pb, c0, n):
                s_ = nxt("nr", 2)
                pp, pq = BK["nr"][nxt("nrb", 1000) % len(BK["nr"])]
                act(qg[s_][:, 0:n], PS[pb][:, 0:n], AF.Identity, scale=gains[:, gcol:gcol + 1],
                    reads=[bank(pb), "gains"], writes=[("qg", s_)])
                act(sq[s_][:, 0:n], PS[pb][:, 0:n], AF.Square, reads=[bank(pb)], writes=[("sq", s_)])
                pe([(PS[pp][:, 0:n], permM[:], qg[s_][:, 0:n], True, True)], reads=["permM", ("qg", s_)], writes=[bank(pp)])
                pe([(PS[pq][:, 0:n], blk64[:], sq[s_][:, 0:n], True, True)], reads=["blk64", ("sq", s_)], writes=[bank(pq)])
                tt("pool", t1[s_][:, 0:n], qg[s_][:, 0:n], cosT[:, c0:c0 + n], ALU.mult,
                   reads=[("qg", s_), "cosT"], writes=[("t1", s_)])
                tt("dve", t2[s_][:, 0:n], PS[pp][:, 0:n], sinT[:, c0:c0 + n], ALU.mult,
                   reads=[bank(pp), "sinT"], writes=[("t2", s_)])
                tt("pool", t1[s_][:, 0:n], t1[s_][:, 0:n], t2[s_][:, 0:n], ALU.add,
                   reads=[("t1", s_), ("t2", s_)], writes=[("t1", s_)])
                if USE_ARS:
                    act(rr[s_][:, 0:n], PS[pq][:, 0:n], AF.Abs_reciprocal_sqrt, scale=1.0 / 64, bias=EPS,
                        reads=[bank(pq)], writes=[("rr", s_)])
                else:
                    act(rs[s_][:, 0:n], PS[pq][:, 0:n], AF.Ln, scale=1.0 / 64, bias=EPS, reads=[bank(pq)], writes=[("rs", s_)])
                    act(rr[s_][:, 0:n], rs[s_][:, 0:n], AF.Exp, scale=-0.5, reads=[("rs", s_)], writes=[("rr", s_)])
                dst, dres = dst_fn(c0, n)
                tt("dve", dst, t1[s_][:, 0:n], rr[s_][:, 0:n], ALU.mult, reads=[("t1", s_), ("rr", s_)], writes=[dres])
            return post

        def tm_group(wt, wres, chunks, ncols, tiles, post):
            for ci, ch in enumerate(chunks):
                dma("sp", wt[:, :, 128 * ci:128 * (ci + 1)], w_in_t[l, ch], reads=[("w_in_t", l)], adds=[wres])
            for ti in tiles:
                pb = nxt("tmb", 2)
                pe([(PS[pb][:, 0:ncols], hT[:, k, 128 * ti:128 * (ti + 1)], wt[:, k, 0:ncols], k == 0, k == 7) for k in range(8)],
                   reads=[wres, ("hT", ti)], writes=[bank(pb)])
                post(pb, ti)

        def attention(X, xoff, qX, kX, inter):
            esoff = 6 * l
            inter = list(inter)
            qblocks = ([(0, None)] if ctxq else []) + [(2 + 2 * i, 2 * i) for i in range(8)]
            units = []
            for (qt0, lt0) in qblocks:
                for h in range(6):
                    units.append((qt0, lt0, h))
            state = {}

            def keylist(lt0):
                if lt0 is None:
                    return [(0, None, True, True), (1, None, True, True)]
                if X == "A":
                    return [(t, None, True, True) for t in range(NT)]
                KL = [(0, None, True, True), (1, None, True, True)]
                for mi, jt in enumerate((lt0 - 1, lt0, lt0 + 1, lt0 + 2)):
                    if 0 <= jt < 16:
                        KL.append((2 + jt, mi, mi != 3, mi != 0))
                return KL

            def emit_qk(ui, lo, hi):
                qt0, lt0, h = units[ui]
                c, p = h % 3, h // 3
                rows = slice(64 * p, 64 * p + 64)
                if ui not in state:
                    state[ui] = (keylist(lt0), nxt("PT", 2))
                KL, ps_ = state[ui]
                npairs = (len(KL) + 1) // 2
                for pi in range(lo, min(hi, npairs)):
                    idxs = [i for i in (2 * pi, 2 * pi + 1) if i < len(KL)]
                    spb = nxt("sreg", 5)
                    pe([(PS[spb][:, 256 * j:256 * (j + 1)], kX[rows, 128 * KL[i][0]:128 * (KL[i][0] + 1)],
                         qX[rows, c, 128 * qt0:128 * qt0 + 256], True, True) for j, i in enumerate(idxs)],
                       reads=[("kX", KL[i][0]) for i in idxs] + [("qX", c, qt0 // 2)], writes=[bank(spb)])
                    n2 = len(idxs)
                    act(PT[ps_][:, idxs[0]:idxs[0] + n2, :].rearrange("p a b -> p (a b)"), PS[spb][:, 0:256 * n2], AF.Exp,
                        scale=0.125, reads=[bank(spb)], writes=[("PT", ps_, i) for i in idxs])
                    for i in idxs:
                        mk = KL[i][1]
                        if mk is not None:
                            pat = masks[:, 0 if mk < 2 else 2, 0:128]
                            co = 128 * (mk % 2)
                            tt("pool", PT[ps_][:, i, co:co + 128], PT[ps_][:, i, co:co + 128], pat, ALU.mult,
                               reads=[("PT", ps_, i), "masks"], writes=[("PT", ps_, i)])
                return npairs

            def emit_pv(ui):
                qt0, lt0, h = units[ui]
                c, p = h % 3, h // 3
                KL, ps_ = state.pop(ui)
                if h == 0:
                    state["os"] = nxt("ostage", 2)
                os_ = state["os"]
                ob = 5 + nxt("ob", 2)
                O3 = PS[ob][:, 0:130].rearrange("p (u e) -> p u e", e=65)
                for u2 in range(2):
                    lst = [idx for idx, kl in enumerate(KL) if kl[2 + u2]]
                    pe([(O3[:, u2, :], PT[ps_][:, idx, 128 * u2:128 * (u2 + 1)], vaug[:, KL[idx][0], p, :],
                         i == 0, i == len(lst) - 1) for i, idx in enumerate(lst)],
                       reads=[("PT", ps_, idx) for idx in lst] + [("vaug", KL[idx][0]) for idx in lst], writes=[bank(ob)])
                ds = nxt("den", 4)
                dn = den[:, 2 * ds:2 * ds + 2]
                if X == "C":
                    tsc("dve", dn, O3[:, :, 64], esk[:, esoff + h:esoff + h + 1], None, ALU.add, None,
                        reads=[bank(ob), "esk"], writes=[("den", ds)])
                    recip(dn, dn, reads=[("den", ds)], writes=[("den", ds)])
                else:
                    recip(dn, O3[:, :, 64], reads=[bank(ob)], writes=[("den", ds)])
                for u2 in range(2):
                    stt("dve", ostage[os_][:, u2, 64 * h:64 * h + 64], O3[:, u2, 0:64], den[:, 2 * ds + u2:2 * ds + u2 + 1],
                        gX[:, qt0 + u2, 64 * h:64 * h + 64], ALU.mult, ALU.mult,
                        reads=[bank(ob), ("den", ds), ("gX", qt0 + u2)], adds=[("ostage", os_)])
                if h == 5:
                    dma("pool", mix[b, 128 * qt0:128 * (qt0 + 2), xoff:xoff + 384].rearrange("(u p) n -> p u n", p=128),
                        ostage[os_][:], reads=[("ostage", os_)], adds=[("mix", b, qt0), ("mix", b, qt0 + 1)])

            NPRE = 5
            every = max(1, len(units) // max(1, len(inter))) if inter else 0
            emit_qk(0, 0, 99)
            for ui in range(len(units)):
                if ui + 1 < len(units):
                    emit_qk(ui + 1, 0, NPRE)
                emit_pv(ui)
                if ui + 1 < len(units):
                    emit_qk(ui + 1, NPRE, 99)
                if inter and (ui % every == every - 1):
                    inter.pop(0)()
            while inter:
                inter.pop(0)()

        all_tiles = list(range(NT))
        q_blocks = (blocks_c if ctxq else []) + blocks_l
        k_blocks = blocks_c + blocks_l
        def qk_items(X, qb, kb, chq, chk, gq, gk):
            items = []
            for c in range(3):
                def qpost_wrap(c=c):
                    base = qk_post(gq, lambda c0, n: (qb[:, c, c0:c0 + n], ("qX", c, c0 // 256)))

                    def post(pb, c0, n):
                        base(pb, c0, n)
                        if n == 512:
                            S.res[("qX", c, c0 // 256 + 1)] = S.res[("qX", c, c0 // 256)]
                    return post
                items += fm_items(chq + c, q_blocks, qpost_wrap())
            kbase = qk_post(gk, lambda c0, n: (kb[:, c0:c0 + n], ("kXblk", c0)))

            def kpost(pb, c0, n):
                kbase(pb, c0, n)
                for t in range(c0 // 128, (c0 + n) // 128):
                    S.res[("kX", t)] = S.res[("kXblk", c0)]
            items += fm_items(chk, k_blocks, kpost)
            return items

        def vg_post(pb, ti):
            copy("dve", vaug[:, ti, :, 0:64], PS[pb][:, 0:128].rearrange("p (h d) -> p h d", h=2),
                 reads=[bank(pb)], writes=[("vaug", ti)])
            if ti >= 2 or ctxq:
                act(gX[:, ti, :], PS[pb][:, 128:512], AF.Silu, reads=[bank(pb)], writes=[("gX", ti)])

        qA, kA, qC, kC = P["qX"], P["kX"], P["qX"], P["kX"]
        for it in qk_items("A", qA, kA, CH_AQ, CH_AK, 0 + l, 2 + l):
            it()
        tm_group(wtm, "wtm", [CH_AV, CH_AV + 1, CH_AV + 2, CH_AV + 3], 512, all_tiles, vg_post)
        attention("A", 0, qA, kA, [])
        for it in qk_items("C", qC, kC, CH_CQ, CH_CK, 4 + l, 6 + l):
            it()
        tm_group(wtm, "wtm", [CH_CV, CH_CV + 1, CH_CV + 2, CH_CV + 3], 512, all_tiles, vg_post)

        def finish_attention_c():
            if b + 1 < P["nb"]:
                inter = h_items(b + 1, [7])
                P["h_done"] = True
            else:
                inter = []
            attention("C", 640, qC, kC, inter)

        for tix, (tname, ch0) in enumerate((("v", CH_BV), ("x1", CH_BX1), ("x2", CH_BX2))):
            for jj in range(2):
                chn = ch0 + jj
                ccol = 2 * tix + jj

                pbs = nxt("pbuf", 2)
                pbuf = P["pbuf"][pbs]
                pres = ("pbuf", pbs)

                def hpost(pb, c0, n, pbuf=pbuf, pres=pres):
                    off = c0 + 1 if c0 < 256 else c0 + 3
                    copy("act", pbuf[:, off:off + n], PS[pb][:, 0:n], reads=[bank(pb)], adds=[pres])
                fm_chunk(chn, q_blocks, hpost)
                for (c0, n) in q_blocks:
                    off = c0 + 1 if c0 < 256 else c0 + 3
                    pb = 2 + nxt("fmb", 2)
                    pe([(PS[pb][:, 0:n], dg[:, 6 * j + ccol, :], pbuf[:, off - 1 + j:off - 1 + j + n], j == 0, j == 2)
                        for j in range(3)], reads=[pres, "dg"], writes=[bank(pb)])
                    act(ubuf[:, jj, c0:c0 + n], PS[pb][:, 0:n], AF.Identity,
                        bias=cwb[:, 36 + 6 * l + ccol:36 + 6 * l + ccol + 1],
                        reads=[bank(pb), "cwb"], adds=[("ubuf", jj, 0 if c0 < 256 else 256)])
            groups = ([("c", 0, 2)] if ctxq else []) + [("l", 4 * g, 4) for g in range(4)]
            for (kind, t0, nt_) in groups:
                pb = nxt("hb", 2)
                base = 0 if kind == "c" else 256
                pe_tr([(PSB[pb][:, 256 * q + 128 * jj:256 * q + 128 * (jj + 1)],
                        ubuf[:, jj, base + 128 * (t0 + q):base + 128 * (t0 + q + 1)]) for q in range(nt_) for jj in range(2)],
                      ident[:], reads=[("ubuf", 0, base), ("ubuf", 1, base), "ident"], writes=[bank(pb)])
                hs = nxt("hstage", 2)
                copy("act", hstage[hs][:, 0:256 * nt_], PSB[pb][:, 0:256 * nt_], reads=[bank(pb)], writes=[("hstage", hs)])
                dstt = (hyc if kind == "c" else hyl)[tname]
                dma("act", dstt[t0:t0 + nt_, :, b, :].rearrange("c p n -> p c n"),
                    hstage[hs][:, 0:256 * nt_].rearrange("p (c n) -> p c n", n=256),
                    reads=[("hstage", hs)], adds=[("hy", kind, tname)])

        def bg_post(pb, ti):
            bs = nxt("bstage", 2)
            act(bstage[bs][:], PS[pb][:, 0:256], AF.Silu, reads=[bank(pb)], writes=[("bstage", bs)])
            if ti < 2:
                d_, kind = hyc["bg"][ti, :, b, :], "c"
            else:
                d_, kind = hyl["bg"][ti - 2, :, b, :], "l"
            dma("act", d_, bstage[bs][:], reads=[("bstage", bs)], adds=[("hy", kind, "bg")])
        tm_group(wbg, "wbg", [CH_BG, CH_BG + 1], 256, (all_tiles if ctxq else all_tiles[2:]), bg_post)
        finish_attention_c()

    def phase2(l, kind, P):
        nft = 8 if kind == "l" else 1
        hy = hyl if kind == "l" else hyc
        Fd = T["FwEO"] if kind == "l" else T["FwEOc"]
        Fi = T["FinvT"] if kind == "l" else T["FinvTc"]
        tb = 2 if kind == "l" else 0
        U, ZZ, Gr, Gi = P["U"], P["ZZ"], P["Gr"], P["Gi"]
        fw, fin, xo, bgt, ys, yt = P["fw"], P["fin"], P["xo"], P["bgt"], P["ys"], P["yt"]
        osb, XX, mm, zz = P["osb"], P["XX"], P["mm"], P["zz"]
        tagk = ("L%d" % l) if kind == "l" else "C"

        def tile_src(name, cp, par):
            out = []
            for hlf in range(2):
                src = hy[name][2 * cp + hlf].rearrange("(p two) b n -> p two (b n)", two=2)[:, par, :]
                out.append((slice(64 * hlf, 64 * hlf + 64), src))
            return out

        for cp in range(nft):
            for par in range(2):
                for (psl, src) in tile_src("v", cp, par):
                    dma("sp", U[par][psl, cp, :], src, reads=[("hy", kind, "v")], adds=[("U", par, cp)])
        for o in range(2):
            tab = (lambda ri: gtab[l, o, ri]) if kind == "l" else (lambda ri: gtabc[o, ri])
            dma("sp", Gr[:, 0:2 * nft, :], tab(0).rearrange("f p n -> p f n"), reads=[("gtab", tagk)], writes=["Gr"])
            dma("sp", Gi[:, 0:2 * nft, :], tab(1).rearrange("f p n -> p f n"), reads=[("gtab", tagk)], writes=["Gi"])
            for ft in range(nft):
                fs = nxt("fw", 2)
                dma("sp", fw[fs][:, :, 0:nft, :], Fd[ft].rearrange("a p k n -> p a k n"), reads=[], writes=[("fw", fs)])
                for hh in range(2):
                    cs = slice(512 * hh, 512 * (hh + 1))
                    b0 = 4 * nxt("fwb", 2)
                    pe([(PS[b0 + a][:], fw[fs][:, a, kc, :], U[a // 2][:, kc, cs], kc == 0, kc == nft - 1)
                        for a in range(4) for kc in range(nft)],
                       reads=[("fw", fs)] + [("U", par, kc) for par in range(2) for kc in range(nft)],
                       writes=[bank(b0 + a) for a in range(4)])
                    q = nxt("osb", 2)
                    copy("act", osb[q][0][:], PS[b0 + 2][:], reads=[bank(b0 + 2)], writes=[("osr", q)])
                    copy("act", osb[q][1][:], PS[b0 + 3][:], reads=[bank(b0 + 3)], writes=[("osi", q)])
                    xq = nxt("XX", 2)
                    xr, xi, xr2, xi2 = XX[xq]
                    tt("dve", xr[:], PS[b0][:], osb[q][0][:], ALU.add, reads=[bank(b0), ("osr", q)], writes=[("xr", xq)])
                    tt("dve", xr2[:], PS[b0][:], osb[q][0][:], ALU.subtract, reads=[bank(b0), ("osr", q)], writes=[("xr2", xq)])
                    tt("dve", xi[:], PS[b0 + 1][:], osb[q][1][:], ALU.add, reads=[bank(b0 + 1), ("osi", q)], writes=[("xi", xq)])
                    tt("dve", xi2[:], osb[q][1][:], PS[b0 + 1][:], ALU.subtract, reads=[bank(b0 + 1), ("osi", q)], writes=[("xi2", xq)])
                    for bb in range(2):
                        c2 = slice(256 * bb, 256 * (bb + 1))
                        zc = slice(512 * hh + 256 * bb, 512 * hh + 256 * (bb + 1))
                        zq = nxt("zz", 2)
                        zr, zi, zr2, zi2 = zz[zq]

                        def prod(dst, dres, xa, xares, ga, gares, xb, xbres, gb, gbres, op, gt):
                            mq = nxt("mm", 4)
                            ma, mb = mm[mq]
                            tt("dve", ma[:], xa[:, c2], ga[:, gt, :], ALU.mult, reads=[(xares, xq), gares], writes=[("ma", mq)])
                            tt("dve", mb[:], xb[:, c2], gb[:, gt, :], ALU.mult, reads=[(xbres, xq), gbres], writes=[("mb", mq)])
                            tt("pool", dst[:], ma[:], mb[:], op, reads=[("ma", mq), ("mb", mq)], writes=[(dres, zq)])
                        prod(zr, "zr", xr, "xr", Gr, "Gr", xi, "xi", Gi, "Gi", ALU.subtract, ft)
                        prod(zi, "zi", xr, "xr", Gi, "Gi", xi, "xi", Gr, "Gr", ALU.add, ft)
                        prod(zr2, "zr2", xr2, "xr2", Gr, "Gr", xi2, "xi2", Gi, "Gi", ALU.subtract, nft + ft)
                        prod(zi2, "zi2", xr2, "xr2", Gi, "Gi", xi2, "xi2", Gr, "Gr", ALU.add, nft + ft)
                        tt("pool", ZZ[0][:, ft, zc], zr[:], zr2[:], ALU.add, reads=[("zr", zq), ("zr2", zq)], adds=[("Z", 0, ft)])
                        tt("pool", ZZ[1][:, ft, zc], zr[:], zr2[:], ALU.subtract, reads=[("zr", zq), ("zr2", zq)], adds=[("Z", 1, ft)])
                        tt("pool", ZZ[0][:, nft + ft, zc], zi[:], zi2[:], ALU.subtract, reads=[("zi", zq), ("zi2", zq)],
                           adds=[("Z", 0, nft + ft)])
                        tt("pool", ZZ[1][:, nft + ft, zc], zi[:], zi2[:], ALU.add, reads=[("zi", zq), ("zi2", zq)],
                           adds=[("Z", 1, nft + ft)])
            xname = "x1" if o == 0 else "x2"
            for cp in range(nft):
                for par in range(2):
                    is_ = nxt("fin", 2)
                    dma("sp", fin[is_][:, 0:2 * nft, :], Fi[cp, par], reads=[], writes=[("fin", is_)])
                    xs_ = nxt("xo", 2)
                    for (psl, src) in tile_src(xname, cp, par):
                        dma("sp", xo[xs_][psl, :], src, reads=[("hy", kind, xname)], adds=[("xo", xs_)])
                    if o == 1:
                        for (psl, src) in tile_src("bg", cp, par):
                            dma("sp", bgt[xs_][psl, :], src, reads=[("hy", kind, "bg")], adds=[("bgt", xs_)])
                    pr0 = 2 * nxt("ivb", 4)
                    pe([(PS[pr0 + hh][:], fin[is_][:, kc, :], ZZ[par][:, kc, 512 * hh:512 * (hh + 1)], kc == 0, kc == 2 * nft - 1)
                        for kc in range(2 * nft) for hh in range(2)],
                       reads=[("fin", is_)] + [("Z", par, kc) for kc in range(2 * nft)], writes=[bank(pr0), bank(pr0 + 1)])
                    for hh in range(2):
                        cs = slice(512 * hh, 512 * (hh + 1))
                        if o == 0:
                            tt("dve", U[par][:, cp, cs], PS[pr0 + hh][:], xo[xs_][:, cs], ALU.mult,
                               reads=[bank(pr0 + hh), ("xo", xs_)], adds=[("U", par, cp)])
                        else:
                            ysl = nxt("ys", 2)
                            tt("dve", yt[ysl][:], PS[pr0 + hh][:], xo[xs_][:, cs], ALU.mult,
                               reads=[bank(pr0 + hh), ("xo", xs_)], writes=[("yt", ysl)])
                            tt("pool", ys[xs_][:, cs], yt[ysl][:], bgt[xs_][:, cs], ALU.mult,
                               reads=[("yt", ysl), ("bgt", xs_)], adds=[("ys", xs_)])
                    if o == 1:
                        r0 = 128 * tb + 256 * cp
                        dma("pool", mix[:, r0:r0 + 256, 384:640].rearrange("b (p two) n -> p two b n", two=2)[:, par],
                            ys[xs_][:].rearrange("p (b n) -> p b n", n=256), reads=[("ys", xs_)],
                            adds=[("mix", bb, tb + 2 * cp + hf) for bb in range(NB) for hf in range(2)])

    def phase3(l, b, P):
        wout, gate, mixt, mixT, xt, tmp, xo = P["wout"], P["gate"], P["mixt"], P["mixT"], P["xt"], P["tmp"], P["xo"]
        last = (l == nlayers - 1)
        tiles = list(range(NT)) if l == 0 else list(range(2, NT))
        dma("sp", gate[0][:], modrow[l, b, 2048:3072].partition_broadcast(128), reads=[("modrow", l)], writes=[("gate", 0)])
        if l == 0:
            dma("sp", gate[1][:], modrow[l, 4, 2048:3072].partition_broadcast(128), reads=[("modrow", l)], writes=[("gate", 1)])
        for ti in tiles:
            ms = nxt("mixt", 4)
            dma("sp", mixt[ms][:], mix[b, 128 * ti:128 * (ti + 1), :], reads=[("mix", b, ti)], writes=[("mixt", ms)])
            pb = nxt("hb", 2)
            pe_tr([(PSB[pb][:, 128 * k:128 * (k + 1)], mixt[ms][:, 128 * k:128 * (k + 1)]) for k in range(8)], ident[:],
                  reads=[("mixt", ms), "ident"], writes=[bank(pb)])
            copy("act", mixT[ms][:], PSB[pb][:], reads=[bank(pb)], writes=[("mixT", ms)])
            ob = 2 + 2 * nxt("opb", 3)
            pe([(PS[ob + hh][:], mixT[ms][:, 128 * k:128 * (k + 1)], wout[:, k, 512 * hh:512 * (hh + 1)], k == 0, k == 7)
                for k in range(8) for hh in range(2)], reads=[("mixT", ms), "wout"], writes=[bank(ob), bank(ob + 1)])
            s_ = nxt("xt3", 4)
            src, sres = xsrc(l, b, ti)
            dma("sp", xt[s_][:], src, reads=sres, writes=[("xt3", s_)])
            gsel = 1 if ti < 2 else 0
            for hh in range(2):
                cs = slice(512 * hh, 512 * (hh + 1))
                tt("dve", tmp[s_][:, cs], PS[ob + hh][:], gate[gsel][:, cs], ALU.mult,
                   reads=[bank(ob + hh), ("gate", gsel)], adds=[("tmp3", s_)])
            tt("pool", xo[s_][:], xt[s_][:], tmp[s_][:], ALU.add, reads=[("xt3", s_), ("tmp3", s_)], writes=[("xo3", s_)])
            if last:
                dst = y[b, 128 * (ti - 2):128 * (ti - 1), :]
                dres = ("y", b, ti)
            elif ti < 2:
                dst = ctxs[b, 128 * ti:128 * (ti + 1), :]
                dres = ("xs", b, ti)
            else:
                dst = xs[b, 128 * (ti - 2):128 * (ti - 1), :]
                dres = ("xs", b, ti)
            dma("pool", dst, xo[s_][:], reads=[("xo3", s_)], writes=[dres])

    for l in range(nlayers):
        S.epoch = l + 1
        if "p1" in phases:
            with ExitStack() as p1:
                P = {}
                P["hT"] = sb("hT", [128, 8, NT * 128], BF16, p1)
                P["qX"] = sb("qX", [128, 3, NT * 128], BF16, p1)
                P["kX"] = sb("kX", [128, NT * 128], BF16, p1)
                P["vaug"] = sb("vaug", [128, NT, 2, 65], BF16, p1)
                P["gX"] = sb("gX", [128, NT, 384], BF16, p1)
                P["PT"] = [sb("PT%d" % i, [128, NT, 256], BF16, p1) for i in range(2)]
                P["xt"] = [sb("xt%d" % i, [128, D], F32, p1) for i in range(4)]
                P["xn"] = [sb("xn%d" % i, [128, D], BF16, p1) for i in range(4)]
                P["junk"] = sb("junk", [128, D], BF16, p1)
                P["st4"] = sb("st4", [128, 16], F32, p1)
                P["wfm"] = [sb("wfm%d" % i, [128, 8, 128], BF16, p1) for i in range(2)]
                P["wtm"] = sb("wtm", [128, 8, 512], BF16, p1)
                P["wbg"] = sb("wbg", [128, 8, 256], BF16, p1)
                P["pbuf"] = [sb("pbuf%d" % i, [128, 2308], BF16, p1) for i in range(2)]
                P["dg"] = sb("dg", [128, 18, 128], BF16, p1)
                for jc in range(18):
                    tsc("dve", P["dg"][:, jc, :], ident[:], cwb[:, 18 * l + jc:18 * l + jc + 1], None, ALU.mult, None,
                        reads=["ident", "cwb"], adds=["dg"])
                P["ubuf"] = sb("ubuf", [128, 2, NT * 128], BF16, p1)
                for nm, dt_ in (("qg", BF16), ("sq", BF16), ("t1", BF16), ("t2", BF16), ("rs", F32), ("rr", F32)):
                    P[nm] = [sb(nm + "%d" % i, [128, 512], dt_, p1) for i in range(2)]
                P["ostage"] = [sb("ostage%d" % i, [128, 2, 384], BF16, p1) for i in range(2)]
                P["hstage"] = [sb("hstage%d" % i, [128, 1024], BF16, p1) for i in range(2)]
                P["bstage"] = [sb("bstage%d" % i, [128, 256], BF16, p1) for i in range(2)]
                P["den"] = sb("den", [128, 8], F32, p1)
                for i in range(2):
                    memset("pool", P["pbuf"][i][:], 0.0, writes=[("pbuf", i)])
                memset("pool", P["vaug"][:], 1.0, writes=[("vaug", t) for t in range(NT)])
                P["nb"] = nb
                P["h_done"] = False
                for b in range(nb):
                    phase1(l, b, P)
            S.barrier()
        if "p2" in phases:
            with ExitStack() as p2:
                P = {}
                P["U"] = [sb("U%d" % i, [128, 8, 1024], BF16, p2) for i in range(2)]
                P["ZZ"] = [sb("ZZ%d" % i, [128, 16, 1024], BF16, p2) for i in range(2)]
                P["Gr"] = sb("Gr", [128, 16, 256], BF16, p2)
                P["Gi"] = sb("Gi", [128, 16, 256], BF16, p2)
                P["fw"] = [sb("fw%d" % i, [128, 4, 8, 128], BF16, p2) for i in range(2)]
                P["fin"] = [sb("fin%d" % i, [128, 16, 128], BF16, p2) for i in range(2)]
                P["xo"] = [sb("xo%d" % i, [128, 1024], BF16, p2) for i in range(2)]
                P["bgt"] = [sb("bgt%d" % i, [128, 1024], BF16, p2) for i in range(2)]
                P["yt"] = [sb("yt%d" % i, [128, 512], F32, p2) for i in range(2)]
                P["ys"] = [sb("ys%d" % i, [128, 1024], BF16, p2) for i in range(2)]
                P["osb"] = [[sb("os%d%d" % (i, j), [128, 512], F32, p2) for j in range(2)] for i in range(2)]
                P["XX"] = [[sb("xx%d%d" % (i, j), [128, 512], BF16, p2) for j in range(4)] for i in range(2)]
                P["mm"] = [[sb("mm%d%d" % (i, j), [128, 256], F32, p2) for j in range(2)] for i in range(4)]
                P["zz"] = [[sb("zz%d%d" % (i, j), [128, 256], F32, p2) for j in range(4)] for i in range(2)]
                phase2(l, "l", P)
                if l == 0 and nlayers > 1:
                    phase2(l, "c", P)
            S.barrier()
        if "p3" in phases:
            with ExitStack() as p3:
                P = {}
                P["wout"] = sb("wout", [128, 8, D], BF16, p3)
                P["gate"] = [sb("gate%d" % i, [128, D], F32, p3) for i in range(2)]
                P["mixt"] = [sb("mixt%d" % i, [128, D], BF16, p3) for i in range(4)]
                P["mixT"] = [sb("mixT%d" % i, [128, D], BF16, p3) for i in range(4)]
                P["xt"] = [sb("x3t%d" % i, [128, D], F32, p3) for i in range(4)]
                P["tmp"] = [sb("x3m%d" % i, [128, D], F32, p3) for i in range(4)]
                P["xo"] = [sb("x3o%d" % i, [128, D], F32, p3) for i in range(4)]
                dma("sp", P["wout"][:], w_out_t[l], reads=[("w_out_t", l)], writes=["wout"])
                for b in range(nb):
                    phase3(l, b, P)
            S.barrier()

    if "modrow" in dbg_out:
        dma("sp", dbg_out["modrow"].ap(), modrow.ap(), reads=[("modrow", 0), ("modrow", 1)], writes=["dbg_modrow"])
    if "gtab" in dbg_out:
        dma("sp", dbg_out["gtab"].ap(), gtab.ap(), reads=[("gtab", "L0"), ("gtab", "L1")], writes=["dbg_gtab"])
    if "gtabc" in dbg_out:
        dma("sp", dbg_out["gtabc"].ap(), gtabc.ap(), reads=[("gtab", "C")], writes=["dbg_gtabc"])
    if "AB" in dbg_out:
        dma("sp", dbg_out["AB"].ap(), AB[:],
            reads=[("AB", l, 1) for l in range(2)] + [("AB", l, 0, k) for l in range(2) for k in range(8)],
            writes=["dbg_AB"])

    S.barrier()
    for name, src in (("mix", mix), ("xs", xs), ("ctxs", ctxs), ("hy_v", hyl["v"]), ("hy_x1", hyl["x1"]), ("hy_x2", hyl["x2"]),
                      ("hy_bg", hyl["bg"]), ("hyc_v", hyc["v"]), ("w_in_t", w_in_t)):
        if name in dbg_out:
            dma("sp", dbg_out[name].ap(), src.ap(), reads=[], writes=["dbg_" + name])
    S.finish()
    S.make_sems(es)
    with nc.Block() as block:
        @block.tensor
        def _(e):
            S.replay("pe", e)

        @block.scalar
        def _(e):
            S.replay("act", e)

        @block.vector
        def _(e):
            S.replay("dve", e)

        @block.gpsimd
        def _(e):
            S.replay("pool", e)

        @block.sync
        def _(e):
            S.replay("sp", e)
    es.close()
    return nc


def make_in_maps(inputs):
    cst = _consts()
    maps = []
    for i in range(NCORES):
        m = {}
        b0 = i * NB
        m["x"] = np.ascontiguousarray(inputs["x"][b0:b0 + NB])
        m["ctx"] = np.ascontiguousarray(inputs["ctx"][b0:b0 + NB])
        m["cR"] = np.ascontiguousarray(
            np.concatenate([inputs["c"][b0:b0 + NB], inputs["c_ctx"][None, :]], 0))
        for k in PARAM_SHAPES:
            m[k] = np.ascontiguousarray(inputs[k])
        for k, v in cst.items():
            m["c_" + k] = v
        maps.append(m)
    return maps


def kernel(**inputs):
    inputs = {k: np.asarray(v) for k, v in inputs.items()}
    nc = build_program()
    maps = make_in_maps(inputs)
    res = run_bass_kernel_spmd(nc, maps, core_ids=list(range(NCORES)))
    return np.concatenate([r["y"] for r in res.results], axis=0).astype(np.float32)
```
